# Optimizing a Trainium2 kernel written in Bass

```python
import math
import jax
import jax.numpy as jnp
from jax import lax
import numpy as np

D_MODEL = 1024
BATCH = 32
SEQ = 2048
DEPTH = 4

GRID_W = 64
CTX_LEN = 256
N_MIXERS = 4
EPS = 1e-6
ROPE_THETA = 10000.0
Q_BLOCK = 128
NEG_INF = -1e30

MLA_HEADS = 16
MLA_NOPE = 64
MLA_ROPE = 32
MLA_QK = MLA_NOPE + MLA_ROPE
MLA_V = 64
MLA_Q_LORA = 384
MLA_KV_LORA = 256

S5_GROUP = 16
S5_GROUPS = D_MODEL // S5_GROUP
S5_STATE = 64
S5_DT_MIN = 1e-3
S5_DT_MAX = 1e-1

NA_HEADS = 16
NA_HEAD_DIM = D_MODEL // NA_HEADS
NA_WIN_H = 8
NA_WIN_W = 16

GQA_HEADS = 8
GQA_KV_HEADS = 2
GQA_HEAD_DIM = D_MODEL // GQA_HEADS

FFN_HIDDEN = -(-(8 * D_MODEL) // (3 * 256)) * 256

kernel_name = 'hybrid_interleaved_mla_s5_natten_gqa_dit'


def rmsnorm(x, g):
    xf = x.astype(jnp.float32)
    y = xf * lax.rsqrt(jnp.mean(xf * xf, axis=-1, keepdims=True) + EPS)
    return (y * g.astype(jnp.float32)).astype(x.dtype)


def modulate(x, g, shift, scale):
    return rmsnorm(x, g) * (1 + scale) + shift


def ada_terms(cvec, w, b):
    m = jax.nn.silu(cvec) @ w + b
    return jnp.split(m[..., None, :], 6, axis=-1)


def swiglu(h, w_in, w_out):
    a, b = jnp.split(h @ w_in, 2, axis=-1)
    return (jax.nn.silu(a) * b) @ w_out


def axial_rope_tables(n_tokens, rot_dim):
    t = jnp.arange(n_tokens, dtype=jnp.int32)
    axis_dim = rot_dim // 2
    inv_freq = ROPE_THETA ** (-jnp.arange(0, axis_dim, 2, dtype=jnp.float32) / axis_dim)

    def table(pos):
        ang = pos.astype(jnp.float32)[:, None] * inv_freq[None, :]
        return jnp.cos(ang), jnp.sin(ang)

    return table(t // GRID_W), table(t % GRID_W)


def rotate_pairs(x, cos, sin):
    d2 = x.shape[-1] // 2
    x1, x2 = x[..., :d2], x[..., d2:]
    c, s = cos[:, None, :], sin[:, None, :]
    return jnp.concatenate([x1 * c - x2 * s, x1 * s + x2 * c], axis=-1).astype(x.dtype)


def apply_axial_rope(x, tables):
    (cos_r, sin_r), (cos_c, sin_c) = tables
    half = x.shape[-1] // 2
    return jnp.concatenate([rotate_pairs(x[..., :half], cos_r, sin_r),
                            rotate_pairs(x[..., half:], cos_c, sin_c)], axis=-1)


def ctx_attention(q, k, v):
    B, C, H, dk = q.shape
    Hk = k.shape[2]
    qg = q.reshape(B, C, Hk, H // Hk, dk)
    s = jnp.einsum('bqkgd,bnkd->bkgqn', qg, k).astype(jnp.float32) * (dk ** -0.5)
    p = jax.nn.softmax(s, axis=-1).astype(v.dtype)
    o = jnp.einsum('bkgqn,bnkd->bqkgd', p, v)
    return o.reshape(B, C, H, v.shape[-1])


def latent_attention(q, kc, vc, kl, vl):
    B, S, H, dk = q.shape
    Hk = kl.shape[2]
    G = H // Hk
    dv = vl.shape[-1]
    k = jnp.concatenate([kc, kl], axis=1)
    v = jnp.concatenate([vc, vl], axis=1)
    nb = S // Q_BLOCK
    qb = q.reshape(B, nb, Q_BLOCK, Hk, G, dk).transpose(1, 0, 2, 3, 4, 5)
    scale = dk ** -0.5

    def one_block(qi):
        s = jnp.einsum('bqkgd,bnkd->bkgqn', qi, k).astype(jnp.float32) * scale
        p = jax.nn.softmax(s, axis=-1).astype(v.dtype)
        return jnp.einsum('bkgqn,bnkd->bqkgd', p, v)

    o = lax.map(one_block, qb)
    return o.transpose(1, 0, 2, 3, 4, 5).reshape(B, S, H, dv)


def qk_normed_qkv(h, w_qkv, n_q, n_kv, dh, g_qn, g_kn, tables, need_q):
    B, L, _ = h.shape
    q_cols = n_q * dh
    if need_q:
        z = h @ w_qkv
        q = rmsnorm(z[..., :q_cols].reshape(B, L, n_q, dh), g_qn)
        kv = z[..., q_cols:]
    else:
        q = None
        kv = h @ w_qkv[:, q_cols:]
    k = rmsnorm(kv[..., :n_kv * dh].reshape(B, L, n_kv, dh), g_kn)
    v = kv[..., n_kv * dh:].reshape(B, L, n_kv, dh)
    if tables is not None:
        k = apply_axial_rope(k, tables)
        if need_q:
            q = apply_axial_rope(q, tables)
    return q, k, v


def mla_queries(h, w_in, g_q, w_uq, g_qn, tables):
    B, L, _ = h.shape
    cq = rmsnorm(h @ w_in[:, :MLA_Q_LORA], g_q)
    q = rmsnorm((cq @ w_uq).reshape(B, L, MLA_HEADS, MLA_QK), g_qn)
    if tables is not None:
        q = jnp.concatenate([q[..., :MLA_NOPE], apply_axial_rope(q[..., MLA_NOPE:], tables)], axis=-1)
    return q


def mla_keys_values(h, w_in, g_kv, w_ukv, g_kn, tables):
    B, L, _ = h.shape
    z = h @ w_in[:, MLA_Q_LORA:]
    ckv = rmsnorm(z[..., :MLA_KV_LORA], g_kv)
    k_rope = jnp.broadcast_to(z[..., None, MLA_KV_LORA:], (B, L, MLA_HEADS, MLA_ROPE))
    kv = (ckv @ w_ukv).reshape(B, L, MLA_HEADS, MLA_NOPE + MLA_V)
    k = rmsnorm(jnp.concatenate([kv[..., :MLA_NOPE], k_rope], axis=-1), g_kn)
    if tables is not None:
        k = jnp.concatenate([k[..., :MLA_NOPE], apply_axial_rope(k[..., MLA_NOPE:], tables)], axis=-1)
    return k, kv[..., MLA_NOPE:]


def mla_mixer(hc, hl, w_in, g_q, g_kv, w_uq, w_ukv, g_qn, g_kn, w_o, tables, ctx_out):
    B, S, _ = hl.shape
    kc, vc = mla_keys_values(hc, w_in, g_kv, w_ukv, g_kn, None)
    kl, vl = mla_keys_values(hl, w_in, g_kv, w_ukv, g_kn, tables)
    ql = mla_queries(hl, w_in, g_q, w_uq, g_qn, tables)
    yl = latent_attention(ql, kc, vc, kl, vl).reshape(B, S, MLA_HEADS * MLA_V) @ w_o
    yc = None
    if ctx_out:
        qc = mla_queries(hc, w_in, g_q, w_uq, g_qn, None)
        yc = ctx_attention(qc, kc, vc).reshape(B, hc.shape[1], MLA_HEADS * MLA_V) @ w_o
    return yc, yl


def s5_discretise(a_re, a_im, log_dt, b_re, b_im):
    f32 = jnp.float32
    a_re, a_im = a_re.astype(f32), a_im.astype(f32)
    b_re, b_im = b_re.astype(f32), b_im.astype(f32)
    dt = jnp.exp(log_dt.astype(f32))[:, None]
    mag = jnp.exp(dt * a_re)
    ab_re = mag * jnp.cos(dt * a_im)
    ab_im = mag * jnp.sin(dt * a_im)
    den = a_re * a_re + a_im * a_im
    nr = ab_re - 1.0
    f_re = (nr * a_re + ab_im * a_im) / den
    f_im = (ab_im * a_re - nr * a_im) / den
    bb_re = f_re[..., None] * b_re - f_im[..., None] * b_im
    bb_im = f_re[..., None] * b_im + f_im[..., None] * b_re
    return ab_re, ab_im, bb_re, bb_im


def complex_affine_combine(e1, e2):
    a1r, a1i, b1r, b1i = e1
    a2r, a2i, b2r, b2i = e2
    return (a2r * a1r - a2i * a1i, a2r * a1i + a2i * a1r,
            a2r * b1r - a2i * b1i + b2r, a2r * b1i + a2i * b1r + b2i)


def s5_scan(ab_re, ab_im, bu_re, bu_im, reverse):
    L = bu_re.shape[1]
    a_re = jnp.broadcast_to(ab_re, (1, L) + ab_re.shape)
    a_im = jnp.broadcast_to(ab_im, (1, L) + ab_im.shape)
    _, _, h_re, h_im = lax.associative_scan(complex_affine_combine, (a_re, a_im, bu_re, bu_im),
                                            reverse=reverse, axis=1)
    return h_re, h_im


def s5_drive(u, bb_re, bb_im):
    return (jnp.einsum('blgc,gpc->blgp', u, bb_re), jnp.einsum('blgc,gpc->blgp', u, bb_im))


def s5_readout(h_re, h_im, c_re, c_im):
    return jnp.einsum('blgp,gcp->blgc', h_re, c_re) - jnp.einsum('blgp,gcp->blgc', h_im, c_im)


def s5_glu(y, w_glu):
    g = jax.nn.gelu(y)
    a, b = jnp.split(g @ w_glu, 2, axis=-1)
    return a * jax.nn.sigmoid(b)


def s5_mixer(hc, hl, a_re, a_im, log_dt, b_re, b_im, c_re, c_im, d_skip, w_glu, ctx_out):
    f32 = jnp.float32
    B, S, D = hl.shape
    C = hc.shape[1]
    uc = hc.astype(f32).reshape(B, C, S5_GROUPS, S5_GROUP)
    ul = hl.astype(f32).reshape(B, S, S5_GROUPS, S5_GROUP)
    yl = d_skip.astype(f32) * hl.astype(f32)
    yc = d_skip.astype(f32) * hc.astype(f32) if ctx_out else None
    for direction in range(2):
        reverse = direction == 1
        ab_re, ab_im, bb_re, bb_im = s5_discretise(a_re[direction], a_im[direction], log_dt[direction],
                                                   b_re[direction], b_im[direction])
        cr, ci = c_re[direction].astype(f32), c_im[direction].astype(f32)
        bc_re, bc_im = s5_drive(uc, bb_re, bb_im)
        sc_re, sc_im = s5_scan(ab_re, ab_im, bc_re, bc_im, reverse)
        edge_c = 0 if reverse else C - 1
        h0_re, h0_im = sc_re[:, edge_c], sc_im[:, edge_c]
        bl_re, bl_im = s5_drive(ul, bb_re, bb_im)
        edge_l = S - 1 if reverse else 0
        bl_re = bl_re.at[:, edge_l].add(ab_re * h0_re - ab_im * h0_im)
        bl_im = bl_im.at[:, edge_l].add(ab_re * h0_im + ab_im * h0_re)
        sl_re, sl_im = s5_scan(ab_re, ab_im, bl_re, bl_im, reverse)
        yl = yl + s5_readout(sl_re, sl_im, cr, ci).reshape(B, S, D)
        if ctx_out:
            yc = yc + s5_readout(sc_re, sc_im, cr, ci).reshape(B, C, D)
    out_l = s5_glu(yl.astype(hl.dtype), w_glu)
    out_c = s5_glu(yc.astype(hc.dtype), w_glu) if ctx_out else None
    return out_c, out_l


def na_mixer(hc, hl, w_qkv, g_qn, g_kn, rpb, w_o, ctx_out):
    B, S, _ = hl.shape
    rows = S // GRID_W
    kh = min(NA_WIN_H, rows)
    qc, kc, vc = qk_normed_qkv(hc, w_qkv, NA_HEADS, NA_HEADS, NA_HEAD_DIM, g_qn, g_kn, None, ctx_out)
    ql, kl, vl = qk_normed_qkv(hl, w_qkv, NA_HEADS, NA_HEADS, NA_HEAD_DIM, g_qn, g_kn, None, True)
    grid = (B, rows, GRID_W, NA_HEADS, NA_HEAD_DIM)
    q_grid, k_grid, v_grid = ql.reshape(grid), kl.reshape(grid), vl.reshape(grid)
    j = jnp.arange(GRID_W)
    col_start = jnp.clip(j - NA_WIN_W // 2, 0, GRID_W - NA_WIN_W)
    col_ok = (j[None, :] >= col_start[:, None]) & (j[None, :] < col_start[:, None] + NA_WIN_W)
    col_idx = jnp.clip(j[None, :] - j[:, None] + NA_WIN_W - 1, 0, 2 * NA_WIN_W - 2)
    key_ok = jnp.broadcast_to(col_ok[:, None, :], (GRID_W, kh, GRID_W)).reshape(GRID_W, kh * GRID_W)
    scale = NA_HEAD_DIM ** -0.5
    n_ctx = kc.shape[1]

    def one_row(i):
        r0 = jnp.clip(i - kh // 2, 0, rows - kh)
        kb = lax.dynamic_slice_in_dim(k_grid, r0, kh, axis=1).reshape(B, kh * GRID_W, NA_HEADS, NA_HEAD_DIM)
        vb = lax.dynamic_slice_in_dim(v_grid, r0, kh, axis=1).reshape(B, kh * GRID_W, NA_HEADS, NA_HEAD_DIM)
        qi = lax.dynamic_index_in_dim(q_grid, i, axis=1, keepdims=False)
        row_idx = r0 + jnp.arange(kh) - i + NA_WIN_H - 1
        bias = rpb[:, row_idx][:, :, col_idx]
        bias = bias.transpose(0, 2, 1, 3).reshape(NA_HEADS, GRID_W, kh * GRID_W).astype(jnp.float32)
        s_lat = jnp.einsum('bqhd,bkhd->bhqk', qi, kb).astype(jnp.float32) * scale + bias
        s_lat = jnp.where(key_ok, s_lat, NEG_INF)
        s_ctx = jnp.einsum('bqhd,bkhd->bhqk', qi, kc).astype(jnp.float32) * scale
        p = jax.nn.softmax(jnp.concatenate([s_ctx, s_lat], axis=-1), axis=-1).astype(vb.dtype)
        return (jnp.einsum('bhqk,bkhd->bqhd', p[..., :n_ctx], vc)
                + jnp.einsum('bhqk,bkhd->bqhd', p[..., n_ctx:], vb))

    o = lax.map(one_row, jnp.arange(rows))
    yl = o.transpose(1, 0, 2, 3, 4).reshape(B, S, NA_HEADS * NA_HEAD_DIM) @ w_o
    yc = None
    if ctx_out:
        yc = ctx_attention(qc, kc, vc).reshape(B, n_ctx, NA_HEADS * NA_HEAD_DIM) @ w_o
    return yc, yl


def gqa_mixer(hc, hl, w_qkv, g_qn, g_kn, w_o, tables, ctx_out):
    B, S, _ = hl.shape
    qc, kc, vc = qk_normed_qkv(hc, w_qkv, GQA_HEADS, GQA_KV_HEADS, GQA_HEAD_DIM, g_qn, g_kn, None, ctx_out)
    ql, kl, vl = qk_normed_qkv(hl, w_qkv, GQA_HEADS, GQA_KV_HEADS, GQA_HEAD_DIM, g_qn, g_kn, tables, True)
    yl = latent_attention(ql, kc, vc, kl, vl).reshape(B, S, GQA_HEADS * GQA_HEAD_DIM) @ w_o
    yc = None
    if ctx_out:
        yc = ctx_attention(qc, kc, vc).reshape(B, hc.shape[1], GQA_HEADS * GQA_HEAD_DIM) @ w_o
    return yc, yl


def setup_inputs(seed: int = 0) -> dict:
    key = jax.random.key(seed)
    keys = jax.random.split(key, 64)
    counter = iter(range(64))
    f32 = jnp.float32
    D = D_MODEL
    G, P, CG = S5_GROUPS, S5_STATE, S5_GROUP
    nA, nB, nC, nD = (len(range(k, DEPTH, N_MIXERS)) for k in range(N_MIXERS))

    def nrm(shape, scale):
        return scale * jax.random.normal(keys[next(counter)], shape, f32)

    def gain(shape):
        return 1.0 + nrm(shape, 0.05)

    n_idx = jnp.arange(S5_STATE, dtype=f32)
    return {
        'x': nrm((BATCH, SEQ, D), 1.0),
        'c': nrm((BATCH, D), 1.0),
        'ctx': nrm((BATCH, CTX_LEN, D), 1.0),
        'c_ctx': nrm((D,), 1.0),
        'ada_w': nrm((DEPTH, D, 6 * D), 0.5 * D ** -0.5),
        'ada_b': nrm((DEPTH, 6 * D), 0.02),
        'norm_mix': gain((DEPTH, D)),
        'norm_ffn': gain((DEPTH, D)),
        'ffn_w_in': nrm((DEPTH, D, 2 * FFN_HIDDEN), D ** -0.5),
        'ffn_w_out': nrm((DEPTH, FFN_HIDDEN, D), FFN_HIDDEN ** -0.5),
        'mla_w_in': nrm((nA, D, MLA_Q_LORA + MLA_KV_LORA + MLA_ROPE), D ** -0.5),
        'mla_g_q': gain((nA, MLA_Q_LORA)),
        'mla_g_kv': gain((nA, MLA_KV_LORA)),
        'mla_w_uq': nrm((nA, MLA_Q_LORA, MLA_HEADS * MLA_QK), MLA_Q_LORA ** -0.5),
        'mla_w_ukv': nrm((nA, MLA_KV_LORA, MLA_HEADS * (MLA_NOPE + MLA_V)), MLA_KV_LORA ** -0.5),
        'mla_g_qn': gain((nA, MLA_QK)),
        'mla_g_kn': gain((nA, MLA_QK)),
        'mla_w_o': nrm((nA, MLA_HEADS * MLA_V, D), (MLA_HEADS * MLA_V) ** -0.5),
        's5_a_re': -0.5 + nrm((nB, 2, G, P), 0.01),
        's5_a_im': jnp.pi * n_idx + nrm((nB, 2, G, P), 0.01),
        's5_log_dt': jax.random.uniform(keys[next(counter)], (nB, 2, G), f32,
                                        math.log(S5_DT_MIN), math.log(S5_DT_MAX)),
        's5_b_re': nrm((nB, 2, G, P, CG), (2 * CG) ** -0.5),
        's5_b_im': nrm((nB, 2, G, P, CG), (2 * CG) ** -0.5),
        's5_c_re': nrm((nB, 2, G, CG, P), P ** -0.5),
        's5_c_im': nrm((nB, 2, G, CG, P), P ** -0.5),
        's5_d': nrm((nB, D), 0.5),
        's5_w_glu': nrm((nB, D, 2 * D), D ** -0.5),
        'na_w_qkv': nrm((nC, D, 3 * NA_HEADS * NA_HEAD_DIM), D ** -0.5),
        'na_g_qn': gain((nC, NA_HEAD_DIM)),
        'na_g_kn': gain((nC, NA_HEAD_DIM)),
        'na_rpb': nrm((nC, NA_HEADS, 2 * NA_WIN_H - 1, 2 * NA_WIN_W - 1), 0.1),
        'na_w_o': nrm((nC, NA_HEADS * NA_HEAD_DIM, D), (NA_HEADS * NA_HEAD_DIM) ** -0.5),
        'gqa_w_qkv': nrm((nD, D, (GQA_HEADS + 2 * GQA_KV_HEADS) * GQA_HEAD_DIM), D ** -0.5),
        'gqa_g_qn': gain((nD, GQA_HEAD_DIM)),
        'gqa_g_kn': gain((nD, GQA_HEAD_DIM)),
        'gqa_w_o': nrm((nD, GQA_HEADS * GQA_HEAD_DIM, D), (GQA_HEADS * GQA_HEAD_DIM) ** -0.5),
    }


def reference(x, c, ctx, c_ctx, ada_w, ada_b, norm_mix, norm_ffn, ffn_w_in, ffn_w_out,
              mla_w_in, mla_g_q, mla_g_kv, mla_w_uq, mla_w_ukv, mla_g_qn, mla_g_kn, mla_w_o,
              s5_a_re, s5_a_im, s5_log_dt, s5_b_re, s5_b_im, s5_c_re, s5_c_im, s5_d, s5_w_glu,
              na_w_qkv, na_g_qn, na_g_kn, na_rpb, na_w_o,
              gqa_w_qkv, gqa_g_qn, gqa_g_kn, gqa_w_o):
    S = x.shape[1]
    mla_tables = axial_rope_tables(S, MLA_ROPE)
    gqa_tables = axial_rope_tables(S, GQA_HEAD_DIM)
    xl, xc = x, ctx
    for i in range(DEPTH):
        kind, j = i % N_MIXERS, i // N_MIXERS
        ctx_out = i < DEPTH - 1
        sh_l, sc_l, gt_l, sh2_l, sc2_l, gt2_l = ada_terms(c, ada_w[i], ada_b[i])
        sh_c, sc_c, gt_c, sh2_c, sc2_c, gt2_c = ada_terms(c_ctx, ada_w[i], ada_b[i])
        hl = modulate(xl, norm_mix[i], sh_l, sc_l)
        hc = modulate(xc, norm_mix[i], sh_c, sc_c)
        if kind == 0:
            yc, yl = mla_mixer(hc, hl, mla_w_in[j], mla_g_q[j], mla_g_kv[j], mla_w_uq[j], mla_w_ukv[j],
                               mla_g_qn[j], mla_g_kn[j], mla_w_o[j], mla_tables, ctx_out)
        elif kind == 1:
            yc, yl = s5_mixer(hc, hl, s5_a_re[j], s5_a_im[j], s5_log_dt[j], s5_b_re[j], s5_b_im[j],
                              s5_c_re[j], s5_c_im[j], s5_d[j], s5_w_glu[j], ctx_out)
        elif kind == 2:
            yc, yl = na_mixer(hc, hl, na_w_qkv[j], na_g_qn[j], na_g_kn[j], na_rpb[j], na_w_o[j], ctx_out)
        else:
            yc, yl = gqa_mixer(hc, hl, gqa_w_qkv[j], gqa_g_qn[j], gqa_g_kn[j], gqa_w_o[j], gqa_tables, ctx_out)
        xl = xl + gt_l * yl
        xl = xl + gt2_l * swiglu(modulate(xl, norm_ffn[i], sh2_l, sc2_l), ffn_w_in[i], ffn_w_out[i])
        if ctx_out:
            xc = xc + gt_c * yc
            xc = xc + gt2_c * swiglu(modulate(xc, norm_ffn[i], sh2_c, sc2_c), ffn_w_in[i], ffn_w_out[i])
    return xl
```

```python
import numpy as np
import concourse.bass as bass
import concourse.mybir as mybir
from concourse.bass_utils import run_bass_kernel_spmd

F32, BF16 = mybir.dt.float32, mybir.dt.bfloat16
ALU, AF, AX = mybir.AluOpType, mybir.ActivationFunctionType, mybir.AxisListType

D = 1024
DC = 8
CTX = 256
SEQ = 2048
NT = CTX + SEQ
FH = 2816
HC = 22
EPS = 1e-6
N_CORES = 8


class Tile:
    __slots__ = ("ap", "name", "w", "r")
    registry = []

    def __init__(self, ap, name=""):
        self.ap, self.name, self.w, self.r = ap, name, None, {}
        Tile.registry.append(self)

    def __getitem__(self, idx):
        return self.ap[idx]


class Prog:
    ENGS = ("pe", "act", "dve", "pool", "sp")

    def __init__(self, nc, stack, n_dma_sems=8, n_sets=4):
        self.nc = nc
        self.eng = {"pe": nc.tensor, "act": nc.scalar, "dve": nc.vector, "pool": nc.gpsimd, "sp": nc.sync}
        self.n_dma = n_dma_sems
        self.dma_q = ("sp", "pool")
        self.sets = []
        for s_ in range(n_sets):
            d = {}
            for e in self.ENGS:
                d[e] = stack.enter_context(nc.semaphore("s%d_%s" % (s_, e)))
            for q in self.dma_q:
                for j in range(n_dma_sems):
                    d[("dma", q, j)] = stack.enter_context(nc.semaphore("s%d_dma_%s%d" % (s_, q, j)))
            self.sets.append(d)
        self.epoch = 0
        self.sems = self.sets[0]
        self.count = {e: 0 for e in self.ENGS}
        self.seen = {e: {} for e in self.ENGS}
        self.dma_uses = {q: [0] * n_dma_sems for q in self.dma_q}
        self.dma_rr = {q: 0 for q in self.dma_q}
        self.n_instr = 0
        Tile.registry = []

    def reset_epoch(self):
        self.barrier()
        self.epoch += 1
        self.sems = self.sets[self.epoch]
        self.count = {e: 0 for e in self.ENGS}
        self.seen = {e: {} for e in self.ENGS}
        self.dma_uses = {q: [0] * self.n_dma for q in self.dma_q}
        for t in Tile.registry:
            t.w, t.r = None, {}

    def _need(self, eng, key, val):
        if self.seen[eng].get(key, 0) >= val:
            return
        self.seen[eng][key] = val
        self.eng[eng].wait_ge(self.sems[key], val)

    def _deps(self, eng, reads, writes):
        for t in reads:
            if t.w is not None:
                key, val, src = t.w
                if not (src == eng and eng == "pe"):
                    self._need(eng, key, val)
        strict = eng == "pool"
        for t in writes:
            if t.w is not None:
                key, val, src = t.w
                if src != eng or strict:
                    self._need(eng, key, val)
            for key, (val, src) in t.r.items():
                if src != eng or strict:
                    self._need(eng, key, val)

    def _mark(self, tok, reads, writes):
        key, val, src = tok
        for t in reads:
            t.r[key] = (val, src)
        for t in writes:
            t.w = tok
            t.r = {}

    def op(self, eng, fn, reads=(), writes=()):
        self._deps(eng, reads, writes)
        self.count[eng] += 1
        tok = (eng, self.count[eng], eng)
        self._mark(tok, reads, writes)
        fn(self.eng[eng]).then_inc(self.sems[eng], 1)
        self.n_instr += 1
        return tok

    def dma(self, eng, out_ap, in_ap, reads=(), writes=()):
        j = self.dma_rr[eng]
        self.dma_rr[eng] = (j + 1) % self.n_dma
        self.dma_uses[eng][j] += 1
        u = self.dma_uses[eng][j]
        key = ("dma", eng, j)
        if u > 1:
            self._need(eng, key, 16 * (u - 1))
        self._deps(eng, reads, writes)
        tok = (key, 16 * u, "dma")
        self._mark(tok, reads, writes)
        self.eng[eng].dma_start(out=out_ap, in_=in_ap).then_inc(self.sems[key], 16)
        self.n_instr += 1
        return tok

    def barrier(self):
        for e in self.ENGS:
            for f in self.ENGS:
                if f != e and self.count[f]:
                    self._need(e, f, self.count[f])
            for q in self.dma_q:
                for j in range(self.n_dma):
                    if self.dma_uses[q][j]:
                        self._need(e, ("dma", q, j), 16 * self.dma_uses[q][j])

    def finish(self):
        for q in self.dma_q:
            for j in range(self.n_dma):
                if self.dma_uses[q][j]:
                    self._need("sp", ("dma", q, j), 16 * self.dma_uses[q][j])


def build_program(nb, layers, debug_ctx=False, do_mixer=True, do_ffn=True):
    import contextlib
    nc = bass.Bass("TRN2", target_bir_lowering=False)
    gs = contextlib.ExitStack()
    P = Prog(nc, gs)
    nv = nb + 1

    def din(name, shape, dt=F32):
        return nc.dram_tensor(name, list(shape), dt, kind="ExternalInput").ap()

    def dout(name, shape, dt=F32):
        return nc.dram_tensor(name, list(shape), dt, kind="ExternalOutput").ap()

    uid = [0]

    def sb(st, name, shape, dt):
        uid[0] += 1
        return st.enter_context(nc.sbuf_tensor("%s_u%d" % (name, uid[0]), list(shape), dt))

    def T(name=""):
        return Tile(None, name)

    x_d = din("x", [nb, SEQ, D])
    ctx_d = din("ctx", [nb, CTX, D])
    cc_d = din("cc", [128, DC, nv])
    ident_d = din("ident", [128, 128])
    ada_w_d = {L: din("ada_w%d" % L, [D, 6 * D]) for L in layers}
    ada_b_d = din("ada_b", [128, 4, 48])
    nmix_d = din("norm_mix", [128, 4, DC])
    nffn_d = din("norm_ffn", [128, 4, DC])
    ffn_wi_d = {L: din("ffn_wi%d" % L, [D, 2 * FH]) for L in layers} if do_ffn else {}
    ffn_wo_d = {L: din("ffn_wo%d" % L, [FH, D]) for L in layers} if do_ffn else {}
    mixw = {}
    if do_mixer and 0 in layers:
        mixw["mla_win"] = din("mla_win", [D, 896])
        mixw["mla_gl"] = din("mla_gl", [128, 5])
        mixw["mla_wuq"] = din("mla_wuq", [16, 384, 256])
        mixw["mla_wuk"] = din("mla_wuk", [16, 256, 128])
        mixw["mla_wv"] = din("mla_wv", [256, 1024])
        mixw["mla_wo"] = din("mla_wo", [D, D])
        mixw["mla_g"] = din("mla_g", [128, 4])
        mixw["mla_rope"] = din("mla_rope", [128, 4, 64])
    if do_mixer and 1 in layers:
        mixw["s5_as"] = din("s5_as", [128, 2, 64])
        mixw["s5_ldt"] = din("s5_ldt", [128, 64])
        mixw["s5_b"] = din("s5_b", [64, 16, 4, 64])
        mixw["s5_c"] = din("s5_c", [64, 64, 4, 16])
        mixw["s5_d"] = din("s5_d", [128, DC])
        mixw["s5_wglu"] = din("s5_wglu", [D, 2 * D])
    if do_mixer and 2 in layers:
        mixw["na_wqkv"] = din("na_wqkv", [D, 3 * D])
        mixw["na_wo"] = din("na_wo", [D, D])
        mixw["na_g"] = din("na_g", [128, 2])
        mixw["na_M"] = din("na_M", [16, 64, 15, 64])
    if do_mixer and 3 in layers:
        mixw["gqa_wqk"] = din("gqa_wqk", [10, D, 256])
        mixw["gqa_wv"] = din("gqa_wv", [D, 256])
        mixw["gqa_wo"] = din("gqa_wo", [D, D])
        mixw["gqa_g"] = din("gqa_g", [128, 4])
        mixw["gqa_rope"] = din("gqa_rope", [128, 4, 64])
    out_d = dout("out", [nb, SEQ, D])
    outc_d = dout("out_ctx", [nb, CTX, D]) if debug_ctx else None

    XT = sb(gs, "XT", [128, DC, NT], F32)
    ident = sb(gs, "ident", [128, 128], F32)
    onesD = sb(gs, "onesD", [128, 128], BF16)
    ADA = sb(gs, "ADA", [128, 4, 48, nv], F32)
    nmix = sb(gs, "nmix", [128, 4, DC], F32)
    nffn = sb(gs, "nffn", [128, 4, DC], F32)
    GM = sb(gs, "GM", [128, 2, DC], F32)
    PS = [gs.enter_context(nc.psum_tensor("ps%d" % i, [128, 512], F32)) for i in range(8)]

    BLK = [(0, 256), (256, 512), (768, 512), (1280, 512), (1792, 512)]
    tXT = [[T() for _ in range(5)] for _ in range(DC)]
    t_ident, t_ones, t_ADA, t_nmix, t_nffn, t_GM = T(), T(), T(), T(), T(), T()
    t_PS = [T() for _ in range(8)]

    P.op("pool", lambda e: e.memset(onesD[:], 1.0 / D), writes=[t_ones])
    ones1 = sb(gs, "ones1", [128, 128], BF16)
    t_ones1 = T()
    P.op("pool", lambda e: e.memset(ones1[:], 1.0), writes=[t_ones1])
    P.dma("sp", ident[:], ident_d[:, :], writes=[t_ident])
    P.dma("sp", nmix[:], nmix_d[:, :, :], writes=[t_nmix])
    P.dma("sp", nffn[:], nffn_d[:, :, :], writes=[t_nffn])

    with contextlib.ExitStack() as st:
        cc = sb(st, "cc_sb", [128, DC, nv], F32)
        scc = sb(st, "scc", [128, DC, nv], F32)
        adab = sb(st, "adab", [128, 4, 48], F32)
        adaw = [sb(st, "adaw%d" % i, [128, DC, 768], F32) for i in range(2)]
        t_cc, t_scc, t_adab = T(), T(), T()
        t_adaw = [T() for _ in range(2)]
        P.dma("sp", cc[:], cc_d[:, :, :], writes=[t_cc])
        P.dma("sp", adab[:], ada_b_d[:, :, :], writes=[t_adab])
        P.op("act", lambda e: e.activation(out=scc[:], in_=cc[:], func=AF.Silu), reads=[t_cc], writes=[t_scc])
        cnt = 0
        for L in layers:
            for g in range(8):
                bi = cnt % 2
                cnt += 1
                P.dma("sp", adaw[bi][:], ada_w_d[L][:, g * 768:(g + 1) * 768].rearrange("(k p) n -> p k n", p=128),
                      writes=[t_adaw[bi]])
                for jj in range(6):
                    j = g * 6 + jj
                    for k in range(DC):
                        P.op("pe", (lambda e, bi=bi, j=j, jj=jj, k=k: e.matmul(
                            PS[0][:, j * nv:(j + 1) * nv], lhsT=adaw[bi][:, k, jj * 128:(jj + 1) * 128],
                            rhs=scc[:, k, :], start=(k == 0), stop=(k == DC - 1))),
                            reads=[t_adaw[bi], t_scc], writes=[t_PS[0]])
            for v in range(nv):
                P.op("dve", (lambda e, L=L, v=v: e.tensor_tensor(
                    out=ADA[:, L, :, v],
                    in0=PS[0][:, 0:48 * nv].rearrange("p (j v) -> p j v", v=nv)[:, :, v],
                    in1=adab[:, L, :], op=ALU.add)),
                    reads=[t_PS[0], t_adab], writes=[t_ADA])
        P.barrier()

    def ada_ap(L, term, c, v):
        return ADA[:, L, term * DC + c, v:v + 1]

    def blk_of_tile(tt):
        return 0 if tt < 2 else 1 + (tt - 2) // 4

    def load_tokens(b):
        with contextlib.ExitStack() as st:
            io = [sb(st, "io%d" % i, [128, D], F32) for i in range(3)]
            t_io = [[T(), T()] for _ in range(3)]
            for tt in range(NT // 128):
                bi = tt % 3
                src = ctx_d[b, tt * 128:(tt + 1) * 128, :] if tt < 2 else x_d[b, (tt - 2) * 128:(tt - 1) * 128, :]
                P.dma("sp", io[bi][:], src, writes=t_io[bi])
                blk = blk_of_tile(tt)
                for half in range(2):
                    pb = 6 + half
                    for cq in range(4):
                        c = half * 4 + cq
                        P.op("pe", (lambda e, bi=bi, c=c, cq=cq, pb=pb: e.transpose(
                            out=PS[pb][:, cq * 128:(cq + 1) * 128], in_=io[bi][:, c * 128:(c + 1) * 128],
                            identity=ident[:])),
                            reads=[t_io[bi][half], t_ident], writes=[t_PS[pb]])
                    dst = XT[:, half * 4:(half + 1) * 4, tt * 128:(tt + 1) * 128]
                    srcp = PS[pb][:, :].rearrange("p (c t) -> p c t", c=4)
                    wr = [tXT[half * 4 + cq][blk] for cq in range(4)]
                    if half == 0:
                        P.op("dve", (lambda e, dst=dst, srcp=srcp: e.tensor_copy(out=dst, in_=srcp)),
                             reads=[t_PS[pb]], writes=wr)
                    else:
                        P.op("act", (lambda e, dst=dst, srcp=srcp: e.copy(out=dst, in_=srcp)),
                             reads=[t_PS[pb]], writes=wr)
            P.barrier()

    def store_tokens(b):
        with contextlib.ExitStack() as st:
            io = [sb(st, "so%d" % i, [128, D], F32) for i in range(3)]
            t_io = [[T(), T()] for _ in range(3)]
            tiles = range(NT // 128) if debug_ctx else range(2, NT // 128)
            for n_, tt in enumerate(tiles):
                bi = n_ % 3
                blk = blk_of_tile(tt)
                for half in range(2):
                    pb = 6 + half
                    for cq in range(4):
                        c = half * 4 + cq
                        P.op("pe", (lambda e, c=c, cq=cq, pb=pb, tt=tt: e.transpose(
                            out=PS[pb][:, cq * 128:(cq + 1) * 128], in_=XT[:, c, tt * 128:(tt + 1) * 128],
                            identity=ident[:])),
                            reads=[tXT[c][blk], t_ident], writes=[t_PS[pb]])
                    dst = io[bi][:, half * 512:(half + 1) * 512]
                    if half == 0:
                        P.op("dve", (lambda e, dst=dst, pb=pb: e.tensor_copy(out=dst, in_=PS[pb][:, :])),
                             reads=[t_PS[pb]], writes=[t_io[bi][0]])
                    else:
                        P.op("act", (lambda e, dst=dst, pb=pb: e.copy(out=dst, in_=PS[pb][:, :])),
                             reads=[t_PS[pb]], writes=[t_io[bi][1]])
                dstd = outc_d[b, tt * 128:(tt + 1) * 128, :] if tt < 2 else out_d[b, (tt - 2) * 128:(tt - 1) * 128, :]
                P.dma("sp", dstd, io[bi][:], reads=t_io[bi])
            P.barrier()

    def make_gm(L, normt, t_norm, term_scale, b):
        for s_, v in ((0, b), (1, nb)):
            P.op("dve", (lambda e, s_=s_, v=v: e.scalar_tensor_tensor(
                out=GM[:, s_, :], in0=ADA[:, L, term_scale * DC:(term_scale + 1) * DC, v], scalar=1.0,
                in1=normt[:, L, :], op0=ALU.add, op1=ALU.mult)),
                reads=[t_ADA, t_norm], writes=[t_GM])

    def norm_block(nb_, L, b, blk, term_shift, dst, dtile):
        sq, rstd, tmpn, t_sq, t_rstd, t_tmpn = nb_["sq"], nb_["rstd"], nb_["tmpn"], nb_["t_sq"], nb_["t_rstd"], nb_["t_tmpn"]
        t0, n = BLK[blk]
        s_ = 1 if blk == 0 else 0
        v = nb if blk == 0 else b
        pb = 5
        for c in range(DC):
            i = c % 2
            P.op("pool", (lambda e, i=i, c=c: e.tensor_tensor(
                out=sq[i][:, :n], in0=XT[:, c, t0:t0 + n], in1=XT[:, c, t0:t0 + n], op=ALU.mult)),
                reads=[tXT[c][blk]], writes=[t_sq[i]])
            P.op("pe", (lambda e, i=i, c=c: e.matmul(
                PS[pb][:, :n], lhsT=onesD[:], rhs=sq[i][:, :n], start=(c == 0), stop=(c == DC - 1))),
                reads=[t_sq[i], t_ones], writes=[t_PS[pb]])
        ri = blk % 2
        P.op("act", (lambda e: e.activation(out=rstd[ri][:, :n], in_=PS[pb][:, :n], func=AF.Sqrt,
                                            bias=EPS, scale=1.0)),
             reads=[t_PS[pb]], writes=[t_rstd[ri]])
        P.op("dve", (lambda e: e.reciprocal(out=rstd[ri][:, :n], in_=rstd[ri][:, :n])),
             reads=[t_rstd[ri]], writes=[t_rstd[ri]])
        for c in range(DC):
            i = c % 2
            P.op("dve", (lambda e, i=i, c=c: e.tensor_tensor(
                out=tmpn[i][:, :n], in0=XT[:, c, t0:t0 + n], in1=rstd[ri][:, :n], op=ALU.mult)),
                reads=[tXT[c][blk], t_rstd[ri]], writes=[t_tmpn[i]])
            P.op("act", (lambda e, i=i, c=c: e.activation(
                out=dst(c), in_=tmpn[i][:, :n], func=AF.Identity,
                bias=ada_ap(L, term_shift, c, v), scale=GM[:, s_, c:c + 1])),
                reads=[t_tmpn[i], t_GM, t_ADA], writes=[dtile(c)])

    def norm_scratch(st):
        return {
            "sq": [sb(st, "sq%d" % i, [128, 512], BF16) for i in range(2)],
            "rstd": [sb(st, "rstd%d" % i, [128, 512], F32) for i in range(2)],
            "tmpn": [sb(st, "tmpn%d" % i, [128, 512], F32) for i in range(2)],
            "t_sq": [T(), T()], "t_rstd": [T(), T()], "t_tmpn": [T(), T()],
        }


    def mixer_norm(ns, L, b, HT, tHT):
        make_gm(L, nmix, t_nmix, 1, b)
        for blk in range(5):
            t0, n = BLK[blk]
            norm_block(ns, L, b, blk, 0, (lambda c, t0=t0, n=n: HT[:, c, t0:t0 + n]), (lambda c, blk=blk: tHT[c][blk]))

    def attention(qT, tq, kT, tk, vfn, tv, key_tiles, n, dk, dv, scale, outT, tout, PT, tPT, rs, trs, onesv, po=0):
        nk = len(key_tiles)
        for i, kt in enumerate(key_tiles):
            sbk = 3 + (i % 2)
            P.op("pe", (lambda e, kt=kt, sbk=sbk: e.matmul(PS[sbk][:, :n], lhsT=kT(kt), rhs=qT, start=True, stop=True)),
                 reads=[tk(kt), tq], writes=[t_PS[sbk]])
            pi = i % len(PT)
            P.op("act", (lambda e, sbk=sbk, pi=pi: e.activation(out=PT[pi][:, :n], in_=PS[sbk][:, :n], func=AF.Exp,
                                                                scale=scale)),
                 reads=[t_PS[sbk]], writes=[tPT[pi]])
            P.op("pe", (lambda e, kt=kt, pi=pi, i=i: e.matmul(PS[5][po:po + dv, :n], lhsT=vfn(kt), rhs=PT[pi][:, :n],
                                                              start=(i == 0), stop=(i == nk - 1))),
                 reads=[tv(kt), tPT[pi]], writes=[t_PS[5]])
            P.op("pe", (lambda e, pi=pi, i=i: e.matmul(PS[6][po:po + dv, :n], lhsT=onesv, rhs=PT[pi][:, :n],
                                                       start=(i == 0), stop=(i == nk - 1))),
                 reads=[t_ones1, tPT[pi]], writes=[t_PS[6]])
        P.op("dve", (lambda e: e.reciprocal(out=rs[po:po + dv, :n], in_=PS[6][po:po + dv, :n])), reads=[t_PS[6]], writes=[trs])
        P.op("dve", (lambda e: e.tensor_tensor(out=outT, in0=PS[5][po:po + dv, :n], in1=rs[po:po + dv, :n], op=ALU.mult)),
             reads=[t_PS[5], trs], writes=[tout])

    def out_proj(L, b, wo_d, OT, tOT, nchunk, krows, blocks, st):
        wo = [sb(st, "wop%d" % i, [128, nchunk, 128], BF16) for i in range(1)]
        t_wo = [T()]
        cnt = 0
        for c in range(DC):
            wi_ = 0
            cnt += 1
            P.dma("pool", wo[wi_][:krows, :, :],
                  wo_d[:, c * 128:(c + 1) * 128].rearrange("(j p) n -> p j n", p=krows), writes=[t_wo[wi_]])
            for blk in blocks:
                t0, n = BLK[blk]
                v = nb if blk == 0 else b
                for j in range(nchunk):
                    P.op("pe", (lambda e, wi_=wi_, j=j, t0=t0, n=n: e.matmul(
                        PS[7][:, :n], lhsT=wo[wi_][:krows, j, :], rhs=OT[:krows, j, t0:t0 + n],
                        start=(j == 0), stop=(j == nchunk - 1))),
                        reads=[t_wo[wi_], tOT[j][blk]], writes=[t_PS[7]])
                P.op("dve", (lambda e, c=c, v=v, t0=t0, n=n: e.scalar_tensor_tensor(
                    out=XT[:, c, t0:t0 + n], in0=PS[7][:, :n], scalar=ada_ap(L, 2, c, v),
                    in1=XT[:, c, t0:t0 + n], op0=ALU.mult, op1=ALU.add)),
                    reads=[t_PS[7], t_ADA, tXT[c][blk]], writes=[tXT[c][blk]])


    def rope_norm_evac(i, blk, n, gt, t_gt, gcol, rope, t_rope, zs_src, t_zs, sc, dst, dtile, inv_n, ranges):
        sqz, t_sqz, rq, t_rq, t1, tt1, t2, tt2 = sc
        P.op("act", (lambda e: e.activation(out=sqz[:, :n], in_=PS[0][:, :n], func=AF.Square)),
             reads=[t_PS[0]], writes=[t_sqz])
        P.op("pe", (lambda e: e.matmul(PS[2][:, :n], lhsT=ones1[:, :], rhs=sqz[:, :n], start=True, stop=True)),
             reads=[t_ones1, t_sqz], writes=[t_PS[2]])
        P.op("act", (lambda e: e.activation(out=rq[:, :n], in_=PS[2][:, :n], func=AF.Sqrt, bias=EPS, scale=inv_n)),
             reads=[t_PS[2]], writes=[t_rq])
        P.op("dve", (lambda e: e.reciprocal(out=rq[:, :n], in_=rq[:, :n])), reads=[t_rq], writes=[t_rq])
        if blk == 0:
            P.op("dve", (lambda e: e.scalar_tensor_tensor(
                out=dst, in0=PS[0][:, :n], scalar=gt[:, gcol:gcol + 1], in1=rq[:, :n], op0=ALU.mult, op1=ALU.mult)),
                reads=[t_PS[0], t_gt, t_rq], writes=[dtile])
            return
        r0 = (blk - 1) * 8
        rope_hi = 0
        for (p0, p1, kind) in ranges:
            np_ = p1 - p0
            if kind == "none":
                P.op("dve", (lambda e, p0=p0, p1=p1: e.tensor_scalar(
                    out=t1[p0:p1, :n], in0=PS[0][p0:p1, :n], scalar1=gt[p0:p1, gcol:gcol + 1], scalar2=None,
                    op0=ALU.mult)), reads=[t_PS[0], t_gt], writes=[tt1])
                continue
            rope_hi = max(rope_hi, p1)
            if kind == "row":
                ctab = rope[p0:p1, 0, r0:r0 + 8].unsqueeze(2).broadcast_to([np_, 8, 64])
                stab = rope[p0:p1, 1, r0:r0 + 8].unsqueeze(2).broadcast_to([np_, 8, 64])
            else:
                ctab = rope[p0:p1, 2, :].unsqueeze(1).broadcast_to([np_, 8, 64])
                stab = rope[p0:p1, 3, :].unsqueeze(1).broadcast_to([np_, 8, 64])
            z = PS[0][p0:p1, :n].rearrange("p (r c) -> p r c", c=64)
            zs = zs_src(p0, p1).rearrange("p (r c) -> p r c", c=64)
            o1 = t1[p0:p1, :n].rearrange("p (r c) -> p r c", c=64)
            o2 = t2[p0:p1, :n].rearrange("p (r c) -> p r c", c=64)
            P.op("dve", (lambda e, z=z, o1=o1, ctab=ctab, p0=p0, p1=p1: e.scalar_tensor_tensor(
                out=o1, in0=z, scalar=gt[p0:p1, gcol:gcol + 1], in1=ctab, op0=ALU.mult, op1=ALU.mult)),
                reads=[t_PS[0], t_gt, t_rope], writes=[tt1])
            P.op("dve", (lambda e, zs=zs, o2=o2, stab=stab, p0=p0, p1=p1: e.scalar_tensor_tensor(
                out=o2, in0=zs, scalar=gt[p0:p1, gcol + 1:gcol + 2], in1=stab, op0=ALU.mult, op1=ALU.mult)),
                reads=[t_zs, t_gt, t_rope], writes=[tt2])
        P.op("pool", (lambda e: e.tensor_tensor(out=t1[:rope_hi, :n], in0=t1[:rope_hi, :n], in1=t2[:rope_hi, :n],
                                                op=ALU.add)),
             reads=[tt1, tt2], writes=[tt1])
        P.op("dve", (lambda e: e.tensor_tensor(out=dst, in0=t1[:, :n], in1=rq[:, :n], op=ALU.mult)),
             reads=[tt1, t_rq], writes=[dtile])

    def mla(L, b):
        ctx_out = (L != 3) or debug_ctx
        qblocks = [0, 1, 2, 3, 4] if ctx_out else [1, 2, 3, 4]
        allb = [0, 1, 2, 3, 4]
        with contextlib.ExitStack() as st:
            CQ = sb(st, "CQ", [128, 3, NT], BF16)
            CKV = sb(st, "CKV", [128, 2, NT], BF16)
            KR = sb(st, "KR", [128, 2, NT], BF16)
            tCQ = [[T() for _ in range(5)] for _ in range(3)]
            tCKV = [[T() for _ in range(5)] for _ in range(2)]
            tKR = [[T() for _ in range(5)] for _ in range(2)]
            identb = sb(st, "identb", [128, 128], BF16)
            t_identb = T()
            P.op("dve", lambda e: e.tensor_copy(out=identb[:], in_=ident[:]), reads=[t_ident], writes=[t_identb])
            with contextlib.ExitStack() as st2:
                HT = sb(st2, "HT", [128, DC, NT], BF16)
                tHT = [[T() for _ in range(5)] for _ in range(DC)]
                with contextlib.ExitStack() as st3:
                    mixer_norm(norm_scratch(st3), L, b, HT, tHT)
                    P.barrier()
                win = sb(st2, "win", [128, DC, 896], BF16)
                gl = sb(st2, "gl", [128, 5], F32)
                sqa = [sb(st2, "sqa%d" % i, [128, 512], BF16) for i in range(2)]
                rqa = sb(st2, "rqa", [128, 512], F32)
                t_win, t_gl, t_sqa, t_rqa = T(), T(), [T(), T()], T()
                P.dma("pool", win[:], mixw["mla_win"].rearrange("(k p) n -> p k n", p=128), writes=[t_win])
                P.dma("sp", gl[:], mixw["mla_gl"][:, :], writes=[t_gl])
                for blk in allb:
                    t0, n = BLK[blk]
                    for (c0, ncs, dstb, tdst, inv_n, g0) in ((0, 3, CQ, tCQ, 1.0 / 384, 0), (3, 2, CKV, tCKV, 1.0 / 256, 3)):
                        for ci in range(ncs):
                            for k in range(DC):
                                P.op("pe", (lambda e, ci=ci, k=k, c0=c0: e.matmul(
                                    PS[ci][:, :n], lhsT=win[:, k, (c0 + ci) * 128:(c0 + ci + 1) * 128],
                                    rhs=HT[:, k, t0:t0 + n], start=(k == 0), stop=(k == DC - 1))),
                                    reads=[t_win, tHT[k][blk]], writes=[t_PS[ci]])
                        for ci in range(ncs):
                            i = ci % 2
                            P.op("act", (lambda e, ci=ci, i=i: e.activation(out=sqa[i][:, :n], in_=PS[ci][:, :n],
                                                                            func=AF.Square)),
                                 reads=[t_PS[ci]], writes=[t_sqa[i]])
                            P.op("pe", (lambda e, ci=ci, i=i, ncs=ncs: e.matmul(
                                PS[3][:, :n], lhsT=ones1[:, :], rhs=sqa[i][:, :n], start=(ci == 0), stop=(ci == ncs - 1))),
                                reads=[t_ones1, t_sqa[i]], writes=[t_PS[3]])
                        P.op("act", (lambda e, inv_n=inv_n: e.activation(out=rqa[:, :n], in_=PS[3][:, :n], func=AF.Sqrt,
                                                                         bias=EPS, scale=inv_n)),
                             reads=[t_PS[3]], writes=[t_rqa])
                        P.op("dve", (lambda e: e.reciprocal(out=rqa[:, :n], in_=rqa[:, :n])), reads=[t_rqa], writes=[t_rqa])
                        for ci in range(ncs):
                            P.op("dve", (lambda e, ci=ci, dstb=dstb, g0=g0: e.scalar_tensor_tensor(
                                out=dstb[:, ci, t0:t0 + n], in0=PS[ci][:, :n], scalar=gl[:, g0 + ci:g0 + ci + 1],
                                in1=rqa[:, :n], op0=ALU.mult, op1=ALU.mult)),
                                reads=[t_PS[ci], t_gl, t_rqa], writes=[tdst[ci][blk]])
                    for ci in range(2):
                        for k in range(DC):
                            P.op("pe", (lambda e, ci=ci, k=k: e.matmul(
                                PS[4 + ci][:, :n], lhsT=win[:, k, (5 + ci) * 128:(6 + ci) * 128],
                                rhs=HT[:, k, t0:t0 + n], start=(k == 0), stop=(k == DC - 1))),
                                reads=[t_win, tHT[k][blk]], writes=[t_PS[4 + ci]])
                        P.op("act", (lambda e, ci=ci: e.copy(out=KR[:, ci, t0:t0 + n], in_=PS[4 + ci][:, :n])),
                             reads=[t_PS[4 + ci]], writes=[tKR[ci][blk]])
                P.barrier()
            OT = sb(st, "OT", [128, 8, NT], BF16)
            tOT = [[T() for _ in range(5)] for _ in range(8)]
            Vg = sb(st, "Vg", [128, NT // 128, 256], BF16)
            tVg = [T() for _ in range(NT // 128)]
            QT = [sb(st, "QT%d" % i, [128, NT], BF16) for i in range(2)]
            KTh = [sb(st, "KTh%d" % i, [128, NT], BF16) for i in range(2)]
            tQT = [[T() for _ in range(5)] for _ in range(2)]
            tKT = [[T() for _ in range(5)] for _ in range(2)]
            wuq = [sb(st, "wuq%d" % i, [128, 3, 256], BF16) for i in range(1)] * 2
            wuk = [sb(st, "wuk%d" % i, [128, 2, 128], BF16) for i in range(1)] * 2
            wvg = sb(st, "wvg", [128, 2, 256], BF16)
            t_wuq, t_wuk, t_wvg = [T()] * 2, [T()] * 2, T()
            gq = sb(st, "gq", [128, 4], F32)
            rope = sb(st, "rope", [128, 4, 64], F32)
            t_gq, t_rope = T(), T()
            sqz = sb(st, "sqz", [128, 512], BF16)
            rq = sb(st, "rq", [128, 512], F32)
            t1 = sb(st, "t1", [128, 512], F32)
            t2 = sb(st, "t2", [128, 512], F32)
            sc = (sqz, T(), rq, T(), t1, T(), t2, T())
            PT = [sb(st, "PT%d" % i, [128, 512], BF16) for i in range(2)]
            tPT = [T() for _ in range(2)]
            rs = sb(st, "rs", [128, 512], F32)
            trs = T()
            P.dma("sp", gq[:], mixw["mla_g"][:, :], writes=[t_gq])
            P.dma("sp", rope[:], mixw["mla_rope"][:, :, :], writes=[t_rope])
            P.op("pool", lambda e: e.memset(t1[:], 0.0), writes=[sc[5]])
            P.op("pool", lambda e: e.memset(t2[:], 0.0), writes=[sc[7]])
            ranges = [(0, 32, "row"), (32, 64, "col"), (64, 128, "none")]
            scale = 96.0 ** -0.5
            for h in range(16):
                hi = h % 2
                if h % 4 == 0:
                    g = h // 4
                    P.dma("pool", wvg[:], mixw["mla_wv"][:, g * 256:(g + 1) * 256].rearrange("(k p) n -> p k n", p=128),
                          writes=[t_wvg])
                    for tt in range(NT // 128):
                        pb = 6 + (tt % 2)
                        blk = blk_of_tile(tt)
                        for k in range(2):
                            P.op("pe", (lambda e, k=k, tt=tt, pb=pb: e.matmul(
                                PS[pb][:, :256], lhsT=CKV[:, k, tt * 128:(tt + 1) * 128], rhs=wvg[:, k, :],
                                start=(k == 0), stop=(k == 1))),
                                reads=[tCKV[k][blk], t_wvg], writes=[t_PS[pb]])
                        P.op("act", (lambda e, tt=tt, pb=pb: e.copy(out=Vg[:, tt, :], in_=PS[pb][:, :256])),
                             reads=[t_PS[pb]], writes=[tVg[tt]])
                P.dma("pool", wuq[hi][:], mixw["mla_wuq"][h].rearrange("(k p) n -> p k n", p=128), writes=[t_wuq[hi]])
                P.dma("pool", wuk[hi][:], mixw["mla_wuk"][h].rearrange("(k p) n -> p k n", p=128), writes=[t_wuk[hi]])
                for blk in allb:
                    t0, n = BLK[blk]
                    for k in range(2):
                        P.op("pe", (lambda e, k=k: e.matmul(PS[0][:, :n], lhsT=wuk[hi][:, k, :], rhs=CKV[:, k, t0:t0 + n],
                                                            start=(k == 0), stop=False)),
                             reads=[t_wuk[hi], tCKV[k][blk]], writes=[t_PS[0]])
                    P.op("pe", (lambda e: e.matmul(PS[0][:, :n], lhsT=identb[:, :], rhs=KR[:, 0, t0:t0 + n],
                                                   start=False, stop=True)),
                         reads=[t_identb, tKR[0][blk]], writes=[t_PS[0]])
                    rope_norm_evac(0, blk, n, gq, t_gq, 2, rope, t_rope,
                                   (lambda p0, p1, t0=t0, n=n: KR[p0:p1, 1, t0:t0 + n]), tKR[1][blk], sc,
                                   KTh[hi][:, t0:t0 + n], tKT[hi][blk], 1.0 / 96, ranges)
                for blk in qblocks:
                    t0, n = BLK[blk]
                    for half in range(2 if blk > 0 else 1):
                        for k in range(3):
                            P.op("pe", (lambda e, k=k, half=half: e.matmul(
                                PS[half][:, :n], lhsT=wuq[hi][:, k, half * 128:(half + 1) * 128],
                                rhs=CQ[:, k, t0:t0 + n], start=(k == 0), stop=(k == 2))),
                                reads=[t_wuq[hi], tCQ[k][blk]], writes=[t_PS[half]])
                    rope_norm_evac(0, blk, n, gq, t_gq, 0, rope, t_rope,
                                   (lambda p0, p1, n=n: PS[1][p0:p1, :n]), t_PS[1], sc,
                                   QT[hi][:, t0:t0 + n], tQT[hi][blk], 1.0 / 96, ranges)
                for blk in qblocks:
                    t0, n = BLK[blk]
                    key_tiles = [0, 1] if blk == 0 else list(range(NT // 128))
                    po = (h % 2) * 64
                    vc = (h % 4) * 64
                    attention(QT[hi][:, t0:t0 + n], tQT[hi][blk],
                              (lambda kt: KTh[hi][:, kt * 128:(kt + 1) * 128]),
                              (lambda kt: tKT[hi][blk_of_tile(kt)]),
                              (lambda kt, vc=vc: Vg[:, kt, vc:vc + 64]), (lambda kt: tVg[kt]),
                              key_tiles, n, 128, 64, scale, OT[po:po + 64, h // 2, t0:t0 + n], tOT[h // 2][blk],
                              PT, tPT, rs, trs, ones1[:, 0:64], po=po)
            out_proj(L, b, mixw["mla_wo"], OT, tOT, 8, 128, qblocks, st)
            P.barrier()


    def na(L, b):
        ctx_out = (L != 3) or debug_ctx
        with contextlib.ExitStack() as st:
            HT = sb(st, "HT", [128, DC, NT], BF16)
            tHT = [[T() for _ in range(5)] for _ in range(DC)]
            with contextlib.ExitStack() as st3:
                mixer_norm(norm_scratch(st3), L, b, HT, tHT)
                P.barrier()
            OT = sb(st, "OT", [128, 8, NT], BF16)
            tOT = [[T() for _ in range(5)] for _ in range(8)]
            QTz = [sb(st, "QTz%d" % i, [128, NT], BF16) for i in range(2)]
            KT = sb(st, "KTc", [128, NT], BF16)
            Vc = sb(st, "Vc", [128, NT // 128, 128], BF16)
            tQT = [[T() for _ in range(5)] for _ in range(2)]
            tKT = [T() for _ in range(5)]
            tVc = [T() for _ in range(NT // 128)]
            wq = [sb(st, "wqc%d" % i, [128, DC, 128], BF16) for i in range(3)]
            t_wq = [T() for _ in range(3)]
            tblA = sb(st, "tblA", [128, 14, 64], F32)
            tblB = sb(st, "tblB", [128, 5, 64], F32)
            t_tblA, t_tblB, t_mask = [T(), T()], [T(), T()], T()
            gq = sb(st, "gq", [128, 2], F32)
            gqs = sb(st, "gqs", [128, 1], F32)
            t_gq, t_gqs = T(), T()
            onesB = sb(st, "onesB", [128, 128], BF16)
            t_onesB = T()
            sqz = sb(st, "sqz", [128, 512], BF16)
            rq = sb(st, "rq", [128, 512], F32)
            t_sqz, t_rq = T(), T()
            xs = [sb(st, "xs%d" % i, [128, 320], F32) for i in range(2)]
            t_xs = [T(), T()]
            PT = [sb(st, "PT%d" % i, [128, 512], BF16) for i in range(2)]
            tPT = [T() for _ in range(2)]
            rs = sb(st, "rs", [128, 512], F32)
            trs = T()
            tz = T()
            P.op("pool", lambda e: e.memset(onesB[:], 0.0), writes=[t_onesB])
            P.op("pool", lambda e: e.memset(onesB[0:64, 0:64], 1.0), writes=[t_onesB])
            P.op("pool", lambda e: e.memset(onesB[64:128, 64:128], 1.0), writes=[t_onesB])
            P.op("pool", lambda e: e.memset(QTz[0][64:128, :], 0.0), writes=[tz])
            P.op("pool", lambda e: e.memset(QTz[1][0:64, :], 0.0), writes=[tz])
            P.op("pool", lambda e: e.memset(tblB[0:64, 0, :], -1e4), writes=[t_mask])
            P.op("pool", lambda e: e.memset(tblB[64:128, 4, :], -1e4), writes=[t_mask])
            P.dma("sp", gq[:], mixw["na_g"][:, :], writes=[t_gq])
            P.op("dve", lambda e: e.tensor_scalar(out=gqs[:], in0=gq[:, 0:1], scalar1=0.125, scalar2=None, op0=ALU.mult),
                 reads=[t_gq], writes=[t_gqs])
            wcnt = [0]

            def proj_norm(col0, gap, t_g, is_q):
                wi_ = wcnt[0] % 3
                wcnt[0] += 1
                P.dma("pool", wq[wi_][:], mixw["na_wqkv"][:, col0:col0 + 128].rearrange("(k p) n -> p k n", p=128),
                      writes=[t_wq[wi_]])
                for blk in range(5):
                    t0, n = BLK[blk]
                    for k in range(DC):
                        P.op("pe", (lambda e, k=k: e.matmul(PS[0][:, :n], lhsT=wq[wi_][:, k, :], rhs=HT[:, k, t0:t0 + n],
                                                            start=(k == 0), stop=(k == DC - 1))),
                             reads=[t_wq[wi_], tHT[k][blk]], writes=[t_PS[0]])
                    P.op("act", (lambda e: e.activation(out=sqz[:, :n], in_=PS[0][:, :n], func=AF.Square)),
                         reads=[t_PS[0]], writes=[t_sqz])
                    P.op("pe", (lambda e: e.matmul(PS[2][:, :n], lhsT=onesB[:, :], rhs=sqz[:, :n], start=True, stop=True)),
                         reads=[t_onesB, t_sqz], writes=[t_PS[2]])
                    P.op("act", (lambda e: e.activation(out=rq[:, :n], in_=PS[2][:, :n], func=AF.Sqrt, bias=EPS,
                                                        scale=1.0 / 64)),
                         reads=[t_PS[2]], writes=[t_rq])
                    P.op("dve", (lambda e: e.reciprocal(out=rq[:, :n], in_=rq[:, :n])), reads=[t_rq], writes=[t_rq])
                    if is_q:
                        for hh in range(2):
                            p0 = hh * 64
                            P.op("dve", (lambda e, hh=hh, p0=p0: e.scalar_tensor_tensor(
                                out=QTz[hh][p0:p0 + 64, t0:t0 + n], in0=PS[0][p0:p0 + 64, :n], scalar=gap[p0:p0 + 64, :],
                                in1=rq[p0:p0 + 64, :n], op0=ALU.mult, op1=ALU.mult)),
                                reads=[t_PS[0], t_g, t_rq, tz], writes=[tQT[hh][blk]])
                    else:
                        P.op("dve", (lambda e: e.scalar_tensor_tensor(
                            out=KT[:, t0:t0 + n], in0=PS[0][:, :n], scalar=gap, in1=rq[:, :n], op0=ALU.mult, op1=ALU.mult)),
                            reads=[t_PS[0], t_g, t_rq], writes=[tKT[blk]])

            scount = [0]
            for j in range(8):
                proj_norm(j * 128, gqs[:, 0:1], t_gqs, True)
                proj_norm(D + j * 128, gq[:, 1:2], t_gq, False)
                wi_ = wcnt[0] % 3
                wcnt[0] += 1
                P.dma("pool", wq[wi_][:], mixw["na_wqkv"][:, 2 * D + j * 128:2 * D + (j + 1) * 128].rearrange(
                    "(k p) n -> p k n", p=128), writes=[t_wq[wi_]])
                for tt in range(NT // 128):
                    blk = blk_of_tile(tt)
                    for k in range(DC):
                        P.op("pe", (lambda e, k=k, tt=tt: e.matmul(
                            PS[1][:, :128], lhsT=HT[:, k, tt * 128:(tt + 1) * 128], rhs=wq[wi_][:, k, :],
                            start=(k == 0), stop=(k == DC - 1))),
                            reads=[tHT[k][blk], t_wq[wi_]], writes=[t_PS[1]])
                    P.op("act", (lambda e, tt=tt: e.copy(out=Vc[:, tt, :], in_=PS[1][:, :128])),
                         reads=[t_PS[1]], writes=[tVc[tt]])
                for hh in range(2):
                    h = 2 * j + hh
                    po = hh * 64
                    Mh = mixw["na_M"][h]
                    P.dma("sp", tblA[0:64, :, :], Mh[:, 0:14, :], writes=[t_tblA[0]])
                    P.dma("sp", tblA[64:128, :, :], Mh[:, 1:15, :], writes=[t_tblA[1]])
                    P.dma("sp", tblB[0:64, 1:5, :], Mh[:, 4:11:2, :], writes=[t_tblB[0]])
                    P.dma("sp", tblB[64:128, 0:4, :], Mh[:, 3:10:2, :], writes=[t_tblB[1]])
                    if ctx_out:
                        attention(QTz[hh][:, 0:256], tQT[hh][0],
                                  (lambda kt: KT[:, kt * 128:(kt + 1) * 128]), (lambda kt: tKT[0]),
                                  (lambda kt: Vc[:, kt, po:po + 64]), (lambda kt: tVc[kt]),
                                  [0, 1], 256, 128, 64, 1.0, OT[po:po + 64, j, 0:256], tOT[j][0], PT, tPT, rs, trs,
                                  ones1[:, 0:64], po=po)
                    for blk in range(1, 5):
                        t0 = BLK[blk][0]
                        for ql in range(8):
                            qr = (blk - 1) * 8 + ql
                            r0 = min(max(qr - 4, 0), 24)
                            d0 = r0 - qr + 7
                            odd = r0 % 2
                            nsl = 5 if odd else 4
                            tlo = 2 + (r0 - odd) // 2
                            sbk = 3 + (scount[0] % 2)
                            xi = scount[0] % 2
                            pi = scount[0] % len(PT)
                            scount[0] += 1
                            qs = QTz[hh][:, 256 + qr * 64:256 + (qr + 1) * 64]
                            for s_ in range(nsl):
                                tt = tlo + s_
                                P.op("pe", (lambda e, s_=s_, tt=tt: e.matmul(
                                    PS[sbk][:, s_ * 64:(s_ + 1) * 64], lhsT=KT[:, tt * 128:(tt + 1) * 128], rhs=qs,
                                    start=True, stop=True)),
                                    reads=[tKT[blk_of_tile(tt)], tQT[hh][blk]], writes=[t_PS[sbk]])
                            for ct in range(2):
                                P.op("pe", (lambda e, ct=ct: e.matmul(
                                    PS[sbk][:, 320 + ct * 64:320 + (ct + 1) * 64],
                                    lhsT=KT[:, ct * 128:(ct + 1) * 128], rhs=qs, start=True, stop=True)),
                                    reads=[tKT[0], tQT[hh][blk]], writes=[t_PS[sbk]])
                            if odd:
                                bias_ap, tb_ = tblB[:, :, :], t_tblB
                            else:
                                bias_ap, tb_ = tblA[:, d0:d0 + 7:2, :], t_tblA
                            nb_ = nsl * 64
                            P.op("dve", (lambda e, bias_ap=bias_ap, nb_=nb_: e.tensor_tensor(
                                out=xs[xi][:, :nb_].rearrange("p (s c) -> p s c", c=64),
                                in0=PS[sbk][:, 0:nb_].rearrange("p (s c) -> p s c", c=64), in1=bias_ap, op=ALU.add)),
                                reads=[t_PS[sbk], tb_[0], tb_[1], t_mask], writes=[t_xs[xi]])
                            P.op("act", (lambda e, nb_=nb_: e.activation(out=PT[pi][:, 0:nb_], in_=xs[xi][:, :nb_], func=AF.Exp)),
                                 reads=[t_xs[xi]], writes=[tPT[pi]])
                            P.op("act", (lambda e: e.activation(out=PT[pi][:, 320:448], in_=PS[sbk][:, 320:448], func=AF.Exp)),
                                 reads=[t_PS[sbk]], writes=[tPT[pi]])
                            mm = [(Vc[:, tlo + s_, po:po + 64], PT[pi][:, s_ * 64:(s_ + 1) * 64], tVc[tlo + s_])
                                  for s_ in range(nsl)]
                            mm += [(Vc[:, ct, po:po + 64], PT[pi][:, 320 + ct * 64:320 + (ct + 1) * 64], tVc[ct])
                                   for ct in range(2)]
                            ocol = slice(ql * 64, (ql + 1) * 64)
                            for i_, (lv, rp, tv_) in enumerate(mm):
                                P.op("pe", (lambda e, lv=lv, rp=rp, i_=i_: e.matmul(
                                    PS[5][po:po + 64, ocol], lhsT=lv, rhs=rp, start=(i_ == 0), stop=(i_ == len(mm) - 1))),
                                    reads=[tv_, tPT[pi]], writes=[t_PS[5]])
                            for i_, (lv, rp, tv_) in enumerate(mm):
                                P.op("pe", (lambda e, rp=rp, i_=i_: e.matmul(
                                    PS[6][po:po + 64, ocol], lhsT=ones1[:, 0:64], rhs=rp,
                                    start=(i_ == 0), stop=(i_ == len(mm) - 1))),
                                    reads=[t_ones1, tPT[pi]], writes=[t_PS[6]])
                        P.op("dve", (lambda e: e.reciprocal(out=rs[po:po + 64, :], in_=PS[6][po:po + 64, :])),
                             reads=[t_PS[6]], writes=[trs])
                        P.op("dve", (lambda e, t0=t0: e.tensor_tensor(out=OT[po:po + 64, j, t0:t0 + 512],
                                                                      in0=PS[5][po:po + 64, :], in1=rs[po:po + 64, :],
                                                                      op=ALU.mult)),
                             reads=[t_PS[5], trs], writes=[tOT[j][blk]])
            out_proj(L, b, mixw["na_wo"], OT, tOT, 8, 128, [0, 1, 2, 3, 4] if ctx_out else [1, 2, 3, 4], st)
            P.barrier()


    s5c = {}

    def s5_consts():
        APW = sb(gs, "s5_APW", [128, 12, 3, 64], F32)
        FF = sb(gs, "s5_FF", [128, 3, 64], F32)
        s5c["APW"], s5c["FF"], s5c["t"] = APW, FF, T()
        with contextlib.ExitStack() as st:
            a = sb(st, "s5a", [128, 2, 64], F32)
            ldt = sb(st, "s5ldt", [128, 64], F32)
            tmp = [sb(st, "s5t%d" % i, [128, 64], F32) for i in range(10)]
            hp = sb(st, "s5hp", [128, 1], F32)
            tt = T()
            P.dma("sp", a[:], mixw["s5_as"][:, :, :], writes=[tt])
            P.dma("sp", ldt[:], mixw["s5_ldt"][:, :], writes=[tt])
            P.op("pool", lambda e: e.memset(hp[:], float(np.pi / 2)), reads=[tt], writes=[tt])
            dt, x, th, mag, cs, sn, wr, wi, ta, tb = tmp

            def dve(fn):
                P.op("dve", fn, reads=[tt], writes=[tt])

            def act(fn):
                P.op("act", fn, reads=[tt], writes=[tt])
            act(lambda e: e.activation(out=dt[:], in_=ldt[:], func=AF.Exp))
            dve(lambda e: e.tensor_tensor(out=x[:], in0=dt[:], in1=a[:, 0, :], op=ALU.mult))
            dve(lambda e: e.tensor_tensor(out=th[:], in0=dt[:], in1=a[:, 1, :], op=ALU.mult))
            act(lambda e: e.activation(out=mag[:], in_=x[:], func=AF.Exp, scale=1.0 / 16))
            act(lambda e: e.activation(out=sn[:], in_=th[:], func=AF.Sin, scale=1.0 / 16))
            act(lambda e: e.activation(out=cs[:], in_=th[:], func=AF.Sin, scale=1.0 / 16, bias=hp[:, 0:1]))
            dve(lambda e: e.tensor_tensor(out=wr[:], in0=mag[:], in1=cs[:], op=ALU.mult))
            dve(lambda e: e.tensor_tensor(out=wi[:], in0=mag[:], in1=sn[:], op=ALU.mult))
            cur = (wr, wi)
            nxt_ = (x, th)
            for it in range(15):
                cr, ci = cur
                if it >= 3:
                    k = it - 3
                    nr_, ni_ = APW[:, k, 0, :], APW[:, k, 1, :]
                else:
                    nr_, ni_ = nxt_[0][:], nxt_[1][:]
                dve(lambda e, cr=cr: e.tensor_tensor(out=ta[:], in0=cr[:] if not isinstance(cr, bass.AP) else cr,
                                                     in1=cr[:] if not isinstance(cr, bass.AP) else cr, op=ALU.mult))
                dve(lambda e, ci=ci: e.tensor_tensor(out=tb[:], in0=ci[:] if not isinstance(ci, bass.AP) else ci,
                                                     in1=ci[:] if not isinstance(ci, bass.AP) else ci, op=ALU.mult))
                dve(lambda e, cr=cr, ci=ci, ni_=ni_: e.scalar_tensor_tensor(
                    out=ni_, in0=cr[:] if not isinstance(cr, bass.AP) else cr, scalar=2.0,
                    in1=ci[:] if not isinstance(ci, bass.AP) else ci, op0=ALU.mult, op1=ALU.mult))
                dve(lambda e, nr_=nr_: e.tensor_tensor(out=nr_, in0=ta[:], in1=tb[:], op=ALU.subtract))
                if it >= 3:
                    dve(lambda e, k=k: e.tensor_scalar(out=APW[:, k, 2, :], in0=APW[:, k, 1, :], scalar1=-1.0,
                                                       scalar2=None, op0=ALU.mult))
                    cur = (APW[:, k, 0, :], APW[:, k, 1, :])
                else:
                    cur = nxt_
                    nxt_ = (wr, wi) if nxt_[0] is x else (x, th)
            abr, abi = APW[:, 0, 0, :], APW[:, 0, 1, :]
            are, aim = a[:, 0, :], a[:, 1, :]
            den, nr1, m1, m2, m3, m4 = dt, mag, cs, sn, ta, tb
            dve(lambda e: e.tensor_tensor(out=den[:], in0=are, in1=are, op=ALU.mult))
            dve(lambda e: e.tensor_tensor(out=m1[:], in0=aim, in1=aim, op=ALU.mult))
            dve(lambda e: e.tensor_tensor(out=den[:], in0=den[:], in1=m1[:], op=ALU.add))
            dve(lambda e: e.reciprocal(out=den[:], in_=den[:]))
            dve(lambda e: e.tensor_scalar(out=nr1[:], in0=abr, scalar1=-1.0, scalar2=None, op0=ALU.add))
            dve(lambda e: e.tensor_tensor(out=m1[:], in0=nr1[:], in1=are, op=ALU.mult))
            dve(lambda e: e.tensor_tensor(out=m2[:], in0=abi, in1=aim, op=ALU.mult))
            dve(lambda e: e.tensor_tensor(out=m1[:], in0=m1[:], in1=m2[:], op=ALU.add))
            dve(lambda e: e.tensor_tensor(out=FF[:, 0, :], in0=m1[:], in1=den[:], op=ALU.mult))
            dve(lambda e: e.tensor_tensor(out=m3[:], in0=abi, in1=are, op=ALU.mult))
            dve(lambda e: e.tensor_tensor(out=m4[:], in0=nr1[:], in1=aim, op=ALU.mult))
            dve(lambda e: e.tensor_tensor(out=m3[:], in0=m3[:], in1=m4[:], op=ALU.subtract))
            dve(lambda e: e.tensor_tensor(out=FF[:, 1, :], in0=m3[:], in1=den[:], op=ALU.mult))
            dve(lambda e: e.tensor_scalar(out=FF[:, 2, :], in0=FF[:, 1, :], scalar1=-1.0, scalar2=None, op0=ALU.mult))
            P.op("dve", lambda e: e.tensor_copy(out=tmp[0][:, 0:1], in_=FF[:, 0, 0:1]), reads=[tt], writes=[s5c["t"]])
            P.barrier()

    def s5(L, b):
        ctx_out = (L != 3) or debug_ctx
        APW, FF, t_c = s5c["APW"], s5c["FF"], s5c["t"]
        with contextlib.ExitStack() as st:
            HT = sb(st, "HT", [128, DC, NT], BF16)
            tHT = [[T() for _ in range(5)] for _ in range(DC)]
            with contextlib.ExitStack() as st3:
                mixer_norm(norm_scratch(st3), L, b, HT, tHT)
                P.barrier()
            X = [[sb(st, "scan%d%d" % (i, r), [128, NT], F32) for r in range(2)] for i in range(2)]
            tX = [[T(), T()], [T(), T()]]
            tX2 = [[T(), T()], [T(), T()]]
            BP = sb(st, "BP", [128, 4, 2, 2, 128], F32)
            BBP = sb(st, "BBP", [128, 4, 2, 2, 128], BF16)
            CP = sb(st, "CP", [128, 4, 2, 2, 128], F32)
            dsk = sb(st, "dsk", [128, DC], F32)
            ytmp = [sb(st, "ytmp%d" % i, [128, 512], F32) for i in range(2)]
            t_BP, t_BBP, t_CP, t_dsk, t_ytmp = T(), T(), T(), T(), [T(), T()]
            P.op("pool", lambda e: e.memset(BP[:], 0.0), writes=[t_BP])
            P.op("pool", lambda e: e.memset(CP[:], 0.0), writes=[t_CP])
            P.dma("sp", dsk[:], mixw["s5_d"][:, :], writes=[t_dsk])

            def pos(blk, d):
                t0, n = BLK[blk]
                if d == 0:
                    return t0
                return 2048 if blk == 0 else t0 - 256

            import os
            stop = os.environ.get("S5_STOP", "")
            for c in range(DC):
                if stop == "consts":
                    break
                for gl in range(8):
                    g = 8 * c + gl
                    sl, hf = gl // 2, gl % 2
                    P.dma("sp", BP[gl * 16:(gl + 1) * 16, sl, :, :, hf * 64:(hf + 1) * 64].rearrange("c d r p -> c (d r) p"),
                          mixw["s5_b"][g], reads=[t_BP], writes=[t_BP])
                    P.dma("sp", CP[hf * 64:(hf + 1) * 64, sl, :, :, gl * 16:(gl + 1) * 16].rearrange("p d r c -> p (d r) c"),
                          mixw["s5_c"][g], reads=[t_CP], writes=[t_CP])
                P.op("act", lambda e: e.copy(out=BBP[:], in_=BP[:]), reads=[t_BP], writes=[t_BBP])
                P.op("pool", lambda e: e.tensor_scalar(out=CP[:, :, :, 1, :], in0=CP[:, :, :, 1, :], scalar1=-1.0,
                                                       scalar2=None, op0=ALU.mult), reads=[t_CP], writes=[t_CP])
                if stop == "place":
                    continue
                first = True
                nmm = 4 * 2 * 2
                imm = 0
                for sl in range(4):
                    stg = 4 * c + sl
                    for d in range(2):
                        col = d * 32 + stg
                        for blk in range(5):
                            t0, n = BLK[blk]
                            p0 = pos(blk, d)
                            for ri in range(2):
                                P.op("pe", (lambda e, ri=ri: e.matmul(PS[ri][:, :n], lhsT=BBP[:, sl, d, ri, :],
                                                                      rhs=HT[:, c, t0:t0 + n], start=True, stop=True)),
                                     reads=[t_BBP, tHT[c][blk]], writes=[t_PS[ri]])
                            P.op("act", (lambda e: e.activation(out=X[0][1][:, p0:p0 + n], in_=PS[1][:, :n], func=AF.Identity,
                                                                scale=FF[:, 0, col:col + 1])),
                                 reads=[t_PS[1], t_c], writes=[tX[0][1], tX2[0][1]])
                            P.op("act", (lambda e: e.activation(out=X[0][0][:, p0:p0 + n], in_=PS[0][:, :n], func=AF.Identity,
                                                                scale=FF[:, 0, col:col + 1])),
                                 reads=[t_PS[0], t_c], writes=[tX[0][0], tX2[0][0]])
                            P.op("dve", (lambda e: e.scalar_tensor_tensor(
                                out=X[0][0][:, p0:p0 + n], in0=PS[1][:, :n], scalar=FF[:, 2, col:col + 1],
                                in1=X[0][0][:, p0:p0 + n], op0=ALU.mult, op1=ALU.add)),
                                reads=[t_PS[1], t_c, tX[0][0]], writes=[tX[0][0]])
                            P.op("dve", (lambda e: e.scalar_tensor_tensor(
                                out=X[0][1][:, p0:p0 + n], in0=PS[0][:, :n], scalar=FF[:, 1, col:col + 1],
                                in1=X[0][1][:, p0:p0 + n], op0=ALU.mult, op1=ALU.add)),
                                reads=[t_PS[0], t_c, tX[0][1]], writes=[tX[0][1]])
                        for k in range(0 if stop == "drive" else 12):
                            dd = 1 << k
                            src, dst = X[k % 2], X[(k + 1) % 2]
                            tsrc, tdst = tX[k % 2], tX[(k + 1) % 2]
                            ar, ai, nai = APW[:, k, 0, col:col + 1], APW[:, k, 1, col:col + 1], APW[:, k, 2, col:col + 1]
                            m = NT - dd
                            if d == 0:
                                lo, hi, keep = slice(0, m), slice(dd, NT), slice(0, dd)
                            else:
                                lo, hi, keep = slice(dd, NT), slice(0, m), slice(m, NT)
                            tsrc2, tdst2 = tX2[k % 2], tX2[(k + 1) % 2]
                            rd = [tsrc[0], tsrc[1], tsrc2[0], tsrc2[1], t_c]
                            P.op("act", (lambda e: e.copy(out=dst[0][:, keep], in_=src[0][:, keep])),
                                 reads=rd, writes=[tdst2[0]])
                            P.op("pool", (lambda e: e.tensor_copy(out=dst[1][:, keep], in_=src[1][:, keep])),
                                 reads=rd, writes=[tdst2[1]])
                            P.op("dve", (lambda e: e.scalar_tensor_tensor(
                                out=dst[0][:, hi], in0=src[0][:, lo], scalar=ar, in1=src[0][:, hi], op0=ALU.mult, op1=ALU.add)),
                                reads=rd, writes=[tdst[0]])
                            P.op("dve", (lambda e: e.scalar_tensor_tensor(
                                out=dst[0][:, hi], in0=src[1][:, lo], scalar=nai, in1=dst[0][:, hi], op0=ALU.mult, op1=ALU.add)),
                                reads=rd + [tdst[0]], writes=[tdst[0]])
                            P.op("dve", (lambda e: e.scalar_tensor_tensor(
                                out=dst[1][:, hi], in0=src[1][:, lo], scalar=ar, in1=src[1][:, hi], op0=ALU.mult, op1=ALU.add)),
                                reads=rd, writes=[tdst[1]])
                            P.op("dve", (lambda e: e.scalar_tensor_tensor(
                                out=dst[1][:, hi], in0=src[0][:, lo], scalar=ai, in1=dst[1][:, hi], op0=ALU.mult, op1=ALU.add)),
                                reads=rd + [tdst[1]], writes=[tdst[1]])
                        for ri in range(2):
                            for blk in range(5):
                                t0, n = BLK[blk]
                                p0 = pos(blk, d)
                                P.op("pe", (lambda e, ri=ri, blk=blk: e.matmul(
                                    PS[2 + blk][:, :n], lhsT=CP[:, sl, d, ri, :], rhs=X[0][ri][:, p0:p0 + n],
                                    start=(imm == 0), stop=(imm == nmm - 1))),
                                    reads=[t_CP, tX[0][ri], tX2[0][ri]], writes=[t_PS[2 + blk]])
                            imm += 1
                for blk in range(5):
                    t0, n = BLK[blk]
                    yi = blk % 2
                    P.op("dve", (lambda e, blk=blk: e.scalar_tensor_tensor(
                        out=ytmp[yi][:, :n], in0=HT[:, c, t0:t0 + n], scalar=dsk[:, c:c + 1], in1=PS[2 + blk][:, :n],
                        op0=ALU.mult, op1=ALU.add)),
                        reads=[tHT[c][blk], t_dsk, t_PS[2 + blk]], writes=[t_ytmp[yi]])
                    P.op("act", (lambda e: e.activation(out=HT[:, c, t0:t0 + n], in_=ytmp[yi][:, :n],
                                                        func=AF.Gelu_apprx_tanh)),
                         reads=[t_ytmp[yi]], writes=[tHT[c][blk]])
            P.barrier()
            wg = [sb(st, "wg%d" % i, [128, DC, 256], BF16) for i in range(2)]
            t_wg = [[T(), T()], [T(), T()]]
            sg = [sb(st, "sg%d" % i, [128, 512], F32) for i in range(2)]
            t_sg = [T(), T()]
            cnt = 0
            blocks = [0, 1, 2, 3, 4] if ctx_out else [1, 2, 3, 4]
            for c2 in range(DC):
                wi_ = c2 % 2
                for half in range(2):
                    P.dma("pool", wg[wi_][:, :, half * 128:(half + 1) * 128],
                          mixw["s5_wglu"][:, half * D + c2 * 128:half * D + (c2 + 1) * 128].rearrange("(k p) n -> p k n", p=128),
                          writes=[t_wg[wi_][half]])
                for blk in blocks:
                    t0, n = BLK[blk]
                    v = nb if blk == 0 else b
                    si = cnt % 2
                    cnt += 1
                    for half in range(2):
                        for k in range(DC):
                            P.op("pe", (lambda e, half=half, k=k: e.matmul(
                                PS[half][:, :n], lhsT=wg[wi_][:, k, half * 128:(half + 1) * 128], rhs=HT[:, k, t0:t0 + n],
                                start=(k == 0), stop=(k == DC - 1))),
                                reads=[t_wg[wi_][half], tHT[k][blk]], writes=[t_PS[half]])
                    P.op("act", (lambda e: e.activation(out=sg[si][:, :n], in_=PS[1][:, :n], func=AF.Sigmoid)),
                         reads=[t_PS[1]], writes=[t_sg[si]])
                    P.op("dve", (lambda e: e.tensor_tensor(out=sg[si][:, :n], in0=PS[0][:, :n], in1=sg[si][:, :n], op=ALU.mult)),
                         reads=[t_PS[0], t_sg[si]], writes=[t_sg[si]])
                    P.op("dve", (lambda e, c2=c2, v=v: e.scalar_tensor_tensor(
                        out=XT[:, c2, t0:t0 + n], in0=sg[si][:, :n], scalar=ada_ap(L, 2, c2, v), in1=XT[:, c2, t0:t0 + n],
                        op0=ALU.mult, op1=ALU.add)),
                        reads=[t_sg[si], t_ADA, tXT[c2][blk]], writes=[tXT[c2][blk]])
            P.barrier()

    def gqa(L, b):
        ctx_out = (L != 3) or debug_ctx
        qblocks = [0, 1, 2, 3, 4] if ctx_out else [1, 2, 3, 4]
        with contextlib.ExitStack() as st:
            HT = sb(st, "HT", [128, DC, NT], BF16)
            tHT = [[T() for _ in range(5)] for _ in range(DC)]
            with contextlib.ExitStack() as st2:
                mixer_norm(norm_scratch(st2), L, b, HT, tHT)
                P.barrier()
            OT = sb(st, "OT", [128, 8, NT], BF16)
            tOT = [[T() for _ in range(5)] for _ in range(8)]
            KT = sb(st, "KT", [128, 2, NT], BF16)
            tKT = [[T() for _ in range(5)] for _ in range(2)]
            V = sb(st, "V", [128, NT // 128, 256], BF16)
            tV = [T() for _ in range(NT // 128)]
            QT = [sb(st, "QT%d" % i, [128, NT], BF16) for i in range(2)]
            tQT = [[T() for _ in range(5)] for _ in range(2)]
            wqk = [sb(st, "wqk%d" % i, [128, DC, 256], BF16) for i in range(1)] * 2
            t_wqk = [T()] * 2
            gq = sb(st, "gq", [128, 4], F32)
            rope = sb(st, "rope", [128, 4, 64], F32)
            t_gq, t_rope = T(), T()
            onesH = sb(st, "onesH", [128, 128], BF16)
            t_onesH = T()
            P.op("pool", lambda e: e.memset(onesH[:], 1.0 / 128), writes=[t_onesH])
            P.dma("sp", gq[:], mixw["gqa_g"][:, :], writes=[t_gq])
            P.dma("sp", rope[:], mixw["gqa_rope"][:, :, :], writes=[t_rope])
            stv = contextlib.ExitStack()
            wv = sb(stv, "wv", [128, DC, 256], BF16)
            t_wv = T()
            P.dma("pool", wv[:], mixw["gqa_wv"].rearrange("(k p) n -> p k n", p=128), writes=[t_wv])
            for tt in range(NT // 128):
                pb = tt % 2
                blk = blk_of_tile(tt)
                for k in range(DC):
                    P.op("pe", (lambda e, k=k, tt=tt, pb=pb: e.matmul(
                        PS[pb][:, :256], lhsT=HT[:, k, tt * 128:(tt + 1) * 128], rhs=wv[:, k, :],
                        start=(k == 0), stop=(k == DC - 1))),
                        reads=[tHT[k][blk], t_wv], writes=[t_PS[pb]])
                if tt % 2 == 0:
                    P.op("dve", (lambda e, tt=tt, pb=pb: e.tensor_copy(out=V[:, tt, :], in_=PS[pb][:, :256])),
                         reads=[t_PS[pb]], writes=[tV[tt]])
                else:
                    P.op("act", (lambda e, tt=tt, pb=pb: e.copy(out=V[:, tt, :], in_=PS[pb][:, :256])),
                         reads=[t_PS[pb]], writes=[tV[tt]])
            P.barrier()
            stv.close()
            sqz = [sb(st, "sqz%d" % i, [128, 512], BF16) for i in range(1)] * 2
            t_sqz = [T()] * 2
            rq = [sb(st, "rq%d" % i, [128, 512], F32) for i in range(1)] * 2
            t_rq = [T()] * 2
            t1 = [sb(st, "t1_%d" % i, [128, 512], F32) for i in range(1)] * 2
            t2 = [sb(st, "t2_%d" % i, [128, 512], F32) for i in range(1)] * 2
            tt1, tt2 = [T()] * 2, [T()] * 2
            PT = [sb(st, "PT%d" % i, [128, 512], BF16) for i in range(2)]
            tPT = [T() for _ in range(2)]
            rs = sb(st, "rs", [128, 512], F32)
            trs = T()
            cnt = [0]

            def qk_unit(u, gcol, dst, dtile, blocks):
                wi_ = cnt[0] % 2
                cnt[0] += 1
                P.dma("pool", wqk[wi_][:], mixw["gqa_wqk"][u].rearrange("(k p) n -> p k n", p=128),
                      writes=[t_wqk[wi_]])
                for blk in blocks:
                    t0, n = BLK[blk]
                    i = blk % 2
                    for half in range(2 if blk > 0 else 1):
                        for k in range(DC):
                            P.op("pe", (lambda e, k=k, half=half, t0=t0, n=n: e.matmul(
                                PS[half][:, :n], lhsT=wqk[wi_][:, k, half * 128:(half + 1) * 128],
                                rhs=HT[:, k, t0:t0 + n], start=(k == 0), stop=(k == DC - 1))),
                                reads=[t_wqk[wi_], tHT[k][blk]], writes=[t_PS[half]])
                    P.op("act", (lambda e, i=i, n=n: e.activation(out=sqz[i][:, :n], in_=PS[0][:, :n], func=AF.Square)),
                         reads=[t_PS[0]], writes=[t_sqz[i]])
                    P.op("pe", (lambda e, i=i, n=n: e.matmul(PS[2][:, :n], lhsT=onesH[:], rhs=sqz[i][:, :n],
                                                             start=True, stop=True)),
                         reads=[t_onesH, t_sqz[i]], writes=[t_PS[2]])
                    P.op("act", (lambda e, i=i, n=n: e.activation(out=rq[i][:, :n], in_=PS[2][:, :n], func=AF.Sqrt,
                                                                  bias=EPS, scale=1.0)),
                         reads=[t_PS[2]], writes=[t_rq[i]])
                    P.op("dve", (lambda e, i=i, n=n: e.reciprocal(out=rq[i][:, :n], in_=rq[i][:, :n])),
                         reads=[t_rq[i]], writes=[t_rq[i]])
                    if blk == 0:
                        P.op("dve", (lambda e, i=i, n=n: e.scalar_tensor_tensor(
                            out=dst(blk), in0=PS[0][:, :n], scalar=gq[:, gcol:gcol + 1], in1=rq[i][:, :n],
                            op0=ALU.mult, op1=ALU.mult)),
                            reads=[t_PS[0], t_gq, t_rq[i]], writes=[dtile(blk)])
                        continue
                    r0 = (blk - 1) * 8
                    for hf in range(2):
                        p0, p1 = hf * 64, hf * 64 + 64
                        if hf == 0:
                            ctab = rope[p0:p1, 0, r0:r0 + 8].unsqueeze(2).broadcast_to([64, 8, 64])
                            stab = rope[p0:p1, 1, r0:r0 + 8].unsqueeze(2).broadcast_to([64, 8, 64])
                        else:
                            ctab = rope[p0:p1, 2, :].unsqueeze(1).broadcast_to([64, 8, 64])
                            stab = rope[p0:p1, 3, :].unsqueeze(1).broadcast_to([64, 8, 64])
                        z = PS[0][p0:p1, :n].rearrange("p (r c) -> p r c", c=64)
                        zs = PS[1][p0:p1, :n].rearrange("p (r c) -> p r c", c=64)
                        o1 = t1[i][p0:p1, :n].rearrange("p (r c) -> p r c", c=64)
                        o2 = t2[i][p0:p1, :n].rearrange("p (r c) -> p r c", c=64)
                        P.op("dve", (lambda e, z=z, o1=o1, ctab=ctab, p0=p0, p1=p1: e.scalar_tensor_tensor(
                            out=o1, in0=z, scalar=gq[p0:p1, gcol:gcol + 1], in1=ctab, op0=ALU.mult, op1=ALU.mult)),
                            reads=[t_PS[0], t_gq, t_rope], writes=[tt1[i]])
                        P.op("dve", (lambda e, zs=zs, o2=o2, stab=stab, p0=p0, p1=p1: e.scalar_tensor_tensor(
                            out=o2, in0=zs, scalar=gq[p0:p1, gcol + 1:gcol + 2], in1=stab, op0=ALU.mult, op1=ALU.mult)),
                            reads=[t_PS[1], t_gq, t_rope], writes=[tt2[i]])
                    P.op("pool", (lambda e, i=i, n=n: e.tensor_tensor(out=t1[i][:, :n], in0=t1[i][:, :n], in1=t2[i][:, :n],
                                                                      op=ALU.add)),
                         reads=[tt1[i], tt2[i]], writes=[tt1[i]])
                    P.op("dve", (lambda e, i=i, n=n, blk=blk: e.tensor_tensor(out=dst(blk), in0=t1[i][:, :n], in1=rq[i][:, :n],
                                                                               op=ALU.mult)),
                         reads=[tt1[i], t_rq[i]], writes=[dtile(blk)])

            for kv in range(2):
                qk_unit(8 + kv, 2, (lambda blk, kv=kv: KT[:, kv, BLK[blk][0]:BLK[blk][0] + BLK[blk][1]]),
                        (lambda blk, kv=kv: tKT[kv][blk]), [0, 1, 2, 3, 4])
            scale = 128.0 ** -0.5
            for h in range(8):
                qi = h % 2
                kv = h // 4
                qk_unit(h, 0, (lambda blk, qi=qi: QT[qi][:, BLK[blk][0]:BLK[blk][0] + BLK[blk][1]]),
                        (lambda blk, qi=qi: tQT[qi][blk]), qblocks)
                for blk in qblocks:
                    t0, n = BLK[blk]
                    key_tiles = [0, 1] if blk == 0 else list(range(NT // 128))
                    attention(QT[qi][:, t0:t0 + n], tQT[qi][blk],
                              (lambda kt, kv=kv: KT[:, kv, kt * 128:(kt + 1) * 128]),
                              (lambda kt, kv=kv: tKT[kv][blk_of_tile(kt)]),
                              (lambda kt, kv=kv: V[:, kt, kv * 128:(kv + 1) * 128]), (lambda kt: tV[kt]),
                              key_tiles, n, 128, 128, scale, OT[:, h, t0:t0 + n], tOT[h][blk], PT, tPT, rs, trs,
                              ones1[:, :])
            out_proj(L, b, mixw["gqa_wo"], OT, tOT, 8, 128, qblocks, st)
            P.barrier()

    def ffn(L, b):
        with contextlib.ExitStack() as st:
            ns = norm_scratch(st)
            GT = sb(st, "GT", [128, HC, 512], BF16)
            HTb = [sb(st, "HTb%d" % i, [128, DC, 512], BF16) for i in range(2)]
            tHTb = [[T() for _ in range(DC)] for _ in range(2)]
            silu_t = [sb(st, "silu%d" % i, [128, 512], F32) for i in range(2)]
            NWI, NWO = 4, 3
            wi = [sb(st, "wi%d" % i, [128, DC, 256], BF16) for i in range(NWI)]
            wo = [sb(st, "wo%d" % i, [128, HC, 128], BF16) for i in range(NWO)]
            t_GT = [T() for _ in range(HC)]
            t_silu = [T(), T()]
            t_wi = [[T(), T()] for _ in range(NWI)]
            t_wo = [T() for _ in range(NWO)]
            make_gm(L, nffn, t_nffn, 4, b)
            wic = woc = 0
            for blk in range(5):
                if blk == 0 and L == 3 and not debug_ctx:
                    continue
                t0, n = BLK[blk]
                v = nb if blk == 0 else b
                hb = blk % 2
                norm_block(ns, L, b, blk, 3, (lambda c, hb=hb, n=n: HTb[hb][:, c, :n]), (lambda c, hb=hb: tHTb[hb][c]))
                for j in range(HC):
                    wi_i = wic % NWI
                    wic += 1
                    for half in range(2):
                        c0 = half * FH + j * 128
                        P.dma("pool", wi[wi_i][:, :, half * 128:(half + 1) * 128],
                              ffn_wi_d[L][:, c0:c0 + 128].rearrange("(k p) n -> p k n", p=128),
                              writes=[t_wi[wi_i][half]])
                    pa, pbk = (0, 1) if j % 2 == 0 else (2, 3)
                    for half, pbank in ((0, pa), (1, pbk)):
                        for k in range(DC):
                            P.op("pe", (lambda e, wi_i=wi_i, half=half, k=k, pbank=pbank, hb=hb: e.matmul(
                                PS[pbank][:, :n], lhsT=wi[wi_i][:, k, half * 128:(half + 1) * 128],
                                rhs=HTb[hb][:, k, :n], start=(k == 0), stop=(k == DC - 1))),
                                reads=[t_wi[wi_i][half], tHTb[hb][k]], writes=[t_PS[pbank]])
                    si = j % 2
                    P.op("act", (lambda e, si=si, pa=pa: e.activation(out=silu_t[si][:, :n], in_=PS[pa][:, :n],
                                                                      func=AF.Silu)),
                         reads=[t_PS[pa]], writes=[t_silu[si]])
                    P.op("dve", (lambda e, si=si, pbk=pbk, j=j: e.tensor_tensor(
                        out=GT[:, j, :n], in0=PS[pbk][:, :n], in1=silu_t[si][:, :n], op=ALU.mult)),
                        reads=[t_PS[pbk], t_silu[si]], writes=[t_GT[j]])
                for c in range(DC):
                    wo_i = woc % NWO
                    woc += 1
                    P.dma("pool", wo[wo_i][:],
                          ffn_wo_d[L][:, c * 128:(c + 1) * 128].rearrange("(j p) n -> p j n", p=128),
                          writes=[t_wo[wo_i]])
                    pbank = 4
                    for j in range(HC):
                        P.op("pe", (lambda e, wo_i=wo_i, j=j, pbank=pbank: e.matmul(
                            PS[pbank][:, :n], lhsT=wo[wo_i][:, j, :], rhs=GT[:, j, :n],
                            start=(j == 0), stop=(j == HC - 1))),
                            reads=[t_wo[wo_i], t_GT[j]], writes=[t_PS[pbank]])
                    P.op("dve", (lambda e, c=c, pbank=pbank, v=v: e.scalar_tensor_tensor(
                        out=XT[:, c, t0:t0 + n], in0=PS[pbank][:, :n], scalar=ada_ap(L, 5, c, v),
                        in1=XT[:, c, t0:t0 + n], op0=ALU.mult, op1=ALU.add)),
                        reads=[t_PS[pbank], t_ADA, tXT[c][blk]], writes=[tXT[c][blk]])
            P.barrier()

    if do_mixer and 1 in layers:
        s5_consts()
    for b in range(nb):
        load_tokens(b)
        for L in layers:
            if do_mixer:
                {0: mla, 1: s5, 2: na, 3: gqa}[L](L, b)
            if do_ffn:
                ffn(L, b)
        store_tokens(b)
        if b < nb - 1:
            P.reset_epoch()

    P.finish()
    gs.close()
    print("instructions:", P.n_instr, {e: P.count[e] for e in P.ENGS})
    return nc


def _fm(vec_rows):
    a = np.asarray(vec_rows, dtype=np.float32)
    lead = a.shape[:-1]
    nchunk = a.shape[-1] // 128
    a = a.reshape(lead + (nchunk, 128))
    return np.ascontiguousarray(np.moveaxis(a, -1, 0))


def _rope_tables(rot_dim):
    axis_dim = rot_dim // 2
    nf = axis_dim // 2
    inv_freq = (np.float32(10000.0) ** (-np.arange(0, axis_dim, 2, dtype=np.float32) / np.float32(axis_dim))).astype(np.float32)
    rows = np.arange(32, dtype=np.float32)[:, None] * inv_freq[None, :]
    cols = np.arange(64, dtype=np.float32)[:, None] * inv_freq[None, :]
    sign = np.concatenate([-np.ones(nf, np.float32), np.ones(nf, np.float32)])
    idx = np.concatenate([np.arange(nf), np.arange(nf)])
    cos_r = np.cos(rows)[:, idx].T.astype(np.float32)
    sin_r = (np.sin(rows)[:, idx] * sign[None, :]).T.astype(np.float32)
    cos_c = np.cos(cols)[:, idx].T.astype(np.float32)
    sin_c = (np.sin(cols)[:, idx] * sign[None, :]).T.astype(np.float32)
    return cos_r, sin_r, cos_c, sin_c


def _prep_gqa(inputs):
    w = inputs["gqa_w_qkv"][0]
    partner = np.concatenate([np.arange(32, 64), np.arange(0, 32), np.arange(96, 128), np.arange(64, 96)])
    units = []
    for u in range(10):
        cols = np.arange(u * 128, (u + 1) * 128)
        units.append(np.concatenate([w[:, cols], w[:, cols[partner]]], axis=1))
    gq, gk = inputs["gqa_g_qn"][0], inputs["gqa_g_kn"][0]
    cos_r, sin_r, cos_c, sin_c = _rope_tables(128)
    rope = np.zeros((128, 4, 64), np.float32)
    rope[:64, 0, :32], rope[:64, 1, :32] = cos_r, sin_r
    rope[64:, 2, :], rope[64:, 3, :] = cos_c, sin_c
    return {
        "gqa_wqk": np.ascontiguousarray(np.stack(units)),
        "gqa_wv": np.ascontiguousarray(w[:, 1280:1536]),
        "gqa_wo": np.ascontiguousarray(inputs["gqa_w_o"][0]),
        "gqa_g": np.ascontiguousarray(np.stack([gq, gq[partner], gk, gk[partner]], axis=1)),
        "gqa_rope": rope,
    }


def _prep_mla(inputs):
    w_in, w_uq, w_ukv = inputs["mla_w_in"][0], inputs["mla_w_uq"][0], inputs["mla_w_ukv"][0]
    orig = -np.ones(128, np.int64)
    orig[0:16] = 64 + np.arange(16)
    orig[32:48] = 80 + np.arange(16)
    orig[64:128] = np.arange(64)
    partner = np.arange(128)
    partner[0:8], partner[8:16] = np.arange(8, 16), np.arange(0, 8)
    partner[32:40], partner[40:48] = np.arange(40, 48), np.arange(32, 40)
    valid = orig >= 0
    is_rope = np.zeros(128, bool)
    is_rope[0:16] = True
    is_rope[32:48] = True
    wuq = np.zeros((16, 384, 256), np.float32)
    wuk = np.zeros((16, 256, 128), np.float32)
    wv = np.zeros((256, 1024), np.float32)
    for h in range(16):
        wuq[h][:, np.nonzero(valid)[0]] = w_uq[:, h * 96 + orig[valid]]
        rp = np.nonzero(is_rope)[0]
        wuq[h][:, 128 + rp] = w_uq[:, h * 96 + orig[partner[rp]]]
        wuk[h][:, 64:128] = w_ukv[:, h * 128:h * 128 + 64]
        wv[:, h * 64:(h + 1) * 64] = w_ukv[:, h * 128 + 64:h * 128 + 128]
    win = np.zeros((D, 896), np.float32)
    win[:, 0:640] = w_in[:, 0:640]
    rp = np.nonzero(is_rope)[0]
    win[:, 640 + rp] = w_in[:, 640 + (orig[rp] - 64)]
    win[:, 768 + rp] = w_in[:, 640 + (orig[partner[rp]] - 64)]
    gl = np.concatenate([inputs["mla_g_q"][0].reshape(3, 128), inputs["mla_g_kv"][0].reshape(2, 128)], axis=0).T
    gq, gk = inputs["mla_g_qn"][0], inputs["mla_g_kn"][0]
    g4 = np.zeros((128, 4), np.float32)
    g4[valid, 0], g4[valid, 2] = gq[orig[valid]], gk[orig[valid]]
    g4[rp, 1], g4[rp, 3] = gq[orig[partner[rp]]], gk[orig[partner[rp]]]
    cos_r, sin_r, cos_c, sin_c = _rope_tables(32)
    rope = np.zeros((128, 4, 64), np.float32)
    rope[0:16, 0, :32], rope[0:16, 1, :32] = cos_r, sin_r
    rope[32:48, 2, :], rope[32:48, 3, :] = cos_c, sin_c
    return {"mla_win": win, "mla_gl": np.ascontiguousarray(gl), "mla_wuq": wuq, "mla_wuk": wuk, "mla_wv": wv,
            "mla_wo": np.ascontiguousarray(inputs["mla_w_o"][0]), "mla_g": g4, "mla_rope": rope}


def _prep_na(inputs):
    rpb = inputs["na_rpb"][0]
    kc = np.arange(64)[:, None]
    qc = np.arange(64)[None, :]
    c0 = np.clip(qc - 8, 0, 48)
    ok = (kc >= c0) & (kc < c0 + 16)
    idx = np.clip(kc - qc + 15, 0, 30)
    M = rpb[:, :, idx]
    M = np.where(ok[None, None], M, np.float32(-1e4)).astype(np.float32)
    gq, gk = inputs["na_g_qn"][0], inputs["na_g_kn"][0]
    return {"na_wqkv": np.ascontiguousarray(inputs["na_w_qkv"][0]), "na_wo": np.ascontiguousarray(inputs["na_w_o"][0]),
            "na_g": np.ascontiguousarray(np.stack([np.tile(gq, 2), np.tile(gk, 2)], axis=1)),
            "na_M": np.ascontiguousarray(np.transpose(M, (0, 2, 1, 3)))}


def _prep_s5(inputs):
    a_re, a_im, ldt = inputs["s5_a_re"][0], inputs["s5_a_im"][0], inputs["s5_log_dt"][0]
    def sm(x):
        return np.ascontiguousarray(np.transpose(x.reshape(2, 32, 2, 64), (2, 3, 0, 1)).reshape(128, 64))
    a_s = np.stack([sm(a_re), sm(a_im)], axis=1)
    ldt_s = sm(np.broadcast_to(ldt[:, :, None], (2, 64, 64)))
    b = np.stack([inputs["s5_b_re"][0], inputs["s5_b_im"][0]])
    cm = np.stack([inputs["s5_c_re"][0], inputs["s5_c_im"][0]])
    return {"s5_as": np.ascontiguousarray(a_s), "s5_ldt": ldt_s,
            "s5_b": np.ascontiguousarray(np.transpose(b, (2, 4, 1, 0, 3)).reshape(64, 16, 4, 64)),
            "s5_c": np.ascontiguousarray(np.transpose(cm, (2, 4, 1, 0, 3)).reshape(64, 64, 4, 16)),
            "s5_d": _fm(inputs["s5_d"][0]), "s5_wglu": np.ascontiguousarray(inputs["s5_w_glu"][0])}


def make_in_maps(inputs, nb, n_cores, layers=(0, 1, 2, 3), do_mixer=True, do_ffn=True):
    maps = []
    x, c, ctx, c_ctx = inputs["x"], inputs["c"], inputs["ctx"], inputs["c_ctx"]
    shared = {
        "ident": np.eye(128, dtype=np.float32),
        "ada_b": _fm(inputs["ada_b"]),
        "norm_mix": _fm(inputs["norm_mix"]),
        "norm_ffn": _fm(inputs["norm_ffn"]),
    }
    for L in layers:
        shared["ada_w%d" % L] = inputs["ada_w"][L]
        if do_ffn:
            shared["ffn_wi%d" % L] = inputs["ffn_w_in"][L]
            shared["ffn_wo%d" % L] = inputs["ffn_w_out"][L]
    if do_mixer and 3 in layers:
        shared.update(_prep_gqa(inputs))
    if do_mixer and 0 in layers:
        shared.update(_prep_mla(inputs))
    if do_mixer and 2 in layers:
        shared.update(_prep_na(inputs))
    if do_mixer and 1 in layers:
        shared.update(_prep_s5(inputs))
    for core in range(n_cores):
        sl = slice(core * nb, (core + 1) * nb)
        cvecs = np.concatenate([c[sl], c_ctx[None, :]], axis=0)
        cc = np.ascontiguousarray(np.transpose(cvecs.reshape(nb + 1, DC, 128), (2, 1, 0)))
        m = dict(shared)
        m.update({"x": np.ascontiguousarray(x[sl]), "ctx": np.ascontiguousarray(ctx[sl]), "cc": cc})
        maps.append(m)
    return maps


def run(inputs, nb=4, n_cores=N_CORES, layers=(0, 1, 2, 3), debug_ctx=False, **kw):
    nc = build_program(nb, list(layers), debug_ctx, **kw)
    maps = make_in_maps(inputs, nb, n_cores, layers, kw.get("do_mixer", True), kw.get("do_ffn", True))
    res = run_bass_kernel_spmd(nc, maps, core_ids=list(range(n_cores)))
    out = np.concatenate([r["out"] for r in res.results], axis=0)
    if debug_ctx:
        outc = np.concatenate([r["out_ctx"] for r in res.results], axis=0)
        return out, outc
    return out


def kernel(**inputs):
    inputs = {k: np.asarray(v) for k, v in inputs.items()}
    return run(inputs).astype(np.float32)
```

```python
import numpy as np
import concourse.bass as bass
import concourse.mybir as mybir
from concourse.bass_utils import run_bass_kernel_spmd

F32, BF16 = mybir.dt.float32, mybir.dt.bfloat16
ALU, AF, AX = mybir.AluOpType, mybir.ActivationFunctionType, mybir.AxisListType

D = 1024
DC = 8
CTX = 256
SEQ = 2048
NT = CTX + SEQ
FH = 2816
HC = 22
EPS = 1e-6
N_CORES = 8


class Tile:
    __slots__ = ("ap", "name", "w", "r")
    registry = []

    def __init__(self, ap, name=""):
        self.ap, self.name, self.w, self.r = ap, name, None, {}
        Tile.registry.append(self)

    def __getitem__(self, idx):
        return self.ap[idx]


class Prog:
    ENGS = ("pe", "act", "dve", "pool", "sp")

    def __init__(self, nc, stack, n_dma_sems=8, n_sets=4):
        self.nc = nc
        self.eng = {"pe": nc.tensor, "act": nc.scalar, "dve": nc.vector, "pool": nc.gpsimd, "sp": nc.sync}
        self.n_dma = n_dma_sems
        self.dma_q = ("sp", "pool")
        self.sets = []
        for s_ in range(n_sets):
            d = {}
            for e in self.ENGS:
                d[e] = stack.enter_context(nc.semaphore("s%d_%s" % (s_, e)))
            for q in self.dma_q:
                for j in range(n_dma_sems):
                    d[("dma", q, j)] = stack.enter_context(nc.semaphore("s%d_dma_%s%d" % (s_, q, j)))
            self.sets.append(d)
        self.epoch = 0
        self.sems = self.sets[0]
        self.count = {e: 0 for e in self.ENGS}
        self.seen = {e: {} for e in self.ENGS}
        self.dma_uses = {q: [0] * n_dma_sems for q in self.dma_q}
        self.dma_rr = {q: 0 for q in self.dma_q}
        self.n_instr = 0
        Tile.registry = []

    def reset_epoch(self):
        self.barrier()
        self.epoch += 1
        self.sems = self.sets[self.epoch]
        self.count = {e: 0 for e in self.ENGS}
        self.seen = {e: {} for e in self.ENGS}
        self.dma_uses = {q: [0] * self.n_dma for q in self.dma_q}
        for t in Tile.registry:
            t.w, t.r = None, {}

    def _need(self, eng, key, val):
        if self.seen[eng].get(key, 0) >= val:
            return
        self.seen[eng][key] = val
        self.eng[eng].wait_ge(self.sems[key], val)

    def _deps(self, eng, reads, writes):
        for t in reads:
            if t.w is not None:
                key, val, src = t.w
                if not (src == eng and eng == "pe"):
                    self._need(eng, key, val)
        strict = eng == "pool"
        for t in writes:
            if t.w is not None:
                key, val, src = t.w
                if src != eng or strict:
                    self._need(eng, key, val)
            for key, (val, src) in t.r.items():
                if src != eng or strict:
                    self._need(eng, key, val)

    def _mark(self, tok, reads, writes):
        key, val, src = tok
        for t in reads:
            t.r[key] = (val, src)
        for t in writes:
            t.w = tok
            t.r = {}

    def op(self, eng, fn, reads=(), writes=()):
        self._deps(eng, reads, writes)
        self.count[eng] += 1
        tok = (eng, self.count[eng], eng)
        self._mark(tok, reads, writes)
        fn(self.eng[eng]).then_inc(self.sems[eng], 1)
        self.n_instr += 1
        return tok

    def dma(self, eng, out_ap, in_ap, reads=(), writes=()):
        j = self.dma_rr[eng]
        self.dma_rr[eng] = (j + 1) % self.n_dma
        self.dma_uses[eng][j] += 1
        u = self.dma_uses[eng][j]
        key = ("dma", eng, j)
        if u > 1:
            self._need(eng, key, 16 * (u - 1))
        self._deps(eng, reads, writes)
        tok = (key, 16 * u, "dma")
        self._mark(tok, reads, writes)
        self.eng[eng].dma_start(out=out_ap, in_=in_ap).then_inc(self.sems[key], 16)
        self.n_instr += 1
        return tok

    def barrier(self):
        for e in self.ENGS:
            for f in self.ENGS:
                if f != e and self.count[f]:
                    self._need(e, f, self.count[f])
            for q in self.dma_q:
                for j in range(self.n_dma):
                    if self.dma_uses[q][j]:
                        self._need(e, ("dma", q, j), 16 * self.dma_uses[q][j])

    def finish(self):
        for q in self.dma_q:
            for j in range(self.n_dma):
                if self.dma_uses[q][j]:
                    self._need("sp", ("dma", q, j), 16 * self.dma_uses[q][j])


def build_program(nb, layers, debug_ctx=False, do_mixer=True, do_ffn=True):
    import contextlib
    nc = bass.Bass("TRN2", target_bir_lowering=False)
    gs = contextlib.ExitStack()
    P = Prog(nc, gs)
    nv = nb + 1

    def din(name, shape, dt=F32):
        return nc.dram_tensor(name, list(shape), dt, kind="ExternalInput").ap()

    def dout(name, shape, dt=F32):
        return nc.dram_tensor(name, list(shape), dt, kind="ExternalOutput").ap()

    uid = [0]

    def sb(st, name, shape, dt):
        uid[0] += 1
        return st.enter_context(nc.sbuf_tensor("%s_u%d" % (name, uid[0]), list(shape), dt))

    def T(name=""):
        return Tile(None, name)

    x_d = din("x", [nb, SEQ, D])
    ctx_d = din("ctx", [nb, CTX, D])
    cc_d = din("cc", [128, DC, nv])
    ident_d = din("ident", [128, 128])
    ada_w_d = {L: din("ada_w%d" % L, [D, 6 * D]) for L in layers}
    ada_b_d = din("ada_b", [128, 4, 48])
    nmix_d = din("norm_mix", [128, 4, DC])
    nffn_d = din("norm_ffn", [128, 4, DC])
    ffn_wi_d = {L: din("ffn_wi%d" % L, [D, 2 * FH]) for L in layers} if do_ffn else {}
    ffn_wo_d = {L: din("ffn_wo%d" % L, [FH, D]) for L in layers} if do_ffn else {}
    mixw = {}
    if do_mixer and 0 in layers:
        mixw["mla_win"] = din("mla_win", [D, 896])
        mixw["mla_gl"] = din("mla_gl", [128, 5])
        mixw["mla_wuq"] = din("mla_wuq", [16, 384, 256])
        mixw["mla_wuk"] = din("mla_wuk", [16, 256, 128])
        mixw["mla_wv"] = din("mla_wv", [256, 1024])
        mixw["mla_wo"] = din("mla_wo", [D, D])
        mixw["mla_g"] = din("mla_g", [128, 4])
        mixw["mla_rope"] = din("mla_rope", [128, 4, 64])
    if do_mixer and 1 in layers:
        mixw["s5_as"] = din("s5_as", [128, 2, 64])
        mixw["s5_ldt"] = din("s5_ldt", [128, 64])
        mixw["s5_b"] = din("s5_b", [64, 16, 4, 64])
        mixw["s5_c"] = din("s5_c", [64, 64, 4, 16])
        mixw["s5_d"] = din("s5_d", [128, DC])
        mixw["s5_wglu"] = din("s5_wglu", [D, 2 * D])
    if do_mixer and 2 in layers:
        mixw["na_wqkv"] = din("na_wqkv", [D, 3 * D])
        mixw["na_wo"] = din("na_wo", [D, D])
        mixw["na_g"] = din("na_g", [128, 2])
        mixw["na_M"] = din("na_M", [16, 64, 15, 64])
    if do_mixer and 3 in layers:
        mixw["gqa_wqk"] = din("gqa_wqk", [10, D, 256])
        mixw["gqa_wv"] = din("gqa_wv", [D, 256])
        mixw["gqa_wo"] = din("gqa_wo", [D, D])
        mixw["gqa_g"] = din("gqa_g", [128, 4])
        mixw["gqa_rope"] = din("gqa_rope", [128, 4, 64])
    out_d = dout("out", [nb, SEQ, D])
    outc_d = dout("out_ctx", [nb, CTX, D]) if debug_ctx else None

    XT = sb(gs, "XT", [128, DC, NT], F32)
    ident = sb(gs, "ident", [128, 128], F32)
    onesD = sb(gs, "onesD", [128, 128], BF16)
    ADA = sb(gs, "ADA", [128, 4, 48, nv], F32)
    nmix = sb(gs, "nmix", [128, 4, DC], F32)
    nffn = sb(gs, "nffn", [128, 4, DC], F32)
    GM = sb(gs, "GM", [128, 2, DC], F32)
    PS = [gs.enter_context(nc.psum_tensor("ps%d" % i, [128, 512], F32)) for i in range(8)]

    BLK = [(0, 256), (256, 512), (768, 512), (1280, 512), (1792, 512)]
    tXT = [[T() for _ in range(5)] for _ in range(DC)]
    t_ident, t_ones, t_ADA, t_nmix, t_nffn, t_GM = T(), T(), T(), T(), T(), T()
    t_PS = [T() for _ in range(8)]

    P.op("pool", lambda e: e.memset(onesD[:], 1.0 / D), writes=[t_ones])
    ones1 = sb(gs, "ones1", [128, 128], BF16)
    t_ones1 = T()
    P.op("pool", lambda e: e.memset(ones1[:], 1.0), writes=[t_ones1])
    P.dma("sp", ident[:], ident_d[:, :], writes=[t_ident])
    P.dma("sp", nmix[:], nmix_d[:, :, :], writes=[t_nmix])
    P.dma("sp", nffn[:], nffn_d[:, :, :], writes=[t_nffn])

    with contextlib.ExitStack() as st:
        cc = sb(st, "cc_sb", [128, DC, nv], F32)
        scc = sb(st, "scc", [128, DC, nv], F32)
        adab = sb(st, "adab", [128, 4, 48], F32)
        adaw = [sb(st, "adaw%d" % i, [128, DC, 768], F32) for i in range(2)]
        t_cc, t_scc, t_adab = T(), T(), T()
        t_adaw = [T() for _ in range(2)]
        P.dma("sp", cc[:], cc_d[:, :, :], writes=[t_cc])
        P.dma("sp", adab[:], ada_b_d[:, :, :], writes=[t_adab])
        P.op("act", lambda e: e.activation(out=scc[:], in_=cc[:], func=AF.Silu), reads=[t_cc], writes=[t_scc])
        cnt = 0
        for L in layers:
            for g in range(8):
                bi = cnt % 2
                cnt += 1
                P.dma("sp", adaw[bi][:], ada_w_d[L][:, g * 768:(g + 1) * 768].rearrange("(k p) n -> p k n", p=128),
                      writes=[t_adaw[bi]])
                for jj in range(6):
                    j = g * 6 + jj
                    for k in range(DC):
                        P.op("pe", (lambda e, bi=bi, j=j, jj=jj, k=k: e.matmul(
                            PS[0][:, j * nv:(j + 1) * nv], lhsT=adaw[bi][:, k, jj * 128:(jj + 1) * 128],
                            rhs=scc[:, k, :], start=(k == 0), stop=(k == DC - 1))),
                            reads=[t_adaw[bi], t_scc], writes=[t_PS[0]])
            for v in range(nv):
                P.op("dve", (lambda e, L=L, v=v: e.tensor_tensor(
                    out=ADA[:, L, :, v],
                    in0=PS[0][:, 0:48 * nv].rearrange("p (j v) -> p j v", v=nv)[:, :, v],
                    in1=adab[:, L, :], op=ALU.add)),
                    reads=[t_PS[0], t_adab], writes=[t_ADA])
        P.barrier()

    def ada_ap(L, term, c, v):
        return ADA[:, L, term * DC + c, v:v + 1]

    def blk_of_tile(tt):
        return 0 if tt < 2 else 1 + (tt - 2) // 4

    def load_tokens(b):
        with contextlib.ExitStack() as st:
            io = [sb(st, "io%d" % i, [128, D], F32) for i in range(3)]
            t_io = [[T(), T()] for _ in range(3)]
            for tt in range(NT // 128):
                bi = tt % 3
                src = ctx_d[b, tt * 128:(tt + 1) * 128, :] if tt < 2 else x_d[b, (tt - 2) * 128:(tt - 1) * 128, :]
                P.dma("sp", io[bi][:], src, writes=t_io[bi])
                blk = blk_of_tile(tt)
                for half in range(2):
                    pb = 6 + half
                    for cq in range(4):
                        c = half * 4 + cq
                        P.op("pe", (lambda e, bi=bi, c=c, cq=cq, pb=pb: e.transpose(
                            out=PS[pb][:, cq * 128:(cq + 1) * 128], in_=io[bi][:, c * 128:(c + 1) * 128],
                            identity=ident[:])),
                            reads=[t_io[bi][half], t_ident], writes=[t_PS[pb]])
                    dst = XT[:, half * 4:(half + 1) * 4, tt * 128:(tt + 1) * 128]
                    srcp = PS[pb][:, :].rearrange("p (c t) -> p c t", c=4)
                    wr = [tXT[half * 4 + cq][blk] for cq in range(4)]
                    if half == 0:
                        P.op("dve", (lambda e, dst=dst, srcp=srcp: e.tensor_copy(out=dst, in_=srcp)),
                             reads=[t_PS[pb]], writes=wr)
                    else:
                        P.op("act", (lambda e, dst=dst, srcp=srcp: e.copy(out=dst, in_=srcp)),
                             reads=[t_PS[pb]], writes=wr)
            P.barrier()

    def store_tokens(b):
        with contextlib.ExitStack() as st:
            io = [sb(st, "so%d" % i, [128, D], F32) for i in range(3)]
            t_io = [[T(), T()] for _ in range(3)]
            tiles = range(NT // 128) if debug_ctx else range(2, NT // 128)
            for n_, tt in enumerate(tiles):
                bi = n_ % 3
                blk = blk_of_tile(tt)
                for half in range(2):
                    pb = 6 + half
                    for cq in range(4):
                        c = half * 4 + cq
                        P.op("pe", (lambda e, c=c, cq=cq, pb=pb, tt=tt: e.transpose(
                            out=PS[pb][:, cq * 128:(cq + 1) * 128], in_=XT[:, c, tt * 128:(tt + 1) * 128],
                            identity=ident[:])),
                            reads=[tXT[c][blk], t_ident], writes=[t_PS[pb]])
                    dst = io[bi][:, half * 512:(half + 1) * 512]
                    if half == 0:
                        P.op("dve", (lambda e, dst=dst, pb=pb: e.tensor_copy(out=dst, in_=PS[pb][:, :])),
                             reads=[t_PS[pb]], writes=[t_io[bi][0]])
                    else:
                        P.op("act", (lambda e, dst=dst, pb=pb: e.copy(out=dst, in_=PS[pb][:, :])),
                             reads=[t_PS[pb]], writes=[t_io[bi][1]])
                dstd = outc_d[b, tt * 128:(tt + 1) * 128, :] if tt < 2 else out_d[b, (tt - 2) * 128:(tt - 1) * 128, :]
                P.dma("sp", dstd, io[bi][:], reads=t_io[bi])
            P.barrier()

    def make_gm(L, normt, t_norm, term_scale, b):
        for s_, v in ((0, b), (1, nb)):
            P.op("dve", (lambda e, s_=s_, v=v: e.scalar_tensor_tensor(
                out=GM[:, s_, :], in0=ADA[:, L, term_scale * DC:(term_scale + 1) * DC, v], scalar=1.0,
                in1=normt[:, L, :], op0=ALU.add, op1=ALU.mult)),
                reads=[t_ADA, t_norm], writes=[t_GM])

    def norm_block(nb_, L, b, blk, term_shift, dst, dtile):
        sq, rstd, tmpn, t_sq, t_rstd, t_tmpn = nb_["sq"], nb_["rstd"], nb_["tmpn"], nb_["t_sq"], nb_["t_rstd"], nb_["t_tmpn"]
        t0, n = BLK[blk]
        s_ = 1 if blk == 0 else 0
        v = nb if blk == 0 else b
        pb = 5
        for c in range(DC):
            i = c % 2
            P.op("pool", (lambda e, i=i, c=c: e.tensor_tensor(
                out=sq[i][:, :n], in0=XT[:, c, t0:t0 + n], in1=XT[:, c, t0:t0 + n], op=ALU.mult)),
                reads=[tXT[c][blk]], writes=[t_sq[i]])
            P.op("pe", (lambda e, i=i, c=c: e.matmul(
                PS[pb][:, :n], lhsT=onesD[:], rhs=sq[i][:, :n], start=(c == 0), stop=(c == DC - 1))),
                reads=[t_sq[i], t_ones], writes=[t_PS[pb]])
        ri = blk % 2
        P.op("act", (lambda e: e.activation(out=rstd[ri][:, :n], in_=PS[pb][:, :n], func=AF.Sqrt,
                                            bias=EPS, scale=1.0)),
             reads=[t_PS[pb]], writes=[t_rstd[ri]])
        P.op("dve", (lambda e: e.reciprocal(out=rstd[ri][:, :n], in_=rstd[ri][:, :n])),
             reads=[t_rstd[ri]], writes=[t_rstd[ri]])
        for c in range(DC):
            i = c % 2
            P.op("dve", (lambda e, i=i, c=c: e.tensor_tensor(
                out=tmpn[i][:, :n], in0=XT[:, c, t0:t0 + n], in1=rstd[ri][:, :n], op=ALU.mult)),
                reads=[tXT[c][blk], t_rstd[ri]], writes=[t_tmpn[i]])
            P.op("act", (lambda e, i=i, c=c: e.activation(
                out=dst(c), in_=tmpn[i][:, :n], func=AF.Identity,
                bias=ada_ap(L, term_shift, c, v), scale=GM[:, s_, c:c + 1])),
                reads=[t_tmpn[i], t_GM, t_ADA], writes=[dtile(c)])

    def norm_scratch(st):
        return {
            "sq": [sb(st, "sq%d" % i, [128, 512], BF16) for i in range(2)],
            "rstd": [sb(st, "rstd%d" % i, [128, 512], F32) for i in range(2)],
            "tmpn": [sb(st, "tmpn%d" % i, [128, 512], F32) for i in range(2)],
            "t_sq": [T(), T()], "t_rstd": [T(), T()], "t_tmpn": [T(), T()],
        }


    def mixer_norm(ns, L, b, HT, tHT):
        make_gm(L, nmix, t_nmix, 1, b)
        for blk in range(5):
            t0, n = BLK[blk]
            norm_block(ns, L, b, blk, 0, (lambda c, t0=t0, n=n: HT[:, c, t0:t0 + n]), (lambda c, blk=blk: tHT[c][blk]))

    def attention(qT, tq, kT, tk, vfn, tv, key_tiles, n, dk, dv, scale, outT, tout, PT, tPT, rs, trs, onesv, po=0):
        nk = len(key_tiles)
        for i, kt in enumerate(key_tiles):
            sbk = 3 + (i % 2)
            P.op("pe", (lambda e, kt=kt, sbk=sbk: e.matmul(PS[sbk][:, :n], lhsT=kT(kt), rhs=qT, start=True, stop=True)),
                 reads=[tk(kt), tq], writes=[t_PS[sbk]])
            pi = i % len(PT)
            P.op("act", (lambda e, sbk=sbk, pi=pi: e.activation(out=PT[pi][:, :n], in_=PS[sbk][:, :n], func=AF.Exp,
                                                                scale=scale)),
                 reads=[t_PS[sbk]], writes=[tPT[pi]])
            P.op("pe", (lambda e, kt=kt, pi=pi, i=i: e.matmul(PS[5][po:po + dv, :n], lhsT=vfn(kt), rhs=PT[pi][:, :n],
                                                              start=(i == 0), stop=(i == nk - 1))),
                 reads=[tv(kt), tPT[pi]], writes=[t_PS[5]])
            P.op("pe", (lambda e, pi=pi, i=i: e.matmul(PS[6][po:po + dv, :n], lhsT=onesv, rhs=PT[pi][:, :n],
                                                       start=(i == 0), stop=(i == nk - 1))),
                 reads=[t_ones1, tPT[pi]], writes=[t_PS[6]])
        P.op("dve", (lambda e: e.reciprocal(out=rs[po:po + dv, :n], in_=PS[6][po:po + dv, :n])), reads=[t_PS[6]], writes=[trs])
        P.op("dve", (lambda e: e.tensor_tensor(out=outT, in0=PS[5][po:po + dv, :n], in1=rs[po:po + dv, :n], op=ALU.mult)),
             reads=[t_PS[5], trs], writes=[tout])

    def out_proj(L, b, wo_d, OT, tOT, nchunk, krows, blocks, st):
        wo = [sb(st, "wop%d" % i, [128, nchunk, 128], BF16) for i in range(1)]
        t_wo = [T()]
        cnt = 0
        for c in range(DC):
            wi_ = 0
            cnt += 1
            P.dma("pool", wo[wi_][:krows, :, :],
                  wo_d[:, c * 128:(c + 1) * 128].rearrange("(j p) n -> p j n", p=krows), writes=[t_wo[wi_]])
            for blk in blocks:
                t0, n = BLK[blk]
                v = nb if blk == 0 else b
                for j in range(nchunk):
                    P.op("pe", (lambda e, wi_=wi_, j=j, t0=t0, n=n: e.matmul(
                        PS[7][:, :n], lhsT=wo[wi_][:krows, j, :], rhs=OT[:krows, j, t0:t0 + n],
                        start=(j == 0), stop=(j == nchunk - 1))),
                        reads=[t_wo[wi_], tOT[j][blk]], writes=[t_PS[7]])
                P.op("dve", (lambda e, c=c, v=v, t0=t0, n=n: e.scalar_tensor_tensor(
                    out=XT[:, c, t0:t0 + n], in0=PS[7][:, :n], scalar=ada_ap(L, 2, c, v),
                    in1=XT[:, c, t0:t0 + n], op0=ALU.mult, op1=ALU.add)),
                    reads=[t_PS[7], t_ADA, tXT[c][blk]], writes=[tXT[c][blk]])


    def rope_norm_evac(i, blk, n, gt, t_gt, gcol, rope, t_rope, zs_src, t_zs, sc, dst, dtile, inv_n, ranges):
        sqz, t_sqz, rq, t_rq, t1, tt1, t2, tt2 = sc
        P.op("act", (lambda e: e.activation(out=sqz[:, :n], in_=PS[0][:, :n], func=AF.Square)),
             reads=[t_PS[0]], writes=[t_sqz])
        P.op("pe", (lambda e: e.matmul(PS[2][:, :n], lhsT=ones1[:, :], rhs=sqz[:, :n], start=True, stop=True)),
             reads=[t_ones1, t_sqz], writes=[t_PS[2]])
        P.op("act", (lambda e: e.activation(out=rq[:, :n], in_=PS[2][:, :n], func=AF.Sqrt, bias=EPS, scale=inv_n)),
             reads=[t_PS[2]], writes=[t_rq])
        P.op("dve", (lambda e: e.reciprocal(out=rq[:, :n], in_=rq[:, :n])), reads=[t_rq], writes=[t_rq])
        if blk == 0:
            P.op("dve", (lambda e: e.scalar_tensor_tensor(
                out=dst, in0=PS[0][:, :n], scalar=gt[:, gcol:gcol + 1], in1=rq[:, :n], op0=ALU.mult, op1=ALU.mult)),
                reads=[t_PS[0], t_gt, t_rq], writes=[dtile])
            return
        r0 = (blk - 1) * 8
        rope_hi = 0
        for (p0, p1, kind) in ranges:
            np_ = p1 - p0
            if kind == "none":
                P.op("dve", (lambda e, p0=p0, p1=p1: e.tensor_scalar(
                    out=t1[p0:p1, :n], in0=PS[0][p0:p1, :n], scalar1=gt[p0:p1, gcol:gcol + 1], scalar2=None,
                    op0=ALU.mult)), reads=[t_PS[0], t_gt], writes=[tt1])
                continue
            rope_hi = max(rope_hi, p1)
            if kind == "row":
                ctab = rope[p0:p1, 0, r0:r0 + 8].unsqueeze(2).broadcast_to([np_, 8, 64])
                stab = rope[p0:p1, 1, r0:r0 + 8].unsqueeze(2).broadcast_to([np_, 8, 64])
            else:
                ctab = rope[p0:p1, 2, :].unsqueeze(1).broadcast_to([np_, 8, 64])
                stab = rope[p0:p1, 3, :].unsqueeze(1).broadcast_to([np_, 8, 64])
            z = PS[0][p0:p1, :n].rearrange("p (r c) -> p r c", c=64)
            zs = zs_src(p0, p1).rearrange("p (r c) -> p r c", c=64)
            o1 = t1[p0:p1, :n].rearrange("p (r c) -> p r c", c=64)
            o2 = t2[p0:p1, :n].rearrange("p (r c) -> p r c", c=64)
            P.op("dve", (lambda e, z=z, o1=o1, ctab=ctab, p0=p0, p1=p1: e.scalar_tensor_tensor(
                out=o1, in0=z, scalar=gt[p0:p1, gcol:gcol + 1], in1=ctab, op0=ALU.mult, op1=ALU.mult)),
                reads=[t_PS[0], t_gt, t_rope], writes=[tt1])
            P.op("dve", (lambda e, zs=zs, o2=o2, stab=stab, p0=p0, p1=p1: e.scalar_tensor_tensor(
                out=o2, in0=zs, scalar=gt[p0:p1, gcol + 1:gcol + 2], in1=stab, op0=ALU.mult, op1=ALU.mult)),
                reads=[t_zs, t_gt, t_rope], writes=[tt2])
        P.op("pool", (lambda e: e.tensor_tensor(out=t1[:rope_hi, :n], in0=t1[:rope_hi, :n], in1=t2[:rope_hi, :n],
                                                op=ALU.add)),
             reads=[tt1, tt2], writes=[tt1])
        P.op("dve", (lambda e: e.tensor_tensor(out=dst, in0=t1[:, :n], in1=rq[:, :n], op=ALU.mult)),
             reads=[tt1, t_rq], writes=[dtile])

    def mla(L, b):
        ctx_out = (L != 3) or debug_ctx
        qblocks = [0, 1, 2, 3, 4] if ctx_out else [1, 2, 3, 4]
        allb = [0, 1, 2, 3, 4]
        with contextlib.ExitStack() as st:
            CQ = sb(st, "CQ", [128, 3, NT], BF16)
            CKV = sb(st, "CKV", [128, 2, NT], BF16)
            KR = sb(st, "KR", [128, 2, NT], BF16)
            tCQ = [[T() for _ in range(5)] for _ in range(3)]
            tCKV = [[T() for _ in range(5)] for _ in range(2)]
            tKR = [[T() for _ in range(5)] for _ in range(2)]
            identb = sb(st, "identb", [128, 128], BF16)
            t_identb = T()
            P.op("dve", lambda e: e.tensor_copy(out=identb[:], in_=ident[:]), reads=[t_ident], writes=[t_identb])
            with contextlib.ExitStack() as st2:
                HT = sb(st2, "HT", [128, DC, NT], BF16)
                tHT = [[T() for _ in range(5)] for _ in range(DC)]
                with contextlib.ExitStack() as st3:
                    mixer_norm(norm_scratch(st3), L, b, HT, tHT)
                    P.barrier()
                win = sb(st2, "win", [128, DC, 896], BF16)
                gl = sb(st2, "gl", [128, 5], F32)
                sqa = [sb(st2, "sqa%d" % i, [128, 512], BF16) for i in range(2)]
                rqa = sb(st2, "rqa", [128, 512], F32)
                t_win, t_gl, t_sqa, t_rqa = T(), T(), [T(), T()], T()
                P.dma("pool", win[:], mixw["mla_win"].rearrange("(k p) n -> p k n", p=128), writes=[t_win])
                P.dma("sp", gl[:], mixw["mla_gl"][:, :], writes=[t_gl])
                for blk in allb:
                    t0, n = BLK[blk]
                    for (c0, ncs, dstb, tdst, inv_n, g0) in ((0, 3, CQ, tCQ, 1.0 / 384, 0), (3, 2, CKV, tCKV, 1.0 / 256, 3)):
                        for ci in range(ncs):
                            for k in range(DC):
                                P.op("pe", (lambda e, ci=ci, k=k, c0=c0: e.matmul(
                                    PS[ci][:, :n], lhsT=win[:, k, (c0 + ci) * 128:(c0 + ci + 1) * 128],
                                    rhs=HT[:, k, t0:t0 + n], start=(k == 0), stop=(k == DC - 1))),
                                    reads=[t_win, tHT[k][blk]], writes=[t_PS[ci]])
                        for ci in range(ncs):
                            i = ci % 2
                            P.op("act", (lambda e, ci=ci, i=i: e.activation(out=sqa[i][:, :n], in_=PS[ci][:, :n],
                                                                            func=AF.Square)),
                                 reads=[t_PS[ci]], writes=[t_sqa[i]])
                            P.op("pe", (lambda e, ci=ci, i=i, ncs=ncs: e.matmul(
                                PS[3][:, :n], lhsT=ones1[:, :], rhs=sqa[i][:, :n], start=(ci == 0), stop=(ci == ncs - 1))),
                                reads=[t_ones1, t_sqa[i]], writes=[t_PS[3]])
                        P.op("act", (lambda e, inv_n=inv_n: e.activation(out=rqa[:, :n], in_=PS[3][:, :n], func=AF.Sqrt,
                                                                         bias=EPS, scale=inv_n)),
                             reads=[t_PS[3]], writes=[t_rqa])
                        P.op("dve", (lambda e: e.reciprocal(out=rqa[:, :n], in_=rqa[:, :n])), reads=[t_rqa], writes=[t_rqa])
                        for ci in range(ncs):
                            P.op("dve", (lambda e, ci=ci, dstb=dstb, g0=g0: e.scalar_tensor_tensor(
                                out=dstb[:, ci, t0:t0 + n], in0=PS[ci][:, :n], scalar=gl[:, g0 + ci:g0 + ci + 1],
                                in1=rqa[:, :n], op0=ALU.mult, op1=ALU.mult)),
                                reads=[t_PS[ci], t_gl, t_rqa], writes=[tdst[ci][blk]])
                    for ci in range(2):
                        for k in range(DC):
                            P.op("pe", (lambda e, ci=ci, k=k: e.matmul(
                                PS[4 + ci][:, :n], lhsT=win[:, k, (5 + ci) * 128:(6 + ci) * 128],
                                rhs=HT[:, k, t0:t0 + n], start=(k == 0), stop=(k == DC - 1))),
                                reads=[t_win, tHT[k][blk]], writes=[t_PS[4 + ci]])
                        P.op("act", (lambda e, ci=ci: e.copy(out=KR[:, ci, t0:t0 + n], in_=PS[4 + ci][:, :n])),
                             reads=[t_PS[4 + ci]], writes=[tKR[ci][blk]])
                P.barrier()
            OT = sb(st, "OT", [128, 8, NT], BF16)
            tOT = [[T() for _ in range(5)] for _ in range(8)]
            Vg = sb(st, "Vg", [128, NT // 128, 256], BF16)
            tVg = [T() for _ in range(NT // 128)]
            QT = [sb(st, "QT%d" % i, [128, NT], BF16) for i in range(2)]
            KTh = [sb(st, "KTh%d" % i, [128, NT], BF16) for i in range(2)]
            tQT = [[T() for _ in range(5)] for _ in range(2)]
            tKT = [[T() for _ in range(5)] for _ in range(2)]
            wuq = [sb(st, "wuq%d" % i, [128, 3, 256], BF16) for i in range(1)] * 2
            wuk = [sb(st, "wuk%d" % i, [128, 2, 128], BF16) for i in range(1)] * 2
            wvg = sb(st, "wvg", [128, 2, 256], BF16)
            t_wuq, t_wuk, t_wvg = [T()] * 2, [T()] * 2, T()
            gq = sb(st, "gq", [128, 4], F32)
            rope = sb(st, "rope", [128, 4, 64], F32)
            t_gq, t_rope = T(), T()
            sqz = sb(st, "sqz", [128, 512], BF16)
            rq = sb(st, "rq", [128, 512], F32)
            t1 = sb(st, "t1", [128, 512], F32)
            t2 = sb(st, "t2", [128, 512], F32)
            sc = (sqz, T(), rq, T(), t1, T(), t2, T())
            PT = [sb(st, "PT%d" % i, [128, 512], BF16) for i in range(2)]
            tPT = [T() for _ in range(2)]
            rs = sb(st, "rs", [128, 512], F32)
            trs = T()
            P.dma("sp", gq[:], mixw["mla_g"][:, :], writes=[t_gq])
            P.dma("sp", rope[:], mixw["mla_rope"][:, :, :], writes=[t_rope])
            P.op("pool", lambda e: e.memset(t1[:], 0.0), writes=[sc[5]])
            P.op("pool", lambda e: e.memset(t2[:], 0.0), writes=[sc[7]])
            ranges = [(0, 32, "row"), (32, 64, "col"), (64, 128, "none")]
            scale = 96.0 ** -0.5
            for h in range(16):
                hi = h % 2
                if h % 4 == 0:
                    g = h // 4
                    P.dma("pool", wvg[:], mixw["mla_wv"][:, g * 256:(g + 1) * 256].rearrange("(k p) n -> p k n", p=128),
                          writes=[t_wvg])
                    for tt in range(NT // 128):
                        pb = 6 + (tt % 2)
                        blk = blk_of_tile(tt)
                        for k in range(2):
                            P.op("pe", (lambda e, k=k, tt=tt, pb=pb: e.matmul(
                                PS[pb][:, :256], lhsT=CKV[:, k, tt * 128:(tt + 1) * 128], rhs=wvg[:, k, :],
                                start=(k == 0), stop=(k == 1))),
                                reads=[tCKV[k][blk], t_wvg], writes=[t_PS[pb]])
                        P.op("act", (lambda e, tt=tt, pb=pb: e.copy(out=Vg[:, tt, :], in_=PS[pb][:, :256])),
                             reads=[t_PS[pb]], writes=[tVg[tt]])
                P.dma("pool", wuq[hi][:], mixw["mla_wuq"][h].rearrange("(k p) n -> p k n", p=128), writes=[t_wuq[hi]])
                P.dma("pool", wuk[hi][:], mixw["mla_wuk"][h].rearrange("(k p) n -> p k n", p=128), writes=[t_wuk[hi]])
                for blk in allb:
                    t0, n = BLK[blk]
                    for k in range(2):
                        P.op("pe", (lambda e, k=k: e.matmul(PS[0][:, :n], lhsT=wuk[hi][:, k, :], rhs=CKV[:, k, t0:t0 + n],
                                                            start=(k == 0), stop=False)),
                             reads=[t_wuk[hi], tCKV[k][blk]], writes=[t_PS[0]])
                    P.op("pe", (lambda e: e.matmul(PS[0][:, :n], lhsT=identb[:, :], rhs=KR[:, 0, t0:t0 + n],
                                                   start=False, stop=True)),
                         reads=[t_identb, tKR[0][blk]], writes=[t_PS[0]])
                    rope_norm_evac(0, blk, n, gq, t_gq, 2, rope, t_rope,
                                   (lambda p0, p1, t0=t0, n=n: KR[p0:p1, 1, t0:t0 + n]), tKR[1][blk], sc,
                                   KTh[hi][:, t0:t0 + n], tKT[hi][blk], 1.0 / 96, ranges)
                for blk in qblocks:
                    t0, n = BLK[blk]
                    for half in range(2 if blk > 0 else 1):
                        for k in range(3):
                            P.op("pe", (lambda e, k=k, half=half: e.matmul(
                                PS[half][:, :n], lhsT=wuq[hi][:, k, half * 128:(half + 1) * 128],
                                rhs=CQ[:, k, t0:t0 + n], start=(k == 0), stop=(k == 2))),
                                reads=[t_wuq[hi], tCQ[k][blk]], writes=[t_PS[half]])
                    rope_norm_evac(0, blk, n, gq, t_gq, 0, rope, t_rope,
                                   (lambda p0, p1, n=n: PS[1][p0:p1, :n]), t_PS[1], sc,
                                   QT[hi][:, t0:t0 + n], tQT[hi][blk], 1.0 / 96, ranges)
                for blk in qblocks:
                    t0, n = BLK[blk]
                    key_tiles = [0, 1] if blk == 0 else list(range(NT // 128))
                    po = (h % 2) * 64
                    vc = (h % 4) * 64
                    attention(QT[hi][:, t0:t0 + n], tQT[hi][blk],
                              (lambda kt: KTh[hi][:, kt * 128:(kt + 1) * 128]),
                              (lambda kt: tKT[hi][blk_of_tile(kt)]),
                              (lambda kt, vc=vc: Vg[:, kt, vc:vc + 64]), (lambda kt: tVg[kt]),
                              key_tiles, n, 128, 64, scale, OT[po:po + 64, h // 2, t0:t0 + n], tOT[h // 2][blk],
                              PT, tPT, rs, trs, ones1[:, 0:64], po=po)
            out_proj(L, b, mixw["mla_wo"], OT, tOT, 8, 128, qblocks, st)
            P.barrier()


    def na(L, b):
        ctx_out = (L != 3) or debug_ctx
        with contextlib.ExitStack() as st:
            HT = sb(st, "HT", [128, DC, NT], BF16)
            tHT = [[T() for _ in range(5)] for _ in range(DC)]
            with contextlib.ExitStack() as st3:
                mixer_norm(norm_scratch(st3), L, b, HT, tHT)
                P.barrier()
            OT = sb(st, "OT", [128, 8, NT], BF16)
            tOT = [[T() for _ in range(5)] for _ in range(8)]
            QTz = [sb(st, "QTz%d" % i, [128, NT], BF16) for i in range(2)]
            KT = sb(st, "KTc", [128, NT], BF16)
            Vc = sb(st, "Vc", [128, NT // 128, 128], BF16)
            tQT = [[T() for _ in range(5)] for _ in range(2)]
            tKT = [T() for _ in range(5)]
            tVc = [T() for _ in range(NT // 128)]
            wq = [sb(st, "wqc%d" % i, [128, DC, 128], BF16) for i in range(3)]
            t_wq = [T() for _ in range(3)]
            tblA = sb(st, "tblA", [128, 14, 64], F32)
            tblB = sb(st, "tblB", [128, 5, 64], F32)
            t_tblA, t_tblB, t_mask = [T(), T()], [T(), T()], T()
            gq = sb(st, "gq", [128, 2], F32)
            gqs = sb(st, "gqs", [128, 1], F32)
            t_gq, t_gqs = T(), T()
            onesB = sb(st, "onesB", [128, 128], BF16)
            t_onesB = T()
            sqz = sb(st, "sqz", [128, 512], BF16)
            rq = sb(st, "rq", [128, 512], F32)
            t_sqz, t_rq = T(), T()
            xs = [sb(st, "xs%d" % i, [128, 320], F32) for i in range(2)]
            t_xs = [T(), T()]
            PT = [sb(st, "PT%d" % i, [128, 512], BF16) for i in range(2)]
            tPT = [T() for _ in range(2)]
            rs = sb(st, "rs", [128, 512], F32)
            trs = T()
            tz = T()
            P.op("pool", lambda e: e.memset(onesB[:], 0.0), writes=[t_onesB])
            P.op("pool", lambda e: e.memset(onesB[0:64, 0:64], 1.0), writes=[t_onesB])
            P.op("pool", lambda e: e.memset(onesB[64:128, 64:128], 1.0), writes=[t_onesB])
            P.op("pool", lambda e: e.memset(QTz[0][64:128, :], 0.0), writes=[tz])
            P.op("pool", lambda e: e.memset(QTz[1][0:64, :], 0.0), writes=[tz])
            P.op("pool", lambda e: e.memset(tblB[0:64, 0, :], -1e4), writes=[t_mask])
            P.op("pool", lambda e: e.memset(tblB[64:128, 4, :], -1e4), writes=[t_mask])
            P.dma("sp", gq[:], mixw["na_g"][:, :], writes=[t_gq])
            P.op("dve", lambda e: e.tensor_scalar(out=gqs[:], in0=gq[:, 0:1], scalar1=0.125, scalar2=None, op0=ALU.mult),
                 reads=[t_gq], writes=[t_gqs])
            wcnt = [0]

            def proj_norm(col0, gap, t_g, is_q):
                wi_ = wcnt[0] % 3
                wcnt[0] += 1
                P.dma("pool", wq[wi_][:], mixw["na_wqkv"][:, col0:col0 + 128].rearrange("(k p) n -> p k n", p=128),
                      writes=[t_wq[wi_]])
                for blk in range(5):
                    t0, n = BLK[blk]
                    for k in range(DC):
                        P.op("pe", (lambda e, k=k: e.matmul(PS[0][:, :n], lhsT=wq[wi_][:, k, :], rhs=HT[:, k, t0:t0 + n],
                                                            start=(k == 0), stop=(k == DC - 1))),
                             reads=[t_wq[wi_], tHT[k][blk]], writes=[t_PS[0]])
                    P.op("act", (lambda e: e.activation(out=sqz[:, :n], in_=PS[0][:, :n], func=AF.Square)),
                         reads=[t_PS[0]], writes=[t_sqz])
                    P.op("pe", (lambda e: e.matmul(PS[2][:, :n], lhsT=onesB[:, :], rhs=sqz[:, :n], start=True, stop=True)),
                         reads=[t_onesB, t_sqz], writes=[t_PS[2]])
                    P.op("act", (lambda e: e.activation(out=rq[:, :n], in_=PS[2][:, :n], func=AF.Sqrt, bias=EPS,
                                                        scale=1.0 / 64)),
                         reads=[t_PS[2]], writes=[t_rq])
                    P.op("dve", (lambda e: e.reciprocal(out=rq[:, :n], in_=rq[:, :n])), reads=[t_rq], writes=[t_rq])
                    if is_q:
                        for hh in range(2):
                            p0 = hh * 64
                            P.op("dve", (lambda e, hh=hh, p0=p0: e.scalar_tensor_tensor(
                                out=QTz[hh][p0:p0 + 64, t0:t0 + n], in0=PS[0][p0:p0 + 64, :n], scalar=gap[p0:p0 + 64, :],
                                in1=rq[p0:p0 + 64, :n], op0=ALU.mult, op1=ALU.mult)),
                                reads=[t_PS[0], t_g, t_rq, tz], writes=[tQT[hh][blk]])
                    else:
                        P.op("dve", (lambda e: e.scalar_tensor_tensor(
                            out=KT[:, t0:t0 + n], in0=PS[0][:, :n], scalar=gap, in1=rq[:, :n], op0=ALU.mult, op1=ALU.mult)),
                            reads=[t_PS[0], t_g, t_rq], writes=[tKT[blk]])

            scount = [0]
            for j in range(8):
                proj_norm(j * 128, gqs[:, 0:1], t_gqs, True)
                proj_norm(D + j * 128, gq[:, 1:2], t_gq, False)
                wi_ = wcnt[0] % 3
                wcnt[0] += 1
                P.dma("pool", wq[wi_][:], mixw["na_wqkv"][:, 2 * D + j * 128:2 * D + (j + 1) * 128].rearrange(
                    "(k p) n -> p k n", p=128), writes=[t_wq[wi_]])
                for tt in range(NT // 128):
                    blk = blk_of_tile(tt)
                    for k in range(DC):
                        P.op("pe", (lambda e, k=k, tt=tt: e.matmul(
                            PS[1][:, :128], lhsT=HT[:, k, tt * 128:(tt + 1) * 128], rhs=wq[wi_][:, k, :],
                            start=(k == 0), stop=(k == DC - 1))),
                            reads=[tHT[k][blk], t_wq[wi_]], writes=[t_PS[1]])
                    P.op("act", (lambda e, tt=tt: e.copy(out=Vc[:, tt, :], in_=PS[1][:, :128])),
                         reads=[t_PS[1]], writes=[tVc[tt]])
                for hh in range(2):
                    h = 2 * j + hh
                    po = hh * 64
                    Mh = mixw["na_M"][h]
                    P.dma("sp", tblA[0:64, :, :], Mh[:, 0:14, :], writes=[t_tblA[0]])
                    P.dma("sp", tblA[64:128, :, :], Mh[:, 1:15, :], writes=[t_tblA[1]])
                    P.dma("sp", tblB[0:64, 1:5, :], Mh[:, 4:11:2, :], writes=[t_tblB[0]])
                    P.dma("sp", tblB[64:128, 0:4, :], Mh[:, 3:10:2, :], writes=[t_tblB[1]])
                    if ctx_out:
                        attention(QTz[hh][:, 0:256], tQT[hh][0],
                                  (lambda kt: KT[:, kt * 128:(kt + 1) * 128]), (lambda kt: tKT[0]),
                                  (lambda kt: Vc[:, kt, po:po + 64]), (lambda kt: tVc[kt]),
                                  [0, 1], 256, 128, 64, 1.0, OT[po:po + 64, j, 0:256], tOT[j][0], PT, tPT, rs, trs,
                                  ones1[:, 0:64], po=po)
                    for blk in range(1, 5):
                        t0 = BLK[blk][0]
                        for ql in range(8):
                            qr = (blk - 1) * 8 + ql
                            r0 = min(max(qr - 4, 0), 24)
                            d0 = r0 - qr + 7
                            odd = r0 % 2
                            nsl = 5 if odd else 4
                            tlo = 2 + (r0 - odd) // 2
                            sbk = 3 + (scount[0] % 2)
                            xi = scount[0] % 2
                            pi = scount[0] % len(PT)
                            scount[0] += 1
                            qs = QTz[hh][:, 256 + qr * 64:256 + (qr + 1) * 64]
                            for s_ in range(nsl):
                                tt = tlo + s_
                                P.op("pe", (lambda e, s_=s_, tt=tt: e.matmul(
                                    PS[sbk][:, s_ * 64:(s_ + 1) * 64], lhsT=KT[:, tt * 128:(tt + 1) * 128], rhs=qs,
                                    start=True, stop=True)),
                                    reads=[tKT[blk_of_tile(tt)], tQT[hh][blk]], writes=[t_PS[sbk]])
                            for ct in range(2):
                                P.op("pe", (lambda e, ct=ct: e.matmul(
                                    PS[sbk][:, 320 + ct * 64:320 + (ct + 1) * 64],
                                    lhsT=KT[:, ct * 128:(ct + 1) * 128], rhs=qs, start=True, stop=True)),
                                    reads=[tKT[0], tQT[hh][blk]], writes=[t_PS[sbk]])
                            if odd:
                                bias_ap, tb_ = tblB[:, :, :], t_tblB
                            else:
                                bias_ap, tb_ = tblA[:, d0:d0 + 7:2, :], t_tblA
                            nb_ = nsl * 64
                            P.op("dve", (lambda e, bias_ap=bias_ap, nb_=nb_: e.tensor_tensor(
                                out=xs[xi][:, :nb_].rearrange("p (s c) -> p s c", c=64),
                                in0=PS[sbk][:, 0:nb_].rearrange("p (s c) -> p s c", c=64), in1=bias_ap, op=ALU.add)),
                                reads=[t_PS[sbk], tb_[0], tb_[1], t_mask], writes=[t_xs[xi]])
                            P.op("act", (lambda e, nb_=nb_: e.activation(out=PT[pi][:, 0:nb_], in_=xs[xi][:, :nb_], func=AF.Exp)),
                                 reads=[t_xs[xi]], writes=[tPT[pi]])
                            P.op("act", (lambda e: e.activation(out=PT[pi][:, 320:448], in_=PS[sbk][:, 320:448], func=AF.Exp)),
                                 reads=[t_PS[sbk]], writes=[tPT[pi]])
                            mm = [(Vc[:, tlo + s_, po:po + 64], PT[pi][:, s_ * 64:(s_ + 1) * 64], tVc[tlo + s_])
                                  for s_ in range(nsl)]
                            mm += [(Vc[:, ct, po:po + 64], PT[pi][:, 320 + ct * 64:320 + (ct + 1) * 64], tVc[ct])
                                   for ct in range(2)]
                            ocol = slice(ql * 64, (ql + 1) * 64)
                            for i_, (lv, rp, tv_) in enumerate(mm):
                                P.op("pe", (lambda e, lv=lv, rp=rp, i_=i_: e.matmul(
                                    PS[5][po:po + 64, ocol], lhsT=lv, rhs=rp, start=(i_ == 0), stop=(i_ == len(mm) - 1))),
                                    reads=[tv_, tPT[pi]], writes=[t_PS[5]])
                            for i_, (lv, rp, tv_) in enumerate(mm):
                                P.op("pe", (lambda e, rp=rp, i_=i_: e.matmul(
                                    PS[6][po:po + 64, ocol], lhsT=ones1[:, 0:64], rhs=rp,
                                    start=(i_ == 0), stop=(i_ == len(mm) - 1))),
                                    reads=[t_ones1, tPT[pi]], writes=[t_PS[6]])
                        P.op("dve", (lambda e: e.reciprocal(out=rs[po:po + 64, :], in_=PS[6][po:po + 64, :])),
                             reads=[t_PS[6]], writes=[trs])
                        P.op("dve", (lambda e, t0=t0: e.tensor_tensor(out=OT[po:po + 64, j, t0:t0 + 512],
                                                                      in0=PS[5][po:po + 64, :], in1=rs[po:po + 64, :],
                                                                      op=ALU.mult)),
                             reads=[t_PS[5], trs], writes=[tOT[j][blk]])
            out_proj(L, b, mixw["na_wo"], OT, tOT, 8, 128, [0, 1, 2, 3, 4] if ctx_out else [1, 2, 3, 4], st)
            P.barrier()


    s5c = {}

    def s5_consts():
        APW = sb(gs, "s5_APW", [128, 12, 3, 64], F32)
        FF = sb(gs, "s5_FF", [128, 3, 64], F32)
        s5c["APW"], s5c["FF"], s5c["t"] = APW, FF, T()
        with contextlib.ExitStack() as st:
            a = sb(st, "s5a", [128, 2, 64], F32)
            ldt = sb(st, "s5ldt", [128, 64], F32)
            tmp = [sb(st, "s5t%d" % i, [128, 64], F32) for i in range(10)]
            hp = sb(st, "s5hp", [128, 1], F32)
            tt = T()
            P.dma("sp", a[:], mixw["s5_as"][:, :, :], writes=[tt])
            P.dma("sp", ldt[:], mixw["s5_ldt"][:, :], writes=[tt])
            P.op("pool", lambda e: e.memset(hp[:], float(np.pi / 2)), reads=[tt], writes=[tt])
            dt, x, th, mag, cs, sn, wr, wi, ta, tb = tmp

            def dve(fn):
                P.op("dve", fn, reads=[tt], writes=[tt])

            def act(fn):
                P.op("act", fn, reads=[tt], writes=[tt])
            act(lambda e: e.activation(out=dt[:], in_=ldt[:], func=AF.Exp))
            dve(lambda e: e.tensor_tensor(out=x[:], in0=dt[:], in1=a[:, 0, :], op=ALU.mult))
            dve(lambda e: e.tensor_tensor(out=th[:], in0=dt[:], in1=a[:, 1, :], op=ALU.mult))
            act(lambda e: e.activation(out=mag[:], in_=x[:], func=AF.Exp, scale=1.0 / 16))
            act(lambda e: e.activation(out=sn[:], in_=th[:], func=AF.Sin, scale=1.0 / 16))
            act(lambda e: e.activation(out=cs[:], in_=th[:], func=AF.Sin, scale=1.0 / 16, bias=hp[:, 0:1]))
            dve(lambda e: e.tensor_tensor(out=wr[:], in0=mag[:], in1=cs[:], op=ALU.mult))
            dve(lambda e: e.tensor_tensor(out=wi[:], in0=mag[:], in1=sn[:], op=ALU.mult))
            cur = (wr, wi)
            nxt_ = (x, th)
            for it in range(15):
                cr, ci = cur
                if it >= 3:
                    k = it - 3
                    nr_, ni_ = APW[:, k, 0, :], APW[:, k, 1, :]
                else:
                    nr_, ni_ = nxt_[0][:], nxt_[1][:]
                dve(lambda e, cr=cr: e.tensor_tensor(out=ta[:], in0=cr[:] if not isinstance(cr, bass.AP) else cr,
                                                     in1=cr[:] if not isinstance(cr, bass.AP) else cr, op=ALU.mult))
                dve(lambda e, ci=ci: e.tensor_tensor(out=tb[:], in0=ci[:] if not isinstance(ci, bass.AP) else ci,
                                                     in1=ci[:] if not isinstance(ci, bass.AP) else ci, op=ALU.mult))
                dve(lambda e, cr=cr, ci=ci, ni_=ni_: e.scalar_tensor_tensor(
                    out=ni_, in0=cr[:] if not isinstance(cr, bass.AP) else cr, scalar=2.0,
                    in1=ci[:] if not isinstance(ci, bass.AP) else ci, op0=ALU.mult, op1=ALU.mult))
                dve(lambda e, nr_=nr_: e.tensor_tensor(out=nr_, in0=ta[:], in1=tb[:], op=ALU.subtract))
                if it >= 3:
                    dve(lambda e, k=k: e.tensor_scalar(out=APW[:, k, 2, :], in0=APW[:, k, 1, :], scalar1=-1.0,
                                                       scalar2=None, op0=ALU.mult))
                    cur = (APW[:, k, 0, :], APW[:, k, 1, :])
                else:
                    cur = nxt_
                    nxt_ = (wr, wi) if nxt_[0] is x else (x, th)
            abr, abi = APW[:, 0, 0, :], APW[:, 0, 1, :]
            are, aim = a[:, 0, :], a[:, 1, :]
            den, nr1, m1, m2, m3, m4 = dt, mag, cs, sn, ta, tb
            dve(lambda e: e.tensor_tensor(out=den[:], in0=are, in1=are, op=ALU.mult))
            dve(lambda e: e.tensor_tensor(out=m1[:], in0=aim, in1=aim, op=ALU.mult))
            dve(lambda e: e.tensor_tensor(out=den[:], in0=den[:], in1=m1[:], op=ALU.add))
            dve(lambda e: e.reciprocal(out=den[:], in_=den[:]))
            dve(lambda e: e.tensor_scalar(out=nr1[:], in0=abr, scalar1=-1.0, scalar2=None, op0=ALU.add))
            dve(lambda e: e.tensor_tensor(out=m1[:], in0=nr1[:], in1=are, op=ALU.mult))
            dve(lambda e: e.tensor_tensor(out=m2[:], in0=abi, in1=aim, op=ALU.mult))
            dve(lambda e: e.tensor_tensor(out=m1[:], in0=m1[:], in1=m2[:], op=ALU.add))
            dve(lambda e: e.tensor_tensor(out=FF[:, 0, :], in0=m1[:], in1=den[:], op=ALU.mult))
            dve(lambda e: e.tensor_tensor(out=m3[:], in0=abi, in1=are, op=ALU.mult))
            dve(lambda e: e.tensor_tensor(out=m4[:], in0=nr1[:], in1=aim, op=ALU.mult))
            dve(lambda e: e.tensor_tensor(out=m3[:], in0=m3[:], in1=m4[:], op=ALU.subtract))
            dve(lambda e: e.tensor_tensor(out=FF[:, 1, :], in0=m3[:], in1=den[:], op=ALU.mult))
            dve(lambda e: e.tensor_scalar(out=FF[:, 2, :], in0=FF[:, 1, :], scalar1=-1.0, scalar2=None, op0=ALU.mult))
            P.op("dve", lambda e: e.tensor_copy(out=tmp[0][:, 0:1], in_=FF[:, 0, 0:1]), reads=[tt], writes=[s5c["t"]])
            P.barrier()

    def s5(L, b):
        ctx_out = (L != 3) or debug_ctx
        APW, FF, t_c = s5c["APW"], s5c["FF"], s5c["t"]
        with contextlib.ExitStack() as st:
            HT = sb(st, "HT", [128, DC, NT], BF16)
            tHT = [[T() for _ in range(5)] for _ in range(DC)]
            with contextlib.ExitStack() as st3:
                mixer_norm(norm_scratch(st3), L, b, HT, tHT)
                P.barrier()
            X = [[sb(st, "scan%d%d" % (i, r), [128, NT], F32) for r in range(2)] for i in range(2)]
            tX = [[T(), T()], [T(), T()]]
            tX2 = [[T(), T()], [T(), T()]]
            BP = sb(st, "BP", [128, 4, 2, 2, 128], F32)
            BBP = sb(st, "BBP", [128, 4, 2, 2, 128], BF16)
            CP = sb(st, "CP", [128, 4, 2, 2, 128], F32)
            dsk = sb(st, "dsk", [128, DC], F32)
            ytmp = [sb(st, "ytmp%d" % i, [128, 512], F32) for i in range(2)]
            t_BP, t_BBP, t_CP, t_dsk, t_ytmp = T(), T(), T(), T(), [T(), T()]
            P.op("pool", lambda e: e.memset(BP[:], 0.0), writes=[t_BP])
            P.op("pool", lambda e: e.memset(CP[:], 0.0), writes=[t_CP])
            P.dma("sp", dsk[:], mixw["s5_d"][:, :], writes=[t_dsk])

            def pos(blk, d):
                t0, n = BLK[blk]
                if d == 0:
                    return t0
                return 2048 if blk == 0 else t0 - 256

            import os
            stop = os.environ.get("S5_STOP", "")
            for c in range(DC):
                if stop == "consts":
                    break
                for gl in range(8):
                    g = 8 * c + gl
                    sl, hf = gl // 2, gl % 2
                    P.dma("sp", BP[gl * 16:(gl + 1) * 16, sl, :, :, hf * 64:(hf + 1) * 64].rearrange("c d r p -> c (d r) p"),
                          mixw["s5_b"][g], reads=[t_BP], writes=[t_BP])
                    P.dma("sp", CP[hf * 64:(hf + 1) * 64, sl, :, :, gl * 16:(gl + 1) * 16].rearrange("p d r c -> p (d r) c"),
                          mixw["s5_c"][g], reads=[t_CP], writes=[t_CP])
                P.op("act", lambda e: e.copy(out=BBP[:], in_=BP[:]), reads=[t_BP], writes=[t_BBP])
                P.op("pool", lambda e: e.tensor_scalar(out=CP[:, :, :, 1, :], in0=CP[:, :, :, 1, :], scalar1=-1.0,
                                                       scalar2=None, op0=ALU.mult), reads=[t_CP], writes=[t_CP])
                if stop == "place":
                    continue
                first = True
                nmm = 4 * 2 * 2
                imm = 0
                for sl in range(4):
                    stg = 4 * c + sl
                    for d in range(2):
                        col = d * 32 + stg
                        for blk in range(5):
                            t0, n = BLK[blk]
                            p0 = pos(blk, d)
                            for ri in range(2):
                                P.op("pe", (lambda e, ri=ri: e.matmul(PS[ri][:, :n], lhsT=BBP[:, sl, d, ri, :],
                                                                      rhs=HT[:, c, t0:t0 + n], start=True, stop=True)),
                                     reads=[t_BBP, tHT[c][blk]], writes=[t_PS[ri]])
                            P.op("act", (lambda e: e.activation(out=X[0][1][:, p0:p0 + n], in_=PS[1][:, :n], func=AF.Identity,
                                                                scale=FF[:, 0, col:col + 1])),
                                 reads=[t_PS[1], t_c], writes=[tX[0][1], tX2[0][1]])
                            P.op("act", (lambda e: e.activation(out=X[0][0][:, p0:p0 + n], in_=PS[0][:, :n], func=AF.Identity,
                                                                scale=FF[:, 0, col:col + 1])),
                                 reads=[t_PS[0], t_c], writes=[tX[0][0], tX2[0][0]])
                            P.op("dve", (lambda e: e.scalar_tensor_tensor(
                                out=X[0][0][:, p0:p0 + n], in0=PS[1][:, :n], scalar=FF[:, 2, col:col + 1],
                                in1=X[0][0][:, p0:p0 + n], op0=ALU.mult, op1=ALU.add)),
                                reads=[t_PS[1], t_c, tX[0][0]], writes=[tX[0][0]])
                            P.op("dve", (lambda e: e.scalar_tensor_tensor(
                                out=X[0][1][:, p0:p0 + n], in0=PS[0][:, :n], scalar=FF[:, 1, col:col + 1],
                                in1=X[0][1][:, p0:p0 + n], op0=ALU.mult, op1=ALU.add)),
                                reads=[t_PS[0], t_c, tX[0][1]], writes=[tX[0][1]])
                        for k in range(0 if stop == "drive" else 12):
                            dd = 1 << k
                            src, dst = X[k % 2], X[(k + 1) % 2]
                            tsrc, tdst = tX[k % 2], tX[(k + 1) % 2]
                            ar, ai, nai = APW[:, k, 0, col:col + 1], APW[:, k, 1, col:col + 1], APW[:, k, 2, col:col + 1]
                            m = NT - dd
                            if d == 0:
                                lo, hi, keep = slice(0, m), slice(dd, NT), slice(0, dd)
                            else:
                                lo, hi, keep = slice(dd, NT), slice(0, m), slice(m, NT)
                            tsrc2, tdst2 = tX2[k % 2], tX2[(k + 1) % 2]
                            rd = [tsrc[0], tsrc[1], tsrc2[0], tsrc2[1], t_c]
                            P.op("act", (lambda e: e.copy(out=dst[0][:, keep], in_=src[0][:, keep])),
                                 reads=rd, writes=[tdst2[0]])
                            P.op("pool", (lambda e: e.tensor_copy(out=dst[1][:, keep], in_=src[1][:, keep])),
                                 reads=rd, writes=[tdst2[1]])
                            P.op("dve", (lambda e: e.scalar_tensor_tensor(
                                out=dst[0][:, hi], in0=src[0][:, lo], scalar=ar, in1=src[0][:, hi], op0=ALU.mult, op1=ALU.add)),
                                reads=rd, writes=[tdst[0]])
                            P.op("dve", (lambda e: e.scalar_tensor_tensor(
                                out=dst[0][:, hi], in0=src[1][:, lo], scalar=nai, in1=dst[0][:, hi], op0=ALU.mult, op1=ALU.add)),
                                reads=rd + [tdst[0]], writes=[tdst[0]])
                            P.op("dve", (lambda e: e.scalar_tensor_tensor(
                                out=dst[1][:, hi], in0=src[1][:, lo], scalar=ar, in1=src[1][:, hi], op0=ALU.mult, op1=ALU.add)),
                                reads=rd, writes=[tdst[1]])
                            P.op("dve", (lambda e: e.scalar_tensor_tensor(
                                out=dst[1][:, hi], in0=src[0][:, lo], scalar=ai, in1=dst[1][:, hi], op0=ALU.mult, op1=ALU.add)),
                                reads=rd + [tdst[1]], writes=[tdst[1]])
                        for ri in range(2):
                            for blk in range(5):
                                t0, n = BLK[blk]
                                p0 = pos(blk, d)
                                P.op("pe", (lambda e, ri=ri, blk=blk: e.matmul(
                                    PS[2 + blk][:, :n], lhsT=CP[:, sl, d, ri, :], rhs=X[0][ri][:, p0:p0 + n],
                                    start=(imm == 0), stop=(imm == nmm - 1))),
                                    reads=[t_CP, tX[0][ri], tX2[0][ri]], writes=[t_PS[2 + blk]])
                            imm += 1
                for blk in range(5):
                    t0, n = BLK[blk]
                    yi = blk % 2
                    P.op("dve", (lambda e, blk=blk: e.scalar_tensor_tensor(
                        out=ytmp[yi][:, :n], in0=HT[:, c, t0:t0 + n], scalar=dsk[:, c:c + 1], in1=PS[2 + blk][:, :n],
                        op0=ALU.mult, op1=ALU.add)),
                        reads=[tHT[c][blk], t_dsk, t_PS[2 + blk]], writes=[t_ytmp[yi]])
                    P.op("act", (lambda e: e.activation(out=HT[:, c, t0:t0 + n], in_=ytmp[yi][:, :n],
                                                        func=AF.Gelu_apprx_tanh)),
                         reads=[t_ytmp[yi]], writes=[tHT[c][blk]])
            P.barrier()
            wg = [sb(st, "wg%d" % i, [128, DC, 256], BF16) for i in range(2)]
            t_wg = [[T(), T()], [T(), T()]]
            sg = [sb(st, "sg%d" % i, [128, 512], F32) for i in range(2)]
            t_sg = [T(), T()]
            cnt = 0
            blocks = [0, 1, 2, 3, 4] if ctx_out else [1, 2, 3, 4]
            for c2 in range(DC):
                wi_ = c2 % 2
                for half in range(2):
                    P.dma("pool", wg[wi_][:, :, half * 128:(half + 1) * 128],
                          mixw["s5_wglu"][:, half * D + c2 * 128:half * D + (c2 + 1) * 128].rearrange("(k p) n -> p k n", p=128),
                          writes=[t_wg[wi_][half]])
                for blk in blocks:
                    t0, n = BLK[blk]
                    v = nb if blk == 0 else b
                    si = cnt % 2
                    cnt += 1
                    for half in range(2):
                        for k in range(DC):
                            P.op("pe", (lambda e, half=half, k=k: e.matmul(
                                PS[half][:, :n], lhsT=wg[wi_][:, k, half * 128:(half + 1) * 128], rhs=HT[:, k, t0:t0 + n],
                                start=(k == 0), stop=(k == DC - 1))),
                                reads=[t_wg[wi_][half], tHT[k][blk]], writes=[t_PS[half]])
                    P.op("act", (lambda e: e.activation(out=sg[si][:, :n], in_=PS[1][:, :n], func=AF.Sigmoid)),
                         reads=[t_PS[1]], writes=[t_sg[si]])
                    P.op("dve", (lambda e: e.tensor_tensor(out=sg[si][:, :n], in0=PS[0][:, :n], in1=sg[si][:, :n], op=ALU.mult)),
                         reads=[t_PS[0], t_sg[si]], writes=[t_sg[si]])
                    P.op("dve", (lambda e, c2=c2, v=v: e.scalar_tensor_tensor(
                        out=XT[:, c2, t0:t0 + n], in0=sg[si][:, :n], scalar=ada_ap(L, 2, c2, v), in1=XT[:, c2, t0:t0 + n],
                        op0=ALU.mult, op1=ALU.add)),
                        reads=[t_sg[si], t_ADA, tXT[c2][blk]], writes=[tXT[c2][blk]])
            P.barrier()

    def gqa(L, b):
        ctx_out = (L != 3) or debug_ctx
        qblocks = [0, 1, 2, 3, 4] if ctx_out else [1, 2, 3, 4]
        with contextlib.ExitStack() as st:
            HT = sb(st, "HT", [128, DC, NT], BF16)
            tHT = [[T() for _ in range(5)] for _ in range(DC)]
            with contextlib.ExitStack() as st2:
                mixer_norm(norm_scratch(st2), L, b, HT, tHT)
                P.barrier()
            OT = sb(st, "OT", [128, 8, NT], BF16)
            tOT = [[T() for _ in range(5)] for _ in range(8)]
            KT = sb(st, "KT", [128, 2, NT], BF16)
            tKT = [[T() for _ in range(5)] for _ in range(2)]
            V = sb(st, "V", [128, NT // 128, 256], BF16)
            tV = [T() for _ in range(NT // 128)]
            QT = [sb(st, "QT%d" % i, [128, NT], BF16) for i in range(2)]
            tQT = [[T() for _ in range(5)] for _ in range(2)]
            wqk = [sb(st, "wqk%d" % i, [128, DC, 256], BF16) for i in range(1)] * 2
            t_wqk = [T()] * 2
            gq = sb(st, "gq", [128, 4], F32)
            rope = sb(st, "rope", [128, 4, 64], F32)
            t_gq, t_rope = T(), T()
            onesH = sb(st, "onesH", [128, 128], BF16)
            t_onesH = T()
            P.op("pool", lambda e: e.memset(onesH[:], 1.0 / 128), writes=[t_onesH])
            P.dma("sp", gq[:], mixw["gqa_g"][:, :], writes=[t_gq])
            P.dma("sp", rope[:], mixw["gqa_rope"][:, :, :], writes=[t_rope])
            stv = contextlib.ExitStack()
            wv = sb(stv, "wv", [128, DC, 256], BF16)
            t_wv = T()
            P.dma("pool", wv[:], mixw["gqa_wv"].rearrange("(k p) n -> p k n", p=128), writes=[t_wv])
            for tt in range(NT // 128):
                pb = tt % 2
                blk = blk_of_tile(tt)
                for k in range(DC):
                    P.op("pe", (lambda e, k=k, tt=tt, pb=pb: e.matmul(
                        PS[pb][:, :256], lhsT=HT[:, k, tt * 128:(tt + 1) * 128], rhs=wv[:, k, :],
                        start=(k == 0), stop=(k == DC - 1))),
                        reads=[tHT[k][blk], t_wv], writes=[t_PS[pb]])
                if tt % 2 == 0:
                    P.op("dve", (lambda e, tt=tt, pb=pb: e.tensor_copy(out=V[:, tt, :], in_=PS[pb][:, :256])),
                         reads=[t_PS[pb]], writes=[tV[tt]])
                else:
                    P.op("act", (lambda e, tt=tt, pb=pb: e.copy(out=V[:, tt, :], in_=PS[pb][:, :256])),
                         reads=[t_PS[pb]], writes=[tV[tt]])
            P.barrier()
            stv.close()
            sqz = [sb(st, "sqz%d" % i, [128, 512], BF16) for i in range(1)] * 2
            t_sqz = [T()] * 2
            rq = [sb(st, "rq%d" % i, [128, 512], F32) for i in range(1)] * 2
            t_rq = [T()] * 2
            t1 = [sb(st, "t1_%d" % i, [128, 512], F32) for i in range(1)] * 2
            t2 = [sb(st, "t2_%d" % i, [128, 512], F32) for i in range(1)] * 2
            tt1, tt2 = [T()] * 2, [T()] * 2
            PT = [sb(st, "PT%d" % i, [128, 512], BF16) for i in range(2)]
            tPT = [T() for _ in range(2)]
            rs = sb(st, "rs", [128, 512], F32)
            trs = T()
            cnt = [0]

            def qk_unit(u, gcol, dst, dtile, blocks):
                wi_ = cnt[0] % 2
                cnt[0] += 1
                P.dma("pool", wqk[wi_][:], mixw["gqa_wqk"][u].rearrange("(k p) n -> p k n", p=128),
                      writes=[t_wqk[wi_]])
                for blk in blocks:
                    t0, n = BLK[blk]
                    i = blk % 2
                    for half in range(2 if blk > 0 else 1):
                        for k in range(DC):
                            P.op("pe", (lambda e, k=k, half=half, t0=t0, n=n: e.matmul(
                                PS[half][:, :n], lhsT=wqk[wi_][:, k, half * 128:(half + 1) * 128],
                                rhs=HT[:, k, t0:t0 + n], start=(k == 0), stop=(k == DC - 1))),
                                reads=[t_wqk[wi_], tHT[k][blk]], writes=[t_PS[half]])
                    P.op("act", (lambda e, i=i, n=n: e.activation(out=sqz[i][:, :n], in_=PS[0][:, :n], func=AF.Square)),
                         reads=[t_PS[0]], writes=[t_sqz[i]])
                    P.op("pe", (lambda e, i=i, n=n: e.matmul(PS[2][:, :n], lhsT=onesH[:], rhs=sqz[i][:, :n],
                                                             start=True, stop=True)),
                         reads=[t_onesH, t_sqz[i]], writes=[t_PS[2]])
                    P.op("act", (lambda e, i=i, n=n: e.activation(out=rq[i][:, :n], in_=PS[2][:, :n], func=AF.Sqrt,
                                                                  bias=EPS, scale=1.0)),
                         reads=[t_PS[2]], writes=[t_rq[i]])
                    P.op("dve", (lambda e, i=i, n=n: e.reciprocal(out=rq[i][:, :n], in_=rq[i][:, :n])),
                         reads=[t_rq[i]], writes=[t_rq[i]])
                    if blk == 0:
                        P.op("dve", (lambda e, i=i, n=n: e.scalar_tensor_tensor(
                            out=dst(blk), in0=PS[0][:, :n], scalar=gq[:, gcol:gcol + 1], in1=rq[i][:, :n],
                            op0=ALU.mult, op1=ALU.mult)),
                            reads=[t_PS[0], t_gq, t_rq[i]], writes=[dtile(blk)])
                        continue
                    r0 = (blk - 1) * 8
                    for hf in range(2):
                        p0, p1 = hf * 64, hf * 64 + 64
                        if hf == 0:
                            ctab = rope[p0:p1, 0, r0:r0 + 8].unsqueeze(2).broadcast_to([64, 8, 64])
                            stab = rope[p0:p1, 1, r0:r0 + 8].unsqueeze(2).broadcast_to([64, 8, 64])
                        else:
                            ctab = rope[p0:p1, 2, :].unsqueeze(1).broadcast_to([64, 8, 64])
                            stab = rope[p0:p1, 3, :].unsqueeze(1).broadcast_to([64, 8, 64])
                        z = PS[0][p0:p1, :n].rearrange("p (r c) -> p r c", c=64)
                        zs = PS[1][p0:p1, :n].rearrange("p (r c) -> p r c", c=64)
                        o1 = t1[i][p0:p1, :n].rearrange("p (r c) -> p r c", c=64)
                        o2 = t2[i][p0:p1, :n].rearrange("p (r c) -> p r c", c=64)
                        P.op("dve", (lambda e, z=z, o1=o1, ctab=ctab, p0=p0, p1=p1: e.scalar_tensor_tensor(
                            out=o1, in0=z, scalar=gq[p0:p1, gcol:gcol + 1], in1=ctab, op0=ALU.mult, op1=ALU.mult)),
                            reads=[t_PS[0], t_gq, t_rope], writes=[tt1[i]])
                        P.op("dve", (lambda e, zs=zs, o2=o2, stab=stab, p0=p0, p1=p1: e.scalar_tensor_tensor(
                            out=o2, in0=zs, scalar=gq[p0:p1, gcol + 1:gcol + 2], in1=stab, op0=ALU.mult, op1=ALU.mult)),
                            reads=[t_PS[1], t_gq, t_rope], writes=[tt2[i]])
                    P.op("pool", (lambda e, i=i, n=n: e.tensor_tensor(out=t1[i][:, :n], in0=t1[i][:, :n], in1=t2[i][:, :n],
                                                                      op=ALU.add)),
                         reads=[tt1[i], tt2[i]], writes=[tt1[i]])
                    P.op("dve", (lambda e, i=i, n=n, blk=blk: e.tensor_tensor(out=dst(blk), in0=t1[i][:, :n], in1=rq[i][:, :n],
                                                                               op=ALU.mult)),
                         reads=[tt1[i], t_rq[i]], writes=[dtile(blk)])

            for kv in range(2):
                qk_unit(8 + kv, 2, (lambda blk, kv=kv: KT[:, kv, BLK[blk][0]:BLK[blk][0] + BLK[blk][1]]),
                        (lambda blk, kv=kv: tKT[kv][blk]), [0, 1, 2, 3, 4])
            scale = 128.0 ** -0.5
            for h in range(8):
                qi = h % 2
                kv = h // 4
                qk_unit(h, 0, (lambda blk, qi=qi: QT[qi][:, BLK[blk][0]:BLK[blk][0] + BLK[blk][1]]),
                        (lambda blk, qi=qi: tQT[qi][blk]), qblocks)
                for blk in qblocks:
                    t0, n = BLK[blk]
                    key_tiles = [0, 1] if blk == 0 else list(range(NT // 128))
                    attention(QT[qi][:, t0:t0 + n], tQT[qi][blk],
                              (lambda kt, kv=kv: KT[:, kv, kt * 128:(kt + 1) * 128]),
                              (lambda kt, kv=kv: tKT[kv][blk_of_tile(kt)]),
                              (lambda kt, kv=kv: V[:, kt, kv * 128:(kv + 1) * 128]), (lambda kt: tV[kt]),
                              key_tiles, n, 128, 128, scale, OT[:, h, t0:t0 + n], tOT[h][blk], PT, tPT, rs, trs,
                              ones1[:, :])
            out_proj(L, b, mixw["gqa_wo"], OT, tOT, 8, 128, qblocks, st)
            P.barrier()

    def ffn(L, b):
        with contextlib.ExitStack() as st:
            ns = norm_scratch(st)
            GW = 1024
            GT = sb(st, "GT", [128, HC, GW], BF16)
            HTg = sb(st, "HTg", [128, DC, GW], BF16)
            silu_t = [sb(st, "silu%d" % i, [128, 512], F32) for i in range(2)]
            NWI, NWO = 3, 2
            wi = [sb(st, "wi%d" % i, [128, DC, 256], BF16) for i in range(NWI)]
            wo = [sb(st, "wo%d" % i, [128, HC, 128], BF16) for i in range(NWO)]
            t_silu = [T(), T()]
            t_wi = [[T(), T()] for _ in range(NWI)]
            t_wo = [T() for _ in range(NWO)]
            make_gm(L, nffn, t_nffn, 4, b)
            wic = woc = 0
            cnt = 0
            skip_ctx = (L == 3 and not debug_ctx)
            groups = [[1, 2], [3, 4]] if skip_ctx else [[0, 1], [2, 3], [4]]
            for grp in groups:
                offs = {}
                o = 0
                for blk in grp:
                    offs[blk] = o
                    o += BLK[blk][1]
                tHg = {blk: [T() for _ in range(DC)] for blk in grp}
                t_GT = {blk: [T() for _ in range(HC)] for blk in grp}
                for blk in grp:
                    n, o = BLK[blk][1], offs[blk]
                    norm_block(ns, L, b, blk, 3, (lambda c, o=o, n=n: HTg[:, c, o:o + n]), (lambda c, blk=blk: tHg[blk][c]))
                for j in range(HC):
                    wi_i = wic % NWI
                    wic += 1
                    for half in range(2):
                        c0 = half * FH + j * 128
                        P.dma("pool", wi[wi_i][:, :, half * 128:(half + 1) * 128],
                              ffn_wi_d[L][:, c0:c0 + 128].rearrange("(k p) n -> p k n", p=128),
                              writes=[t_wi[wi_i][half]])
                    for blk in grp:
                        n, o = BLK[blk][1], offs[blk]
                        pa, pbk = (0, 1) if cnt % 2 == 0 else (2, 3)
                        si = cnt % 2
                        cnt += 1
                        for half, pbank in ((0, pa), (1, pbk)):
                            for k in range(DC):
                                P.op("pe", (lambda e, half=half, k=k, pbank=pbank: e.matmul(
                                    PS[pbank][:, :n], lhsT=wi[wi_i][:, k, half * 128:(half + 1) * 128],
                                    rhs=HTg[:, k, o:o + n], start=(k == 0), stop=(k == DC - 1))),
                                    reads=[t_wi[wi_i][half], tHg[blk][k]], writes=[t_PS[pbank]])
                        P.op("act", (lambda e: e.activation(out=silu_t[si][:, :n], in_=PS[pa][:, :n], func=AF.Silu)),
                             reads=[t_PS[pa]], writes=[t_silu[si]])
                        P.op("dve", (lambda e: e.tensor_tensor(
                            out=GT[:, j, o:o + n], in0=PS[pbk][:, :n], in1=silu_t[si][:, :n], op=ALU.mult)),
                            reads=[t_PS[pbk], t_silu[si]], writes=[t_GT[blk][j]])
                for c in range(DC):
                    wo_i = woc % NWO
                    woc += 1
                    P.dma("pool", wo[wo_i][:],
                          ffn_wo_d[L][:, c * 128:(c + 1) * 128].rearrange("(j p) n -> p j n", p=128),
                          writes=[t_wo[wo_i]])
                    for blk in grp:
                        t0, n = BLK[blk]
                        o = offs[blk]
                        v = nb if blk == 0 else b
                        pbank = 4 + (cnt % 2)
                        cnt += 1
                        for j in range(HC):
                            P.op("pe", (lambda e, j=j: e.matmul(
                                PS[pbank][:, :n], lhsT=wo[wo_i][:, j, :], rhs=GT[:, j, o:o + n],
                                start=(j == 0), stop=(j == HC - 1))),
                                reads=[t_wo[wo_i], t_GT[blk][j]], writes=[t_PS[pbank]])
                        P.op("dve", (lambda e: e.scalar_tensor_tensor(
                            out=XT[:, c, t0:t0 + n], in0=PS[pbank][:, :n], scalar=ada_ap(L, 5, c, v),
                            in1=XT[:, c, t0:t0 + n], op0=ALU.mult, op1=ALU.add)),
                            reads=[t_PS[pbank], t_ADA, tXT[c][blk]], writes=[tXT[c][blk]])
            P.barrier()

    if do_mixer and 1 in layers:
        s5_consts()
    for b in range(nb):
        load_tokens(b)
        for L in layers:
            if do_mixer:
                {0: mla, 1: s5, 2: na, 3: gqa}[L](L, b)
            if do_ffn:
                ffn(L, b)
        store_tokens(b)
        if b < nb - 1:
            P.reset_epoch()

    P.finish()
    gs.close()
    print("instructions:", P.n_instr, {e: P.count[e] for e in P.ENGS})
    return nc


def _fm(vec_rows):
    a = np.asarray(vec_rows, dtype=np.float32)
    lead = a.shape[:-1]
    nchunk = a.shape[-1] // 128
    a = a.reshape(lead + (nchunk, 128))
    return np.ascontiguousarray(np.moveaxis(a, -1, 0))


def _rope_tables(rot_dim):
    axis_dim = rot_dim // 2
    nf = axis_dim // 2
    inv_freq = (np.float32(10000.0) ** (-np.arange(0, axis_dim, 2, dtype=np.float32) / np.float32(axis_dim))).astype(np.float32)
    rows = np.arange(32, dtype=np.float32)[:, None] * inv_freq[None, :]
    cols = np.arange(64, dtype=np.float32)[:, None] * inv_freq[None, :]
    sign = np.concatenate([-np.ones(nf, np.float32), np.ones(nf, np.float32)])
    idx = np.concatenate([np.arange(nf), np.arange(nf)])
    cos_r = np.cos(rows)[:, idx].T.astype(np.float32)
    sin_r = (np.sin(rows)[:, idx] * sign[None, :]).T.astype(np.float32)
    cos_c = np.cos(cols)[:, idx].T.astype(np.float32)
    sin_c = (np.sin(cols)[:, idx] * sign[None, :]).T.astype(np.float32)
    return cos_r, sin_r, cos_c, sin_c


def _prep_gqa(inputs):
    w = inputs["gqa_w_qkv"][0]
    partner = np.concatenate([np.arange(32, 64), np.arange(0, 32), np.arange(96, 128), np.arange(64, 96)])
    units = []
    for u in range(10):
        cols = np.arange(u * 128, (u + 1) * 128)
        units.append(np.concatenate([w[:, cols], w[:, cols[partner]]], axis=1))
    gq, gk = inputs["gqa_g_qn"][0], inputs["gqa_g_kn"][0]
    cos_r, sin_r, cos_c, sin_c = _rope_tables(128)
    rope = np.zeros((128, 4, 64), np.float32)
    rope[:64, 0, :32], rope[:64, 1, :32] = cos_r, sin_r
    rope[64:, 2, :], rope[64:, 3, :] = cos_c, sin_c
    return {
        "gqa_wqk": np.ascontiguousarray(np.stack(units)),
        "gqa_wv": np.ascontiguousarray(w[:, 1280:1536]),
        "gqa_wo": np.ascontiguousarray(inputs["gqa_w_o"][0]),
        "gqa_g": np.ascontiguousarray(np.stack([gq, gq[partner], gk, gk[partner]], axis=1)),
        "gqa_rope": rope,
    }


def _prep_mla(inputs):
    w_in, w_uq, w_ukv = inputs["mla_w_in"][0], inputs["mla_w_uq"][0], inputs["mla_w_ukv"][0]
    orig = -np.ones(128, np.int64)
    orig[0:16] = 64 + np.arange(16)
    orig[32:48] = 80 + np.arange(16)
    orig[64:128] = np.arange(64)
    partner = np.arange(128)
    partner[0:8], partner[8:16] = np.arange(8, 16), np.arange(0, 8)
    partner[32:40], partner[40:48] = np.arange(40, 48), np.arange(32, 40)
    valid = orig >= 0
    is_rope = np.zeros(128, bool)
    is_rope[0:16] = True
    is_rope[32:48] = True
    wuq = np.zeros((16, 384, 256), np.float32)
    wuk = np.zeros((16, 256, 128), np.float32)
    wv = np.zeros((256, 1024), np.float32)
    for h in range(16):
        wuq[h][:, np.nonzero(valid)[0]] = w_uq[:, h * 96 + orig[valid]]
        rp = np.nonzero(is_rope)[0]
        wuq[h][:, 128 + rp] = w_uq[:, h * 96 + orig[partner[rp]]]
        wuk[h][:, 64:128] = w_ukv[:, h * 128:h * 128 + 64]
        wv[:, h * 64:(h + 1) * 64] = w_ukv[:, h * 128 + 64:h * 128 + 128]
    win = np.zeros((D, 896), np.float32)
    win[:, 0:640] = w_in[:, 0:640]
    rp = np.nonzero(is_rope)[0]
    win[:, 640 + rp] = w_in[:, 640 + (orig[rp] - 64)]
    win[:, 768 + rp] = w_in[:, 640 + (orig[partner[rp]] - 64)]
    gl = np.concatenate([inputs["mla_g_q"][0].reshape(3, 128), inputs["mla_g_kv"][0].reshape(2, 128)], axis=0).T
    gq, gk = inputs["mla_g_qn"][0], inputs["mla_g_kn"][0]
    g4 = np.zeros((128, 4), np.float32)
    g4[valid, 0], g4[valid, 2] = gq[orig[valid]], gk[orig[valid]]
    g4[rp, 1], g4[rp, 3] = gq[orig[partner[rp]]], gk[orig[partner[rp]]]
    cos_r, sin_r, cos_c, sin_c = _rope_tables(32)
    rope = np.zeros((128, 4, 64), np.float32)
    rope[0:16, 0, :32], rope[0:16, 1, :32] = cos_r, sin_r
    rope[32:48, 2, :], rope[32:48, 3, :] = cos_c, sin_c
    return {"mla_win": win, "mla_gl": np.ascontiguousarray(gl), "mla_wuq": wuq, "mla_wuk": wuk, "mla_wv": wv,
            "mla_wo": np.ascontiguousarray(inputs["mla_w_o"][0]), "mla_g": g4, "mla_rope": rope}


def _prep_na(inputs):
    rpb = inputs["na_rpb"][0]
    kc = np.arange(64)[:, None]
    qc = np.arange(64)[None, :]
    c0 = np.clip(qc - 8, 0, 48)
    ok = (kc >= c0) & (kc < c0 + 16)
    idx = np.clip(kc - qc + 15, 0, 30)
    M = rpb[:, :, idx]
    M = np.where(ok[None, None], M, np.float32(-1e4)).astype(np.float32)
    gq, gk = inputs["na_g_qn"][0], inputs["na_g_kn"][0]
    return {"na_wqkv": np.ascontiguousarray(inputs["na_w_qkv"][0]), "na_wo": np.ascontiguousarray(inputs["na_w_o"][0]),
            "na_g": np.ascontiguousarray(np.stack([np.tile(gq, 2), np.tile(gk, 2)], axis=1)),
            "na_M": np.ascontiguousarray(np.transpose(M, (0, 2, 1, 3)))}


def _prep_s5(inputs):
    a_re, a_im, ldt = inputs["s5_a_re"][0], inputs["s5_a_im"][0], inputs["s5_log_dt"][0]
    def sm(x):
        return np.ascontiguousarray(np.transpose(x.reshape(2, 32, 2, 64), (2, 3, 0, 1)).reshape(128, 64))
    a_s = np.stack([sm(a_re), sm(a_im)], axis=1)
    ldt_s = sm(np.broadcast_to(ldt[:, :, None], (2, 64, 64)))
    b = np.stack([inputs["s5_b_re"][0], inputs["s5_b_im"][0]])
    cm = np.stack([inputs["s5_c_re"][0], inputs["s5_c_im"][0]])
    return {"s5_as": np.ascontiguousarray(a_s), "s5_ldt": ldt_s,
            "s5_b": np.ascontiguousarray(np.transpose(b, (2, 4, 1, 0, 3)).reshape(64, 16, 4, 64)),
            "s5_c": np.ascontiguousarray(np.transpose(cm, (2, 4, 1, 0, 3)).reshape(64, 64, 4, 16)),
            "s5_d": _fm(inputs["s5_d"][0]), "s5_wglu": np.ascontiguousarray(inputs["s5_w_glu"][0])}


def make_in_maps(inputs, nb, n_cores, layers=(0, 1, 2, 3), do_mixer=True, do_ffn=True):
    maps = []
    x, c, ctx, c_ctx = inputs["x"], inputs["c"], inputs["ctx"], inputs["c_ctx"]
    shared = {
        "ident": np.eye(128, dtype=np.float32),
        "ada_b": _fm(inputs["ada_b"]),
        "norm_mix": _fm(inputs["norm_mix"]),
        "norm_ffn": _fm(inputs["norm_ffn"]),
    }
    for L in layers:
        shared["ada_w%d" % L] = inputs["ada_w"][L]
        if do_ffn:
            shared["ffn_wi%d" % L] = inputs["ffn_w_in"][L]
            shared["ffn_wo%d" % L] = inputs["ffn_w_out"][L]
    if do_mixer and 3 in layers:
        shared.update(_prep_gqa(inputs))
    if do_mixer and 0 in layers:
        shared.update(_prep_mla(inputs))
    if do_mixer and 2 in layers:
        shared.update(_prep_na(inputs))
    if do_mixer and 1 in layers:
        shared.update(_prep_s5(inputs))
    for core in range(n_cores):
        sl = slice(core * nb, (core + 1) * nb)
        cvecs = np.concatenate([c[sl], c_ctx[None, :]], axis=0)
        cc = np.ascontiguousarray(np.transpose(cvecs.reshape(nb + 1, DC, 128), (2, 1, 0)))
        m = dict(shared)
        m.update({"x": np.ascontiguousarray(x[sl]), "ctx": np.ascontiguousarray(ctx[sl]), "cc": cc})
        maps.append(m)
    return maps


def run(inputs, nb=4, n_cores=N_CORES, layers=(0, 1, 2, 3), debug_ctx=False, **kw):
    nc = build_program(nb, list(layers), debug_ctx, **kw)
    maps = make_in_maps(inputs, nb, n_cores, layers, kw.get("do_mixer", True), kw.get("do_ffn", True))
    res = run_bass_kernel_spmd(nc, maps, core_ids=list(range(n_cores)))
    out = np.concatenate([r["out"] for r in res.results], axis=0)
    if debug_ctx:
        outc = np.concatenate([r["out_ctx"] for r in res.results], axis=0)
        return out, outc
    return out


def kernel(**inputs):
    inputs = {k: np.asarray(v) for k, v in inputs.items()}
    return run(inputs).astype(np.float32)
```
